# Optimizing a Trainium2 kernel written in Bass

```python
import math
import jax, jax.numpy as jnp
from jax import lax
import numpy as np

D_MODEL = 2048
BATCH = 4
SEQ = 4096
DEPTH = 1

CHUNK = 64
Q_BLOCK = 128
EPS = 1e-6
NEG_INF = -1e30
ROPE_THETA = 10000.0

MLA_HEADS = 8
Q_LORA = 512
KV_LORA = 256
QK_NOPE = 128
QK_ROPE = 64
V_HEAD = 128
MLA_WIDTH = MLA_HEADS * V_HEAD

FOX_HEADS = 8
FOX_HEAD_DIM = 128
FOX_WIDTH = FOX_HEADS * FOX_HEAD_DIM

MIX_WIDTH = MLA_WIDTH + FOX_WIDTH
IN_SPLITS = (Q_LORA, KV_LORA, QK_ROPE, FOX_WIDTH, FOX_WIDTH, FOX_WIDTH, FOX_HEADS)
IN_DIM = sum(IN_SPLITS)

FFN_HIDDEN = ((8 * D_MODEL // 3 + 255) // 256) * 256

kernel_name = "hybrid_mla_fox_parallel_heads"


def rms_norm(x, g):
    xf = x.astype(jnp.float32)
    y = xf * lax.rsqrt(jnp.mean(xf * xf, axis=-1, keepdims=True) + EPS)
    return y.astype(x.dtype) * g


def apply_rope(x, cos, sin):
    half = x.shape[-1] // 2
    x1, x2 = x[..., :half], x[..., half:]
    return jnp.concatenate([x1 * cos - x2 * sin, x2 * cos + x1 * sin], axis=-1)


def block_attention(q, k, v, scale, causal_unit, log_decay=None):
    B, H, S, _ = q.shape
    dv = v.shape[-1]
    n_blocks = S // Q_BLOCK
    key_unit = jnp.arange(S) // causal_unit

    def one_block(i):
        start = i * Q_BLOCK
        q_i = lax.dynamic_slice_in_dim(q, start, Q_BLOCK, axis=2)
        t = start + jnp.arange(Q_BLOCK)
        s = jnp.einsum('bhqd,bhkd->bhqk', q_i, k).astype(jnp.float32) * scale
        if log_decay is not None:
            c_i = lax.dynamic_slice_in_dim(log_decay, start, Q_BLOCK, axis=2)
            s = s + c_i[..., :, None] - log_decay[..., None, :]
        allowed = key_unit[None, :] <= (t // causal_unit)[:, None]
        s = jnp.where(allowed, s, NEG_INF)
        p = jax.nn.softmax(s, axis=-1).astype(v.dtype)
        return jnp.einsum('bhqk,bhkd->bhqd', p, v)

    out = lax.map(one_block, jnp.arange(n_blocks))
    return out.transpose(1, 2, 0, 3, 4).reshape(B, H, S, dv)


def setup_inputs(seed: int = 0) -> dict:
    key = jax.random.key(seed)
    ks = jax.random.split(key, 20)

    def nrm(k, shape, fan_in):
        return jax.random.normal(k, shape, jnp.float32) * fan_in ** -0.5

    def gain(k, shape):
        return 1.0 + 0.02 * jax.random.normal(k, shape, jnp.float32)

    x = jax.random.normal(ks[0], (BATCH, SEQ, D_MODEL), jnp.float32)
    offset = jax.random.randint(ks[1], (BATCH,), 0, 4096, dtype=jnp.int32)
    positions = (jnp.arange(SEQ, dtype=jnp.int32)[None, :] + offset[:, None]).astype(jnp.int32)
    return {
        "x": x,
        "positions": positions,
        "g_attn_norm": gain(ks[2], (DEPTH, D_MODEL)),
        "w_in": nrm(ks[3], (DEPTH, D_MODEL, IN_DIM), D_MODEL),
        "b_forget": jax.random.uniform(ks[4], (DEPTH, FOX_HEADS), jnp.float32, 1.0, 4.0),
        "g_q_lat": gain(ks[5], (DEPTH, Q_LORA)),
        "w_uq": nrm(ks[6], (DEPTH, Q_LORA, MLA_HEADS * (QK_NOPE + QK_ROPE)), Q_LORA),
        "g_kv_lat": gain(ks[7], (DEPTH, KV_LORA)),
        "w_ukv": nrm(ks[8], (DEPTH, KV_LORA, MLA_HEADS * (QK_NOPE + V_HEAD)), KV_LORA),
        "g_out_mla": gain(ks[9], (DEPTH, MLA_WIDTH)),
        "g_out_fox": gain(ks[10], (DEPTH, FOX_WIDTH)),
        "w_out": nrm(ks[11], (DEPTH, MIX_WIDTH, D_MODEL), MIX_WIDTH),
        "g_ffn_norm": gain(ks[12], (DEPTH, D_MODEL)),
        "w_gate": nrm(ks[13], (DEPTH, D_MODEL, FFN_HIDDEN), D_MODEL),
        "w_up": nrm(ks[14], (DEPTH, D_MODEL, FFN_HIDDEN), D_MODEL),
        "w_down": nrm(ks[15], (DEPTH, FFN_HIDDEN, D_MODEL), FFN_HIDDEN),
        "g_final_norm": gain(ks[16], (D_MODEL,)),
    }


def reference(x, positions, g_attn_norm, w_in, b_forget, g_q_lat, w_uq, g_kv_lat,
              w_ukv, g_out_mla, g_out_fox, w_out, g_ffn_norm, w_gate, w_up, w_down,
              g_final_norm):
    B, S, _ = x.shape
    inv_freq = ROPE_THETA ** (-jnp.arange(0, QK_ROPE, 2, dtype=jnp.float32) / QK_ROPE)
    ang = positions.astype(jnp.float32)[..., None] * inv_freq
    cos = jnp.cos(ang)[:, :, None, :].astype(x.dtype)
    sin = jnp.sin(ang)[:, :, None, :].astype(x.dtype)
    split_idx = list(np.cumsum(IN_SPLITS)[:-1])

    for l in range(DEPTH):
        h = rms_norm(x, g_attn_norm[l])
        proj = h @ w_in[l]
        q_lat, kv_lat, k_rope, fq, fk, fv, f_logit = jnp.split(proj, split_idx, axis=-1)

        q = (rms_norm(q_lat, g_q_lat[l]) @ w_uq[l]).reshape(B, S, MLA_HEADS, QK_NOPE + QK_ROPE)
        q_nope, q_pe = q[..., :QK_NOPE], q[..., QK_NOPE:]
        q_pe = apply_rope(q_pe, cos, sin)
        kv = (rms_norm(kv_lat, g_kv_lat[l]) @ w_ukv[l]).reshape(B, S, MLA_HEADS, QK_NOPE + V_HEAD)
        k_nope, v_mla = kv[..., :QK_NOPE], kv[..., QK_NOPE:]
        k_pe = apply_rope(k_rope[:, :, None, :], cos, sin)
        k_pe = jnp.broadcast_to(k_pe, (B, S, MLA_HEADS, QK_ROPE))
        q_mla = jnp.concatenate([q_nope, q_pe], axis=-1).transpose(0, 2, 1, 3)
        k_mla = jnp.concatenate([k_nope, k_pe], axis=-1).transpose(0, 2, 1, 3)
        v_mla = v_mla.transpose(0, 2, 1, 3)
        o_mla = block_attention(q_mla, k_mla, v_mla, 1.0 / math.sqrt(QK_NOPE + QK_ROPE), CHUNK)
        o_mla = o_mla.transpose(0, 2, 1, 3).reshape(B, S, MLA_WIDTH)

        q_fox = fq.reshape(B, S, FOX_HEADS, FOX_HEAD_DIM).transpose(0, 2, 1, 3)
        k_fox = fk.reshape(B, S, FOX_HEADS, FOX_HEAD_DIM).transpose(0, 2, 1, 3)
        v_fox = fv.reshape(B, S, FOX_HEADS, FOX_HEAD_DIM).transpose(0, 2, 1, 3)
        log_f = jax.nn.log_sigmoid(f_logit.astype(jnp.float32) + b_forget[l].astype(jnp.float32))
        c = jnp.cumsum(log_f.transpose(0, 2, 1), axis=-1)
        o_fox = block_attention(q_fox, k_fox, v_fox, 1.0 / math.sqrt(FOX_HEAD_DIM), 1, c)
        o_fox = o_fox.transpose(0, 2, 1, 3).reshape(B, S, FOX_WIDTH)

        mixed = jnp.concatenate([rms_norm(o_mla, g_out_mla[l]), rms_norm(o_fox, g_out_fox[l])], axis=-1)
        x = x + mixed @ w_out[l]

        h2 = rms_norm(x, g_ffn_norm[l])
        x = x + (jax.nn.silu(h2 @ w_gate[l]) * (h2 @ w_up[l])) @ w_down[l]

    return rms_norm(x, g_final_norm)
```

```python
import contextlib
import math
import numpy as np
import concourse.bass as bass
import concourse.mybir as mybir
from concourse.bass_utils import run_bass_kernel_spmd

F32 = mybir.dt.float32
BF16 = mybir.dt.bfloat16
I32 = mybir.dt.int32
U8 = mybir.dt.uint8
AF = mybir.ActivationFunctionType
ALU = mybir.AluOpType
AX = mybir.AxisListType

D = 2048
S = 4096
NB = 32
NO = 16
FF = 5632
NCH = FF // 128
EPS = 1e-6
QSCALE = 1.0 / math.sqrt(192.0)
FSCALE = 1.0 / math.sqrt(128.0)
PI = float(np.pi)
NEG = -1.0e30
ARENA_BYTES = 211968

DEBUG = False
STOP = 99
NCORES = 8
SKIP = set()
NT_LIMIT = 8
LATSTEP = 99
ENGS = ("pe", "act", "dve", "pool", "sp")


class Op:
    __slots__ = ("eng", "fn", "deps", "needed", "count", "dma_sem")

    def __init__(self, eng, fn, dma_sem):
        self.eng = eng
        self.fn = fn
        self.deps = []
        self.needed = False
        self.count = None
        self.dma_sem = dma_sem


def _base(k):
    return k[0] if isinstance(k, tuple) else k


class Prog:
    def __init__(self, nc):
        self.nc = nc
        self.ops = {e: [] for e in ENGS}
        self.state = {}
        self.inherit = {}
        self.touch = {}
        self.last_dma = {}

    def op(self, eng, fn, reads=(), writes=(), dma_sem=None):
        o = Op(eng, fn, dma_sem)
        deps = []
        for b in list(reads) + list(writes):
            if b not in self.state:
                self.state[b] = [list(self.inherit.get(_base(b), [])), []]
        for b in reads:
            deps.extend(self.state[b][0])
            if _base(b) == "ps":
                deps.extend(r for r in self.state[b][1] if r.eng != eng)
        for b in writes:
            st = self.state[b]
            deps.extend(st[0])
            deps.extend(st[1])
        seen = set()
        for d in deps:
            if id(d) in seen or d is o:
                continue
            seen.add(id(d))
            if d.eng == "pe" and eng == "pe" and d.dma_sem is None and dma_sem is None:
                continue
            if d.dma_sem is not None:
                d = self.last_dma[d.dma_sem]
            d.needed = True
            o.deps.append(d)
        for b in reads:
            rl = self.state[b][1]
            if dma_sem is None:
                for i_ in range(len(rl)):
                    if rl[i_].eng == eng and rl[i_].dma_sem is None:
                        del rl[i_]
                        break
            rl.append(o)
        for b in writes:
            self.state[b] = [[o], []]
        tag = ("d", dma_sem) if dma_sem is not None else ("e", eng)
        for b in list(reads) + list(writes):
            self.touch.setdefault(_base(b), {})[tag] = o
        if dma_sem is not None:
            self.last_dma[dma_sem] = o
        self.ops[eng].append(o)
        return o

    def emit(self):
        nc = self.nc
        sem_names = set()
        for e in ENGS:
            for o in self.ops[e]:
                if o.dma_sem is not None:
                    sem_names.add("d_" + o.dma_sem)
        sem_names = sorted(sem_names)
        all_names = ["e_" + e for e in ENGS] + sem_names
        with contextlib.ExitStack() as st:
            sems = {n: st.enter_context(nc.semaphore(n)) for n in all_names}
            cnt = {n: 0 for n in all_names}
            for e in ENGS:
                for o in self.ops[e]:
                    if o.dma_sem is not None:
                        n = "d_" + o.dma_sem
                        cnt[n] += 16
                        o.count = (n, cnt[n])
                    elif o.needed:
                        n = "e_" + e
                        cnt[n] += 1
                        o.count = (n, cnt[n])
            blk = st.enter_context(nc.Block())

            def run(engobj, e):
                waited = {}
                for o in self.ops[e]:
                    need = {}
                    for d in o.deps:
                        n, v = d.count
                        if v > need.get(n, 0):
                            need[n] = v
                    for n, v in need.items():
                        if waited.get(n, 0) >= v:
                            continue
                        waited[n] = v
                        engobj.wait_ge(sems[n], v)
                    ins = o.fn(engobj)
                    if o.dma_sem is not None:
                        ins.then_inc(sems[o.count[0]], 16)
                    elif o.needed:
                        ins.then_inc(sems[o.count[0]], 1)
                if e == "sp":
                    for n in sem_names:
                        if cnt[n] > 0:
                            engobj.wait_ge(sems[n], cnt[n])

            blk.tensor(lambda t: run(t, "pe"))
            blk.scalar(lambda t: run(t, "act"))
            blk.vector(lambda t: run(t, "dve"))
            blk.gpsimd(lambda t: run(t, "pool"))
            blk.sync(lambda t: run(t, "sp"))


_DTSZ = {F32: 4, BF16: 2, I32: 4, U8: 1}


class Arena:
    def __init__(self, P, arena_ap):
        self.P = P
        self.a = arena_ap
        self.top = 0
        self.live = []
        self.dead = []
        self.uid = 0
        self.peak = 0

    def alloc(self, name, shape, dt, parts=128):
        self.uid += 1
        name = "%s#%d" % (name, self.uid)
        n = 1
        for s in shape:
            n *= s
        size = (n * _DTSZ[dt] + 63) // 64 * 64
        off = self.top
        self.top += size
        self.peak = max(self.peak, self.top)
        assert self.top <= ARENA_BYTES, ("SBUF arena overflow", name, self.top)
        inh = []
        for (nm, o, s) in self.dead:
            if o < off + size and off < o + s:
                inh.extend(self.P.touch.get(nm, {}).values())
        self.P.inherit[name] = inh
        self.live.append((name, off, size))
        v = self.a[0:parts, off:off + n * _DTSZ[dt]].bitcast(dt)
        if len(shape) == 2:
            v = v.rearrange("p (a b) -> p a b", a=shape[0])
        elif len(shape) == 3:
            v = v.rearrange("p (a b c) -> p a b c", a=shape[0], b=shape[1])
        return name, v

    def mark(self):
        return (self.top, len(self.live))

    def release(self, mark):
        top, nl = mark
        self.dead.extend(self.live[nl:])
        del self.live[nl:]
        self.top = top


def build_program():
    nc = bass.Bass("TRN2", target_bir_lowering=False)

    def din(name, shape, dt=F32):
        return nc.dram_tensor(name, list(shape), dt, kind="ExternalInput").ap()

    big2 = STOP >= 2
    big4 = STOP >= 4
    x_nat = din("x_nat", [S, D] if big2 else [128, D])
    x_own = din("x_own", [S // 2, D])
    posN_d = din("pos_nat", [128, NB], I32)
    posO_d = din("pos_own", [128, NO], I32)
    w_in = din("w_in", [D, 3912])
    w_uq = din("w_uq", [512, 1536])
    w_ukv = din("w_ukv", [256, 2048])
    w_out = din("w_out", [D, D] if big4 else [128, D])
    w_gate = din("w_gate", [D, FF] if big4 else [128, FF])
    w_up = din("w_up", [D, FF] if big4 else [128, FF])
    w_down = din("w_down", [FF, D] if big4 else [128, D])
    g_attn = din("g_attn", [D])
    g_q = din("g_q", [512])
    g_kv = din("g_kv", [256])
    g_out = din("g_out", [D])
    g_ffn = din("g_ffn", [D])
    g_fin = din("g_fin", [D])
    b_fg = din("b_fg", [8])
    ident_d = din("c_ident", [128, 128])
    tri_d = din("c_tri", [128, 128])
    ones_d = din("c_ones", [128, 128])
    invf_d = din("c_invf", [128, 32])
    maskM_d = din("c_maskM", [128, 1024])
    maskF_d = din("c_maskF", [128, 1024])
    sel_d = din("c_sel", [128, NO * NB])
    out_d = nc.dram_tensor("out", [S // 2, D], F32, kind="ExternalOutput").ap()

    skind = "ExternalOutput" if DEBUG else "Internal"

    def scratch(name, shape):
        return nc.dram_tensor(name, list(shape), BF16, kind=skind).ap()

    KTM = scratch("s_ktm", [8, 128, S])
    KTF = scratch("s_ktf", [8, 128, S])
    VM = scratch("s_vm", [S, 1024])
    VF = scratch("s_vf", [S, 1024])
    QTN = scratch("s_qtn", [8, 128, S // 2])
    QTP = scratch("s_qtp", [8, 64, S // 2])
    QTF = scratch("s_qtf", [8, 128, S // 2])
    OTS = scratch("s_ots", [16, 128, S // 2])

    with contextlib.ExitStack() as es:
        arena_t = es.enter_context(nc.sbuf_tensor("arena", [128, ARENA_BYTES], U8))
        banks = [es.enter_context(nc.psum_tensor("bank%d" % i, [128, 512], F32)) for i in range(8)]
        P = Prog(nc)
        A = Arena(P, arena_t[:])
        BK = [("ps", i) for i in range(8)]

        def bf_bank(i):
            return banks[i][:].bitcast(BF16).rearrange("p (a b) -> p a b", a=8)

        def dma(q, out, in_, reads, writes, sem):
            return P.op(q, lambda e: e.dma_start(out=out, in_=in_), reads=reads, writes=writes, dma_sem=sem)

        def mm(out, lhsT, rhs, start, stop, reads, writes):
            return P.op("pe", lambda e: e.matmul(out, lhsT=lhsT, rhs=rhs, start=start, stop=stop), reads=reads, writes=writes)

        def tr(out, in_, reads, writes):
            return P.op("pe", lambda e: e.transpose(out=out, in_=in_, identity=identb), reads=list(reads) + [k_identb], writes=writes)

        def act(out, in_, func, reads, writes, bias=None, scale=None, accum=None):
            kw = {}
            if bias is not None:
                kw["bias"] = bias
            if scale is not None:
                kw["scale"] = scale
            if accum is not None:
                kw["accum_out"] = accum
            return P.op("act", lambda e: e.activation(out=out, in_=in_, func=func, **kw), reads=reads, writes=writes)

        def amul(out, in_, c, reads, writes):
            return P.op("act", lambda e: e.mul(out=out, in_=in_, mul=c), reads=reads, writes=writes)

        def tcopy(eng, out, in_, reads, writes):
            if eng == "act":
                return P.op("act", lambda e: e.copy(out=out, in_=in_), reads=reads, writes=writes)
            return P.op(eng, lambda e: e.tensor_copy(out=out, in_=in_), reads=reads, writes=writes)

        def tt(eng, out, in0, in1, op, reads, writes):
            return P.op(eng, lambda e: e.tensor_tensor(out=out, in0=in0, in1=in1, op=op), reads=reads, writes=writes)

        def ts(eng, out, in0, s1, s2, op0, op1, reads, writes):
            if s2 is None:
                return P.op(eng, lambda e: e.tensor_scalar(out=out, in0=in0, scalar1=s1, scalar2=None, op0=op0), reads=reads, writes=writes)
            return P.op(eng, lambda e: e.tensor_scalar(out=out, in0=in0, scalar1=s1, scalar2=s2, op0=op0, op1=op1), reads=reads, writes=writes)

        def stt(eng, out, in0, scalar, in1, op0, op1, reads, writes):
            return P.op(eng, lambda e: e.scalar_tensor_tensor(out=out, in0=in0, scalar=scalar, in1=in1, op0=op0, op1=op1), reads=reads, writes=writes)

        def memset(eng, ap, val, writes):
            return P.op(eng, lambda e: e.memset(ap, val), writes=writes)

        def finish():
            P.emit()
            build_program.peak = A.peak
            return nc

        k_identf, identf = A.alloc("identf", [128], F32)
        k_identb, identb = A.alloc("identb", [128], BF16)
        k_epsb, epsb = A.alloc("epsb", [1], F32)
        k_oneb, oneb = A.alloc("oneb", [1], F32)
        k_hpib, hpib = A.alloc("hpib", [1], F32)
        k_rsa, rstd_att = A.alloc("rstd_att", [NO * 2], F32)
        k_ssa, ssq_att = A.alloc("ssq_att", [NO, 16], F32)
        dma("sp", identf, ident_d, [], [k_identf], "c")
        memset("dve", epsb, EPS, [k_epsb])
        memset("dve", oneb, 1.0, [k_oneb])
        memset("dve", hpib, PI / 2, [k_hpib])
        cnt_sm = [0]

        k_lnv, lnv_all = A.alloc("lnv", [64], F32)

        def rsqrt_mean(ssq_ap, k_ssq, n_feat, out_ap, k_out, width):
            o = cnt_sm[0] % 2
            cnt_sm[0] += 1
            lnv = lnv_all[:, o * 32:o * 32 + width]
            act(lnv, ssq_ap, AF.Ln, [k_ssq, k_epsb], [(k_lnv, o)], bias=epsb, scale=1.0 / n_feat)
            act(out_ap, lnv, AF.Exp, [(k_lnv, o)], [k_out], scale=-0.5)

        attn_mark = A.mark()
        k_trif, trif = A.alloc("trif", [128], F32)
        k_onesf, onesf = A.alloc("onesf", [128], F32)
        k_maskM, maskM = A.alloc("maskM", [1024], F32)
        k_maskF, maskF = A.alloc("maskF", [1024], F32)
        k_sel, sel = A.alloc("sel", [NO, NB], F32)
        k_cosN, cosN = A.alloc("cosN", [NB, 32], F32)
        k_sinN, sinN = A.alloc("sinN", [NB, 32], F32)
        k_cosO, cosO = A.alloc("cosO", [NO, 32], F32)
        k_sinO, sinO = A.alloc("sinO", [NO, 32], F32)
        k_flog, flog = A.alloc("flog", [NB, 8], F32)
        k_negc, negc = A.alloc("negc", [NB, 8], F32)
        k_cown, cown = A.alloc("cown", [NO, 8], F32)
        k_bfg, bfg = A.alloc("bfg", [8], F32)
        k_kpeT, kpeT = A.alloc("kpeT", [S], BF16)
        dma("sp", trif, tri_d, [], [k_trif], "c")
        dma("sp", onesf, ones_d, [], [k_onesf], "c")
        dma("sp", maskM, maskM_d, [], [k_maskM], "c")
        dma("sp", maskF, maskF_d, [], [k_maskF], "c")
        dma("sp", sel, sel_d.rearrange("p (a n) -> p a n", a=NO), [], [k_sel], "c")
        dma("sp", bfg, b_fg.partition_broadcast(128), [], [k_bfg], "c")

        k_piN, pos_iN = A.alloc("pos_iN", [NB], I32)
        k_piO, pos_iO = A.alloc("pos_iO", [NO], I32)
        k_if, invf = A.alloc("invf", [32], F32)
        dma("sp", pos_iN, posN_d, [], [k_piN], "c")
        dma("sp", pos_iO, posO_d, [], [k_piO], "c")
        dma("sp", invf, invf_d, [], [k_if], "c")
        k_join, joinb = A.alloc("joinb", [1], F32)
        memset("dve", joinb, 0.0, [k_join, k_identf, k_trif, k_onesf, k_maskM, k_maskF, k_sel, k_bfg, k_piN, k_piO, k_if])
        tcopy("dve", identb, identf, [k_identf], [k_identb])

        def make_tables(pos_i, k_pi, nb, cos_t, k_cos, sin_t, k_sin, scale, tag):
            m = A.mark()
            k_pf, pos_f = A.alloc("pos_f", [nb], F32)
            k_ang, ang = A.alloc("ang", [nb, 32], F32)
            k_t1, t1 = A.alloc("t1", [nb, 32], F32)
            k_kk, kk = A.alloc("kk", [nb, 32], F32)
            k_rr, rr = A.alloc("rr", [nb, 32], F32)
            k_ra, ra = A.alloc("ra", [nb, 32], F32)
            k_bb, bb = A.alloc("bb", [nb, 32], F32)
            k_sg, sg = A.alloc("sg", [nb, 32], F32)
            tcopy("dve", pos_f, pos_i, [k_pi], [k_pf])
            tt("dve", ang, pos_f.unsqueeze(2).broadcast_to([128, nb, 32]), invf.unsqueeze(1).broadcast_to([128, nb, 32]), ALU.mult, [k_pf, k_if], [k_ang])
            MAGIC = 12582912.0
            C1 = 6.28125
            C2 = 2 * PI - 6.28125
            ts("dve", t1, ang, 1.0 / (2 * PI), MAGIC, ALU.mult, ALU.add, [k_ang], [k_t1])
            ts("dve", kk, t1, -MAGIC, None, ALU.add, None, [k_t1], [k_kk])
            stt("dve", t1, kk, -C1, ang, ALU.mult, ALU.add, [k_kk, k_ang], [k_t1])
            stt("dve", rr, kk, -C2, t1, ALU.mult, ALU.add, [k_kk, k_t1], [k_rr])
            ts("dve", rr, rr, -3.1415925, 3.1415925, ALU.max, ALU.min, [k_rr], [k_rr])
            stt("dve", ra, rr, -1.0, rr, ALU.mult, ALU.max, [k_rr], [k_ra])
            act(cos_t, ra, AF.Sin, [k_ra, k_hpib], [k_cos], bias=hpib, scale=-1.0)
            ts("dve", t1, ra, -PI / 2, None, ALU.add, None, [k_ra], [k_t1])
            stt("dve", bb, t1, -1.0, t1, ALU.mult, ALU.max, [k_t1], [k_bb])
            act(sin_t, bb, AF.Sin, [k_bb, k_hpib], [k_sin], bias=hpib, scale=-1.0)
            act(sg, rr, AF.Sign, [k_rr], [k_sg])
            tt("dve", sin_t, sin_t, sg, ALU.mult, [k_sin, k_sg], [k_sin])
            if scale != 1.0:
                ts("dve", cos_t, cos_t, scale, None, ALU.mult, None, [k_cos], [k_cos])
                ts("dve", sin_t, sin_t, scale, None, ALU.mult, None, [k_sin], [k_sin])
            A.release(m)

        make_tables(pos_iN, k_piN, NB, cosN, k_cosN, sinN, k_sinN, 1.0, "n")
        make_tables(pos_iO, k_piO, NO, cosO, k_cosO, sinO, k_sinO, QSCALE, "o")

        if STOP == 0:
            return finish()

        def wload(dst, src, reads_none, k_dst, sem, pieces, axis_len):
            step = axis_len // pieces
            for i in range(pieces):
                dma("pool", dst[:, i * step:(i + 1) * step], src[:, i * step:(i + 1) * step], [], [(k_dst, i)], sem)

        def wkeys(k, pieces):
            return [(k, i) for i in range(pieces)]

        def norm_transpose(x_rows, bufs, s, g_b, k_g, hT, k_hT, col0, tb):
            (k_xt, xt), (k_hn, hn), (k_sq, sqj), (k_ss, ssq), (k_rs, rstd) = bufs
            dma("sp", xt[:, s], x_rows, [], [(k_xt, s)], "xt%d" % s)
            act(sqj, xt[:, s], AF.Square, [(k_xt, s)], [k_sq, (k_ss, s)], accum=ssq[:, s:s + 1])
            rsqrt_mean(ssq[:, s:s + 1], (k_ss, s), D, rstd[:, s:s + 1], (k_rs, s), 1)
            stt("dve", hn[:, s], xt[:, s], rstd[:, s:s + 1], g_b, ALU.mult, ALU.mult, [(k_xt, s), (k_rs, s), k_g], [(k_hn, s)])
            for half in range(2):
                bv = bf_bank(tb[half])
                for kq in range(8):
                    k = half * 8 + kq
                    tr(bv[:, kq, :], hn[:, s, k * 128:(k + 1) * 128], [(k_hn, s)], [BK[tb[half]]])
                tcopy("act" if half == 0 else "dve", hT[:, half * 8:(half + 1) * 8, col0:col0 + 128], bv, [BK[tb[half]]], [(k_hT, col0 // 128, half)])

        def hT_keys(k_hT, blks):
            return [(k_hT, b, h) for b in blks for h in range(2)]

        p1_mark = A.mark()
        k_gattn, gattn = A.alloc("gattn", [D], F32)
        dma("sp", gattn, g_attn.partition_broadcast(128), [], [k_gattn], "c9")
        nb_bufs = (A.alloc("xt", [2, D], F32), A.alloc("hn", [2, D], BF16), A.alloc("sqj", [D], BF16),
                   A.alloc("ssq", [2], F32), A.alloc("rstd", [2], F32))
        k_hT, hT = A.alloc("hT", [16, 512], BF16)
        pO_mark = A.mark()
        k_gq, gq = A.alloc("gq", [512], F32)
        dma("sp", gq, g_q.partition_broadcast(128), [], [k_gq], "c10")
        k_wq, wq = A.alloc("wq", [16, 512], BF16)
        k_wfq, wfq = A.alloc("wfq", [16, 1024], BF16)
        k_wuqn, wuqn = A.alloc("wuqn", [4, 1024], BF16)
        k_wuqp, wuqp = A.alloc("wuqp", [4, 512], BF16)
        w_in_v = w_in.rearrange("(k p) n -> p k n", p=128)
        wload(wq, w_in_v[:, :, 0:512], None, k_wq, "wq", 4, 16)
        w_uq_v = w_uq.rearrange("(k p) (h d) -> p k h d", p=128, d=192)
        for k in range(4):
            dma("pool", wuqn[:, k, :].rearrange("p (h d) -> p h d", d=128), w_uq_v[:, k, :, 0:128], [], [(k_wuqn, k)], "wuqn")
            dma("pool", wuqp[:, k, :].rearrange("p (h d) -> p h d", d=64), w_uq_v[:, k, :, 128:192], [], [(k_wuqp, k)], "wuqp")
        wload(wfq, w_in_v[:, :, 832:1856], None, k_wfq, "wfq", 8, 16)
        k_ssq_q, ssq_q = A.alloc("ssq_q", [1], F32)
        k_rs_q, rs_q = A.alloc("rs_q", [1], F32)
        k_sq2, sq2 = A.alloc("sq2", [512], BF16)
        k_qn, qn = A.alloc("qn", [512], BF16)
        k_qnT, qnT = A.alloc("qnT", [4, 128], BF16)
        k_qnb, qnb = A.alloc("qnb", [1024], BF16)
        k_qpb, qpb = A.alloc("qpb", [8, 64], BF16)
        rt = [A.alloc("rt%d" % i, [8, 32], F32) for i in range(4)]
        k_qtn, qtn_st = A.alloc("qtn_st", [2, 8, 128], BF16)
        k_qtp, qtp_st = A.alloc("qtp_st", [2, 8, 128], BF16)
        k_qf, qf_st = A.alloc("qf_st", [2, 512], BF16)

        QTN_v = QTN.rearrange("h d t -> d h t")
        QTP_v = QTP.rearrange("h d t -> d h t")
        for ot in range(4):
            for blk in range(4):
                a = ot * 4 + blk
                s = a % 2
                norm_transpose(x_own[a * 128:(a + 1) * 128, :], nb_bufs, s, gattn, k_gattn, hT, k_hT, blk * 128, (0, 1))
                hk = hT_keys(k_hT, [blk])
                for k in range(16):
                    mm(banks[2][:, 0:512], hT[:, k, blk * 128:(blk + 1) * 128], wq[:, k, :], k == 0, k == 15, hk + wkeys(k_wq, 4), [BK[2]])
                act(sq2, banks[2][:, 0:512], AF.Square, [BK[2]], [k_sq2, k_ssq_q], accum=ssq_q)
                rsqrt_mean(ssq_q, k_ssq_q, 512, rs_q, k_rs_q, 1)
                stt("dve", qn, banks[2][:, 0:512], rs_q, gq, ALU.mult, ALU.mult, [BK[2], k_rs_q, k_gq], [k_qn])
                b3 = bf_bank(3)
                for k in range(4):
                    tr(b3[:, k, :], qn[:, k * 128:(k + 1) * 128], [k_qn], [BK[3]])
                tcopy("dve", qnT, b3[:, 0:4, :], [BK[3]], [k_qnT])
                for (bk, wsrc, c0, kw) in ((4, wuqn, 0, wkeys(k_wuqn, 4)), (5, wuqn, 512, wkeys(k_wuqn, 4)), (6, wuqp, 0, wkeys(k_wuqp, 4))):
                    for k in range(4):
                        mm(banks[bk][:, 0:512], qnT[:, k, :], wsrc[:, k, c0:c0 + 512], k == 0, k == 3, [k_qnT] + kw, [BK[bk]])
                amul(qnb[:, 0:512], banks[4][:, 0:512], QSCALE, [BK[4]], [(k_qnb, 0)])
                amul(qnb[:, 512:1024], banks[5][:, 0:512], QSCALE, [BK[5]], [(k_qnb, 1)])
                b6 = banks[6][:, 0:512].rearrange("p (h d) -> p h d", d=64)
                x1v, x2v = b6[:, :, 0:32], b6[:, :, 32:64]
                cb_ = cosO[:, a, :].unsqueeze(1).broadcast_to([128, 8, 32])
                sb_ = sinO[:, a, :].unsqueeze(1).broadcast_to([128, 8, 32])
                tt("dve", rt[0][1], x1v, cb_, ALU.mult, [BK[6], k_cosO], [rt[0][0]])
                tt("dve", rt[1][1], x2v, sb_, ALU.mult, [BK[6], k_sinO], [rt[1][0]])
                tt("dve", rt[2][1], x2v, cb_, ALU.mult, [BK[6], k_cosO], [rt[2][0]])
                tt("dve", rt[3][1], x1v, sb_, ALU.mult, [BK[6], k_sinO], [rt[3][0]])
                tt("dve", qpb[:, :, 0:32], rt[0][1], rt[1][1], ALU.subtract, [rt[0][0], rt[1][0]], [(k_qpb, 0)])
                tt("dve", qpb[:, :, 32:64], rt[2][1], rt[3][1], ALU.add, [rt[2][0], rt[3][0]], [(k_qpb, 1)])
                b7 = bf_bank(7)
                for h in range(8):
                    tr(b7[:, h, :], qnb[:, h * 128:(h + 1) * 128], [(k_qnb, h // 4)], [BK[7]])
                for h in range(8):
                    tr(b3[0:64, h, :], qpb[:, h, :], [(k_qpb, 0), (k_qpb, 1)], [BK[3]])
                tcopy("act", qtn_st[:, s], b7, [BK[7]], [(k_qtn, s)])
                tcopy("dve", qtp_st[0:64, s], b3[0:64], [BK[3]], [(k_qtp, s)])
                dma("sp", QTN_v[:, :, a * 128:(a + 1) * 128], qtn_st[:, s], [(k_qtn, s)], [("QTN", a)], "qtn%d" % s)
                dma("sp", QTP_v[:, :, a * 128:(a + 1) * 128], qtp_st[0:64, s], [(k_qtp, s)], [("QTP", a)], "qtp%d" % s)
            hk = hT_keys(k_hT, range(4))
            for h in range(8):
                bk = 4 + (h % 2)
                for k in range(16):
                    mm(banks[bk][:, 0:512], wfq[:, k, h * 128:(h + 1) * 128], hT[:, k, :], k == 0, k == 15, hk + wkeys(k_wfq, 8), [BK[bk]])
                s2 = h % 2
                if h % 2 == 0:
                    amul(qf_st[:, s2], banks[bk][:, 0:512], FSCALE, [BK[bk]], [(k_qf, s2)])
                else:
                    ts("dve", qf_st[:, s2], banks[bk][:, 0:512], FSCALE, None, ALU.mult, None, [BK[bk]], [(k_qf, s2)])
                dma("sp", QTF[h][:, ot * 512:(ot + 1) * 512], qf_st[:, s2], [(k_qf, s2)], [("QTF", h, ot)], "qf%d" % s2)
        A.release(pO_mark)
        if STOP == 1:
            return finish()

        k_gkv, gkv = A.alloc("gkv", [256], F32)
        dma("sp", gkv, g_kv.partition_broadcast(128), [], [k_gkv], "c11")
        k_wlat, wlat = A.alloc("wlat", [16, 328], BF16)
        k_wfk, wfk = A.alloc("wfk", [16, 1024], BF16)
        k_wfv, wfv = A.alloc("wfv", [16, 1024], BF16)
        k_wkn, wkn = A.alloc("wkn", [2, 1024], BF16)
        k_wkv, wkv = A.alloc("wkv", [2, 1024], BF16)
        wload(wlat[:, :, 0:320], w_in_v[:, :, 512:832], None, k_wlat, "wlat", 2, 16)
        dma("pool", wlat[:, :, 320:328], w_in_v[:, :, 3904:3912], [], [(k_wlat, 2)], "wlat2")
        w_ukv_v = w_ukv.rearrange("(k p) (h t d) -> p k t h d", p=128, t=2, d=128)
        for k in range(2):
            dma("pool", wkn[:, k, :].rearrange("p (h d) -> p h d", d=128), w_ukv_v[:, k, 0], [], [(k_wkn, k)], "wkn")
            dma("pool", wkv[:, k, :].rearrange("p (h d) -> p h d", d=128), w_ukv_v[:, k, 1], [], [(k_wkv, k)], "wkv")
        wload(wfv, w_in_v[:, :, 2880:3904], None, k_wfv, "wfv", 8, 16)
        wload(wfk, w_in_v[:, :, 1856:2880], None, k_wfk, "wfk", 8, 16)
        k_ssq_k, ssq_k = A.alloc("ssq_k", [1], F32)
        k_rs_k, rs_k = A.alloc("rs_k", [1], F32)
        k_sq3, sq3 = A.alloc("sq3", [256], BF16)
        k_kvn, kvn = A.alloc("kvn", [256], BF16)
        k_kvnT, kvnT = A.alloc("kvnT", [2, 512], BF16)
        k_kpb, kpb = A.alloc("kpb", [64], BF16)
        rk = [A.alloc("rk%d" % i, [32], F32) for i in range(4)]
        k_vms, vm_st = A.alloc("vm_st", [2, 1024], BF16)
        k_vfs, vf_st = A.alloc("vf_st", [2, 1024], BF16)
        k_kst, kst = A.alloc("kst", [4, 512], BF16)
        wlat_keys = [(k_wlat, 0), (k_wlat, 1), (k_wlat, 2)]
        for nt in range(NT_LIMIT):
            for blk in range(4):
                n = nt * 4 + blk
                s = n % 2
                norm_transpose(x_nat[n * 128:(n + 1) * 128, :], nb_bufs, s, gattn, k_gattn, hT, k_hT, blk * 128, (0, 1))
                hk = hT_keys(k_hT, [blk])
                if "lat" in SKIP:
                    continue
                for k in range(16):
                    mm(banks[2][:, 0:328], hT[:, k, blk * 128:(blk + 1) * 128], wlat[:, k, :], k == 0, k == 15, hk + wlat_keys, [BK[2]])
                if LATSTEP < 2:
                    continue
                act(sq3, banks[2][:, 0:256], AF.Square, [BK[2]], [k_sq3, k_ssq_k], accum=ssq_k)
                rsqrt_mean(ssq_k, k_ssq_k, 256, rs_k, k_rs_k, 1)
                stt("dve", kvn, banks[2][:, 0:256], rs_k, gkv, ALU.mult, ALU.mult, [BK[2], k_rs_k, k_gkv], [k_kvn])
                if LATSTEP < 3:
                    continue
                x1v, x2v = banks[2][:, 256:288], banks[2][:, 288:320]
                cb_, sb_ = cosN[:, n, :], sinN[:, n, :]
                tt("dve", rk[0][1], x1v, cb_, ALU.mult, [BK[2], k_cosN], [rk[0][0]])
                tt("dve", rk[1][1], x2v, sb_, ALU.mult, [BK[2], k_sinN], [rk[1][0]])
                tt("dve", rk[2][1], x2v, cb_, ALU.mult, [BK[2], k_cosN], [rk[2][0]])
                tt("dve", rk[3][1], x1v, sb_, ALU.mult, [BK[2], k_sinN], [rk[3][0]])
                tt("dve", kpb[:, 0:32], rk[0][1], rk[1][1], ALU.subtract, [rk[0][0], rk[1][0]], [(k_kpb, 0)])
                tt("dve", kpb[:, 32:64], rk[2][1], rk[3][1], ALU.add, [rk[2][0], rk[3][0]], [(k_kpb, 1)])
                if LATSTEP < 4:
                    continue
                tcopy("act", flog[:, n, :], banks[2][:, 320:328], [BK[2]], [(k_flog, n)])
                if LATSTEP < 5:
                    continue
                b3 = bf_bank(3)
                for k in range(2):
                    tr(b3[:, k, :], kvn[:, k * 128:(k + 1) * 128], [k_kvn], [BK[3]])
                tr(b3[0:64, 2, :], kpb, [(k_kpb, 0), (k_kpb, 1)], [BK[3]])
                tcopy("act", kvnT[:, :, blk * 128:(blk + 1) * 128], b3[:, 0:2, :], [BK[3]], [(k_kvnT, blk)])
                tcopy("dve", kpeT[0:64, n * 128:(n + 1) * 128], b3[0:64, 2, :], [BK[3]], [(k_kpeT, n)])
                if "v" in SKIP:
                    continue
                for half in range(2):
                    bk = 4 + half
                    for k in range(2):
                        mm(banks[bk][:, 0:512], kvnT[:, k, blk * 128:(blk + 1) * 128], wkv[:, k, half * 512:(half + 1) * 512], k == 0, k == 1, [(k_kvnT, blk)] + wkeys(k_wkv, 2), [BK[bk]])
                    tcopy("act" if half == 0 else "dve", vm_st[:, s, half * 512:(half + 1) * 512], banks[bk][:, 0:512], [BK[bk]], [(k_vms, s, half)])
                dma("sp", VM[n * 128:(n + 1) * 128, :], vm_st[:, s], [(k_vms, s, 0), (k_vms, s, 1)], [("VM", n)], "vm%d" % s)
                for half in range(2):
                    bk = 6 + half
                    for k in range(16):
                        mm(banks[bk][:, 0:512], hT[:, k, blk * 128:(blk + 1) * 128], wfv[:, k, half * 512:(half + 1) * 512], k == 0, k == 15, hk + wkeys(k_wfv, 8), [BK[bk]])
                    tcopy("act" if half == 0 else "dve", vf_st[:, s, half * 512:(half + 1) * 512], banks[bk][:, 0:512], [BK[bk]], [(k_vfs, s, half)])
                dma("sp", VF[n * 128:(n + 1) * 128, :], vf_st[:, s], [(k_vfs, s, 0), (k_vfs, s, 1)], [("VF", n)], "vf%d" % s)
            hk = hT_keys(k_hT, range(4))
            kc = 0
            if "k" in SKIP:
                continue
            for h in range(8):
                bk = 4 + (h % 2)
                for k in range(16):
                    mm(banks[bk][:, 0:512], wfk[:, k, h * 128:(h + 1) * 128], hT[:, k, :], k == 0, k == 15, hk + wkeys(k_wfk, 8), [BK[bk]])
                s2 = kc % 4
                kc += 1
                tcopy("act" if h % 2 == 0 else "dve", kst[:, s2], banks[bk][:, 0:512], [BK[bk]], [(k_kst, s2)])
                dma("sp", KTF[h][:, nt * 512:(nt + 1) * 512], kst[:, s2], [(k_kst, s2)], [("KTF", h, nt)], "kst%d" % s2)
            kT_keys = [(k_kvnT, b) for b in range(4)]
            for h in range(8):
                bk = 6 + (h % 2)
                for k in range(2):
                    mm(banks[bk][:, 0:512], wkn[:, k, h * 128:(h + 1) * 128], kvnT[:, k, :], k == 0, k == 1, kT_keys + wkeys(k_wkn, 2), [BK[bk]])
                s2 = kc % 4
                kc += 1
                tcopy("act" if h % 2 == 0 else "dve", kst[:, s2], banks[bk][:, 0:512], [BK[bk]], [(k_kst, s2)])
                dma("sp", KTM[h][:, nt * 512:(nt + 1) * 512], kst[:, s2], [(k_kst, s2)], [("KTM", h, nt)], "kst%d" % s2)
        A.release(p1_mark)

        if "cum" in SKIP:
            return finish()
        m = A.mark()
        flog_keys = [(k_flog, n) for n in range(NB)]
        k_z, z = A.alloc("z", [NB, 8], F32)
        k_l, lpos = A.alloc("lpos", [NB, 8], F32)
        k_tb, tbc = A.alloc("tbc", [NB, 8], F32)
        k_pre, pre = A.alloc("pre", [NB, 8], F32)
        k_t4, t4 = A.alloc("t4", [NO, 8, NB], F32)
        tt("dve", z, flog, bfg.unsqueeze(1).broadcast_to([128, NB, 8]), ALU.add, flog_keys + [k_bfg], [k_z])
        ts("dve", z, z, -80.0, None, ALU.max, None, [k_z], [k_z])
        act(lpos, z, AF.Exp, [k_z], [k_l], scale=-1.0)
        act(lpos, lpos, AF.Ln, [k_l, k_oneb], [k_l], bias=oneb, scale=1.0)
        lflat = lpos.rearrange("p n h -> p (n h)")
        mm(banks[0][:, 0:256], trif, lflat, True, True, [k_trif, k_l], [BK[0]])
        mm(banks[1][:, 0:256], onesf, lflat, True, True, [k_onesf, k_l], [BK[1]])
        tcopy("dve", tbc.rearrange("p n h -> p (n h)"), banks[1][:, 0:256], [BK[1]], [k_tb])
        memset("dve", pre[:, 0, :], 0.0, [(k_pre, 0)])
        for n in range(1, NB):
            tt("dve", pre[:, n, :], pre[:, n - 1, :], tbc[:, n - 1, :], ALU.add, [(k_pre, n - 1), k_tb], [(k_pre, n)])
        tt("dve", negc.rearrange("p n h -> p (n h)"), banks[0][:, 0:256], pre.rearrange("p n h -> p (n h)"), ALU.add,
           [BK[0]] + [(k_pre, n) for n in range(NB)], [k_negc])
        tt("dve", t4, sel.unsqueeze(2).broadcast_to([128, NO, 8, NB]),
           negc.rearrange("p n h -> p h n").unsqueeze(1).broadcast_to([128, NO, 8, NB]), ALU.mult, [k_sel, k_negc], [k_t4])
        P.op("dve", lambda e: e.tensor_reduce(out=cown, in_=t4, axis=AX.X, op=ALU.add), reads=[k_t4], writes=[k_cown])
        A.release(m)

        if STOP == 2:
            return finish()

        p2_mark = A.mark()
        k_gout, gout = A.alloc("gout", [D], F32)
        dma("sp", gout, g_out.partition_broadcast(128), [], [k_gout], "c12")
        k_KT, KT = A.alloc("KT", [2, S], BF16)
        k_V, V = A.alloc("V", [2, NB, 130], BF16)
        k_QT, QT = A.alloc("QT", [2, S // 2], BF16)
        k_QP, QP = A.alloc("QP", [2, S // 2], BF16)
        k_cb, cb = A.alloc("cb", [S // 2], F32)
        k_dg, dg = A.alloc("dg", [2, 128], F32)
        k_PT, PT = A.alloc("PT", [3, 512], BF16)
        k_tmp, tmpF = A.alloc("tmpF", [3, 512], F32)
        k_rec, rec = A.alloc("rec", [4], F32)
        k_osq, osq = A.alloc("osq", [128], BF16)
        k_obf, obf = A.alloc("obf", [4, 128], BF16)
        k_OTh, OTh = A.alloc("OTh", [2, S // 2], BF16)
        for sl in range(2):
            memset("dve", V[:, sl, :, 128:130], 1.0, [(k_V, sl, "ones")])
        VM_v = VM.rearrange("(n p) c -> p n c", p=128)
        VF_v = VF.rearrange("(n p) c -> p n c", p=128)
        SB = (0, 1, 2)
        OB = ((3, 4), (5, 6))
        gcount = 0
        pcount = 0
        tcount = 0
        LAG = 2
        for hh in range(16):
            fox = hh >= 8
            h = hh % 8
            sl = hh % 2
            KTsrc, Vsrc, Qsrc = (KTF, VF_v, QTF) if fox else (KTM, VM_v, QTN)
            ktag = "KTF" if fox else "KTM"
            for i in range(4):
                dma("sp", KT[:, sl, i * 1024:(i + 1) * 1024], KTsrc[h][:, i * 1024:(i + 1) * 1024],
                    [(ktag, h, 2 * i), (ktag, h, 2 * i + 1)], [(k_KT, sl, i)], "KT%d_%d" % (sl, i))
            vtag = "VF" if fox else "VM"
            for i in range(4):
                dma("sp", V[:, sl, i * 8:(i + 1) * 8, 0:128], Vsrc[:, i * 8:(i + 1) * 8, h * 128:(h + 1) * 128],
                    [(vtag, n) for n in range(i * 8, (i + 1) * 8)], [(k_V, sl, i)], "V%d" % sl)
            if fox:
                dma("sp", QT[:, sl, :], Qsrc[h], [("QTF", h, ot) for ot in range(4)], [(k_QT, sl)], "QT%d" % sl)
            else:
                dma("sp", QT[:, sl, :], Qsrc[h], [("QTN", a) for a in range(NO)], [(k_QT, sl)], "QT%d" % sl)
                dma("sp", QP[0:64, sl, :], QTP[h], [("QTP", a) for a in range(NO)], [(k_QP, sl)], "QP%d" % sl)
            KT_keys = [(k_KT, sl, i) for i in range(4)]
            V_keys = [(k_V, sl, i) for i in range(4)] + [(k_V, sl, "ones")]
            if fox:
                for a in range(NO):
                    ds_ = a % 2
                    ts("dve", dg[:, ds_], identf, cown[:, a, h:h + 1], -1.0, ALU.mult, ALU.mult, [k_identf, k_cown], [(k_dg, ds_)])
                    mm(banks[7][:, (a % 4) * 128:(a % 4 + 1) * 128], onesf, dg[:, ds_], True, True, [k_onesf, (k_dg, ds_)], [BK[7]])
                    if a % 4 == 3:
                        tcopy("dve", cb[:, (a - 3) * 128:(a + 1) * 128], banks[7][:, 0:512], [BK[7]], [(k_cb, a // 4)])
            for j in range(8):
                nkb = 4 * (j + 1)
                npairs = nkb // 2
                jq = j * 256
                ob = OB[gcount % 2]
                gcount += 1
                pt_slots = {}
                for pp in range(npairs + LAG):
                    if pp < npairs:
                        n0 = 2 * pp
                        sbk = SB[pcount % 3]
                        ps_ = pcount % 3
                        pcount += 1
                        pt_slots[pp] = ps_
                        for i in range(2):
                            n = n0 + i
                            mm(banks[sbk][:, i * 256:(i + 1) * 256], KT[:, sl, n * 128:(n + 1) * 128], QT[:, sl, jq:jq + 256],
                               True, fox, [(k_KT, sl, n // 8), (k_QT, sl)], [BK[sbk]])
                            if not fox:
                                mm(banks[sbk][:, i * 256:(i + 1) * 256], kpeT[0:64, n * 128:(n + 1) * 128], QP[0:64, sl, jq:jq + 256],
                                   False, True, [(k_kpeT, n), (k_QP, sl)], [BK[sbk]])
                        ingroup = n0 >= 4 * j
                        kbl = n0 - 4 * j
                        if not fox:
                            if ingroup:
                                tt("dve", tmpF[:, ps_], banks[sbk][:, 0:512], maskM[:, kbl * 256:(kbl + 2) * 256], ALU.add, [BK[sbk], k_maskM], [(k_tmp, ps_)])
                                act(PT[:, ps_], tmpF[:, ps_], AF.Exp, [(k_tmp, ps_)], [(k_PT, ps_)])
                            else:
                                act(PT[:, ps_], banks[sbk][:, 0:512], AF.Exp, [BK[sbk]], [(k_PT, ps_)])
                        else:
                            for i in range(2):
                                n = n0 + i
                                stt("dve", tmpF[:, ps_, i * 256:(i + 1) * 256], banks[sbk][:, i * 256:(i + 1) * 256], negc[:, n, h:h + 1],
                                    cb[:, jq:jq + 256], ALU.add, ALU.add, [BK[sbk], k_negc, (k_cb, j // 2)], [(k_tmp, ps_, i)])
                            tk = [(k_tmp, ps_, 0), (k_tmp, ps_, 1)]
                            if ingroup:
                                tt("dve", tmpF[:, ps_], tmpF[:, ps_], maskF[:, kbl * 256:(kbl + 2) * 256], ALU.add, tk + [k_maskF], tk)
                            act(PT[:, ps_], tmpF[:, ps_], AF.Exp, tk, [(k_PT, ps_)])
                    if pp >= LAG:
                        q = pp - LAG
                        ps_ = pt_slots[q]
                        for i in range(2):
                            n = 2 * q + i
                            for al in range(2):
                                mm(banks[ob[al]][:, 0:129], PT[:, ps_, i * 256 + al * 128:i * 256 + (al + 1) * 128], V[:, sl, n, 0:129],
                                   n == 0, n == nkb - 1, [(k_PT, ps_)] + V_keys, [BK[ob[al]]])
                for al in range(2):
                    a = 2 * j + al
                    r_ = (gcount % 2) * 2 + al
                    obk = banks[ob[al]]
                    P.op("dve", lambda e, o=rec[:, r_:r_ + 1], i_=obk[:, 128:129]: e.reciprocal(out=o, in_=i_), reads=[BK[ob[al]]], writes=[(k_rec, r_)])
                    act(osq, obk[:, 0:128], AF.Square, [BK[ob[al]], (k_rec, r_)], [k_osq, (k_ssa, a, hh)], scale=rec[:, r_:r_ + 1],
                        accum=ssq_att[:, a, hh:hh + 1])
                    stt("dve", obf[:, r_], obk[:, 0:128], rec[:, r_:r_ + 1], gout[:, hh * 128:(hh + 1) * 128], ALU.mult, ALU.mult,
                        [BK[ob[al]], (k_rec, r_), k_gout], [(k_obf, r_)])
                    b7 = bf_bank(7)
                    tsl = tcount % 8
                    tcount += 1
                    tr(b7[:, tsl, :], obf[:, r_], [(k_obf, r_)], [BK[7]])
                    tcopy("act", OTh[:, sl, a * 128:(a + 1) * 128], b7[:, tsl, :], [BK[7]], [(k_OTh, sl, a)])
            dma("sp", OTS[hh], OTh[:, sl], [(k_OTh, sl, a) for a in range(NO)], [("OTS", hh)], "OTh%d" % sl)
        A.release(p2_mark)
        A.release(attn_mark)
        if STOP == 3:
            return finish()

        k_ssum, ssum = A.alloc("ssum", [NO * 2], F32)
        P.op("dve", lambda e: e.tensor_reduce(out=ssum, in_=ssq_att.rearrange("p a (g h) -> p (a g) h", g=2), axis=AX.X, op=ALU.add),
             reads=[(k_ssa, a, hh) for a in range(NO) for hh in range(16)], writes=[k_ssum])
        rsqrt_mean(ssum, k_ssum, 1024, rstd_att, k_rsa, NO * 2)
        k_gffn, gffn = A.alloc("gffn", [D], F32)
        k_gfin, gfin = A.alloc("gfin", [D], F32)
        dma("sp", gffn, g_ffn.partition_broadcast(128), [], [k_gffn], "c13")
        dma("sp", gfin, g_fin.partition_broadcast(128), [], [k_gfin], "c14")
        k_x1, x1 = A.alloc("x1", [8, D], F32)
        k_h2T, h2T = A.alloc("h2T", [16, 1024], BF16)
        k_ss2, ss2 = A.alloc("ss2", [8], F32)
        k_rs2, rs2 = A.alloc("rs2", [8], F32)
        k_ss3, ss3 = A.alloc("ss3", [8], F32)
        k_rs3, rs3 = A.alloc("rs3", [8], F32)
        OTS_v = OTS.rearrange("c d t -> d c t")
        w_out_v = w_out.rearrange("(c p) n -> p c n", p=128)
        w_gate_v = w_gate.rearrange("(k p) n -> p k n", p=128)
        w_up_v = w_up.rearrange("(k p) n -> p k n", p=128)
        for tg in range(2):
            mo = A.mark()
            k_OTg, OTg = A.alloc("OTg", [16, 1024], BF16)
            k_wo, wo = A.alloc("wo", [2, 16, 512], BF16)
            k_hn2, hn2 = A.alloc("hn2", [2, D], BF16)
            k_sq4, sq4 = A.alloc("sq4", [D], BF16)
            for i in range(4):
                dma("sp", OTg[:, i * 4:(i + 1) * 4, :], OTS_v[:, i * 4:(i + 1) * 4, tg * 1024:(tg + 1) * 1024],
                    [("OTS", c) for c in range(i * 4, (i + 1) * 4)], [(k_OTg, i)], "OTg%d" % i)
            for i in range(8):
                a = tg * 8 + i
                dma("sp", x1[:, i, :], x_own[a * 128:(a + 1) * 128, :], [], [(k_x1, i, ct) for ct in range(4)], "xo%d" % i)
            uc = 0
            for ct in range(4):
                ws = ct % 2
                for q in range(4):
                    dma("pool", wo[:, ws, q * 4:(q + 1) * 4, :], w_out_v[:, q * 4:(q + 1) * 4, ct * 512:(ct + 1) * 512], [], [(k_wo, ws, q)], "wo%d_%d" % (ws, q))
                for i in range(8):
                    a = tg * 8 + i
                    bm, bf_ = ((0, 1), (2, 3), (4, 5))[uc % 3]
                    uc += 1
                    for c in range(8):
                        mm(banks[bm][:, 0:512], OTg[:, c, i * 128:(i + 1) * 128], wo[:, ws, c, :], c == 0, c == 7, [(k_OTg, c // 4), (k_wo, ws, c // 4)], [BK[bm]])
                    for c in range(8, 16):
                        mm(banks[bf_][:, 0:512], OTg[:, c, i * 128:(i + 1) * 128], wo[:, ws, c, :], c == 8, c == 15, [(k_OTg, c // 4), (k_wo, ws, c // 4)], [BK[bf_]])
                    xs = x1[:, i, ct * 512:(ct + 1) * 512]
                    stt("dve", xs, banks[bm][:, 0:512], rstd_att[:, 2 * a:2 * a + 1], xs, ALU.mult, ALU.add, [BK[bm], k_rsa, (k_x1, i, ct)], [(k_x1, i, ct)])
                    stt("dve", xs, banks[bf_][:, 0:512], rstd_att[:, 2 * a + 1:2 * a + 2], xs, ALU.mult, ALU.add, [BK[bf_], k_rsa, (k_x1, i, ct)], [(k_x1, i, ct)])
            for i in range(8):
                act(sq4, x1[:, i, :], AF.Square, [(k_x1, i, ct) for ct in range(4)], [k_sq4, (k_ss2, i)], accum=ss2[:, i:i + 1])
            k_l2, lnv2 = A.alloc("lnv2", [8], F32)
            act(lnv2, ss2, AF.Ln, [(k_ss2, i) for i in range(8)] + [k_epsb], [k_l2], bias=epsb, scale=1.0 / D)
            act(rs2, lnv2, AF.Exp, [k_l2], [k_rs2], scale=-0.5)
            for i in range(8):
                s = i % 2
                stt("dve", hn2[:, s], x1[:, i, :], rs2[:, i:i + 1], gffn, ALU.mult, ALU.mult, [(k_x1, i, ct) for ct in range(4)] + [k_rs2, k_gffn], [(k_hn2, s)])
                for half in range(2):
                    tb = 6 + half
                    bv = bf_bank(tb)
                    for kq in range(8):
                        k = half * 8 + kq
                        tr(bv[:, kq, :], hn2[:, s, k * 128:(k + 1) * 128], [(k_hn2, s)], [BK[tb]])
                    tcopy("act" if half == 0 else "dve", h2T[:, half * 8:(half + 1) * 8, i * 128:(i + 1) * 128], bv, [BK[tb]], [(k_h2T, i, half)])
            A.release(mo)
            mf = A.mark()
            k_wg, wg = A.alloc("wg", [3, 16, 128], BF16)
            k_wu, wu = A.alloc("wu", [3, 16, 128], BF16)
            k_wd, wd = A.alloc("wd", [8, D], BF16)
            k_aT, aT = A.alloc("aT", [2, 4, 1024], BF16)
            k_sg, sgt = A.alloc("sgt", [2, 512], F32)
            h2keys = [[(k_h2T, i, hf) for i in range(tt_ * 4, tt_ * 4 + 4) for hf in range(2)] for tt_ in range(2)]
            ucnt = [0]
            dcnt = [0]

            def ffn_units(g):
                for cg in range(4):
                    c = g * 4 + cg
                    w3 = c % 3
                    dma("pool", wg[:, w3], w_gate_v[:, :, c * 128:(c + 1) * 128], [], [(k_wg, w3)], "wg%d" % w3)
                    dma("pool", wu[:, w3], w_up_v[:, :, c * 128:(c + 1) * 128], [], [(k_wu, w3)], "wu%d" % w3)
                    dma("pool", wd[:, c % 8, :], w_down[c * 128:(c + 1) * 128, :], [], [(k_wd, c % 8)], "wd%d" % (c % 8))
                    for tt_ in range(2):
                        bg, bu = ((0, 1), (2, 3), (4, 5))[ucnt[0] % 3]
                        ss = ucnt[0] % 2
                        ucnt[0] += 1
                        for k in range(16):
                            mm(banks[bg][:, 0:512], wg[:, w3, k, :], h2T[:, k, tt_ * 512:(tt_ + 1) * 512], k == 0, k == 15, [(k_wg, w3)] + h2keys[tt_], [BK[bg]])
                        for k in range(16):
                            mm(banks[bu][:, 0:512], wu[:, w3, k, :], h2T[:, k, tt_ * 512:(tt_ + 1) * 512], k == 0, k == 15, [(k_wu, w3)] + h2keys[tt_], [BK[bu]])
                        act(sgt[:, ss], banks[bg][:, 0:512], AF.Silu, [BK[bg]], [(k_sg, ss)])
                        tt("dve", aT[:, g % 2, cg, tt_ * 512:(tt_ + 1) * 512], sgt[:, ss], banks[bu][:, 0:512], ALU.mult, [(k_sg, ss), BK[bu]], [(k_aT, g % 2, cg, tt_)])

            def ffn_down(g):
                for i in range(8):
                    for ct in range(4):
                        bk = 6 + dcnt[0] % 2
                        dcnt[0] += 1
                        for cg in range(4):
                            c = g * 4 + cg
                            mm(banks[bk][:, 0:512], aT[:, g % 2, cg, i * 128:(i + 1) * 128], wd[:, c % 8, ct * 512:(ct + 1) * 512], cg == 0, cg == 3,
                               [(k_aT, g % 2, cg, i // 4), (k_wd, c % 8)], [BK[bk]])
                        xs = x1[:, i, ct * 512:(ct + 1) * 512]
                        tt("dve", xs, banks[bk][:, 0:512], xs, ALU.add, [BK[bk], (k_x1, i, ct)], [(k_x1, i, ct)])

            NG = NCH // 4
            for g in range(NG + 1):
                if g < NG:
                    ffn_units(g)
                if g >= 1:
                    ffn_down(g - 1)
            k_sq5, sq5 = A.alloc("sq5", [D], BF16)
            for i in range(8):
                act(sq5, x1[:, i, :], AF.Square, [(k_x1, i, ct) for ct in range(4)], [k_sq5, (k_ss3, i)], accum=ss3[:, i:i + 1])
            k_l3, lnv3 = A.alloc("lnv3", [8], F32)
            act(lnv3, ss3, AF.Ln, [(k_ss3, i) for i in range(8)] + [k_epsb], [k_l3], bias=epsb, scale=1.0 / D)
            act(rs3, lnv3, AF.Exp, [k_l3], [k_rs3], scale=-0.5)
            for i in range(8):
                a = tg * 8 + i
                xk = [(k_x1, i, ct) for ct in range(4)]
                stt("dve", x1[:, i, :], x1[:, i, :], rs3[:, i:i + 1], gfin, ALU.mult, ALU.mult, xk + [k_rs3, k_gfin], xk)
                dma("sp", out_d[a * 128:(a + 1) * 128, :], x1[:, i, :], xk, [("out", a)], "xo%d" % i)
            A.release(mf)
        return finish()


def own_blocks(par):
    r = []
    for j in range(8):
        r += [4 * j, 4 * j + 3] if par == 0 else [4 * j + 1, 4 * j + 2]
    return r


def _consts(par):
    ob = own_blocks(par)
    p = np.arange(128)
    maskM = np.zeros((128, 4, 2, 128), np.float32)
    maskF = np.zeros((128, 4, 2, 128), np.float32)
    for kbl in range(4):
        for al in range(2):
            n = kbl
            ia = ob[al]
            s_idx = n * 128 + p[:, None]
            t_idx = ia * 128 + p[None, :]
            maskM[:, kbl, al, :] = np.where((s_idx // 64) <= (t_idx // 64), 0.0, NEG)
            maskF[:, kbl, al, :] = np.where(s_idx <= t_idx, 0.0, NEG)
    sel = np.zeros((128, NO, NB), np.float32)
    for a, ia in enumerate(ob):
        sel[:, a, ia] = 1.0
    invf = (10000.0 ** (-np.arange(0, 64, 2, dtype=np.float32) / 64)).astype(np.float32)
    return {
        "c_ident": np.eye(128, dtype=np.float32),
        "c_tri": np.triu(np.ones((128, 128), np.float32)),
        "c_ones": np.ones((128, 128), np.float32),
        "c_invf": np.ascontiguousarray(np.broadcast_to(invf, (128, 32))),
        "c_maskM": maskM.reshape(128, 1024),
        "c_maskF": maskF.reshape(128, 1024),
        "c_sel": sel.reshape(128, NO * NB),
    }


_NC_CACHE = {}


def kernel(x, positions, g_attn_norm, w_in, b_forget, g_q_lat, w_uq, g_kv_lat, w_ukv, g_out_mla, g_out_fox,
           w_out, g_ffn_norm, w_gate, w_up, w_down, g_final_norm):
    f = lambda a: np.ascontiguousarray(np.asarray(a, dtype=np.float32))
    x = f(x)
    positions = np.asarray(positions).astype(np.int32)
    shared = {
        "w_in": f(w_in)[0], "w_uq": f(w_uq)[0], "w_ukv": f(w_ukv)[0], "w_out": f(w_out)[0],
        "w_gate": f(w_gate)[0], "w_up": f(w_up)[0], "w_down": f(w_down)[0],
        "g_attn": f(g_attn_norm)[0], "g_q": f(g_q_lat)[0], "g_kv": f(g_kv_lat)[0],
        "g_out": np.ascontiguousarray(np.concatenate([f(g_out_mla)[0], f(g_out_fox)[0]])),
        "g_ffn": f(g_ffn_norm)[0], "g_fin": f(g_final_norm), "b_fg": f(b_forget)[0],
    }
    consts = [_consts(0), _consts(1)]
    in_maps = []
    for c in range(8):
        b, par = c // 2, c % 2
        ob = own_blocks(par)
        xb = x[b].reshape(NB, 128, D)
        pb = positions[b].reshape(NB, 128)
        m = dict(shared)
        m.update(consts[par])
        m["x_nat"] = x[b]
        m["x_own"] = np.ascontiguousarray(xb[ob].reshape(NO * 128, D))
        m["pos_nat"] = np.ascontiguousarray(pb.T)
        m["pos_own"] = np.ascontiguousarray(pb[ob].T)
        in_maps.append(m)
    if STOP < 4:
        for m in in_maps:
            for k in ("w_out", "w_gate", "w_up", "w_down"):
                m[k] = np.ascontiguousarray(m[k][0:128])
            if STOP < 2:
                m["x_nat"] = np.ascontiguousarray(m["x_nat"][0:128])
    if "nc" not in _NC_CACHE:
        _NC_CACHE["nc"] = build_program()
    res = run_bass_kernel_spmd(_NC_CACHE["nc"], in_maps[:NCORES], core_ids=list(range(NCORES)))
    kernel.last_results = res
    out = np.zeros((4, S, D), np.float32) if NCORES < 8 else np.empty((4, S, D), np.float32)
    for c in range(NCORES):
        b, par = c // 2, c % 2
        ob = own_blocks(par)
        o = res.results[c]["out"].reshape(NO, 128, D)
        out[b].reshape(NB, 128, D)[ob] = o
    return out
```

```python
import contextlib
import math
import numpy as np
import concourse.bass as bass
import concourse.mybir as mybir
from concourse.bass_utils import run_bass_kernel_spmd

F32 = mybir.dt.float32
BF16 = mybir.dt.bfloat16
I32 = mybir.dt.int32
U8 = mybir.dt.uint8
AF = mybir.ActivationFunctionType
ALU = mybir.AluOpType
AX = mybir.AxisListType

D = 2048
S = 4096
NB = 32
NO = 16
FF = 5632
NCH = FF // 128
EPS = 1e-6
QSCALE = 1.0 / math.sqrt(192.0)
FSCALE = 1.0 / math.sqrt(128.0)
PI = float(np.pi)
NEG = -1.0e30
ARENA_BYTES = 211968

DEBUG = False
STOP = 99
NCORES = 8
SKIP = set()
NT_LIMIT = 8
LATSTEP = 99
ENGS = ("pe", "act", "dve", "pool", "sp")


class Op:
    __slots__ = ("eng", "fn", "deps", "needed", "count", "dma_sem")

    def __init__(self, eng, fn, dma_sem):
        self.eng = eng
        self.fn = fn
        self.deps = []
        self.needed = False
        self.count = None
        self.dma_sem = dma_sem


def _base(k):
    return k[0] if isinstance(k, tuple) else k


class Prog:
    def __init__(self, nc):
        self.nc = nc
        self.ops = {e: [] for e in ENGS}
        self.state = {}
        self.inherit = {}
        self.touch = {}
        self.last_dma = {}

    def op(self, eng, fn, reads=(), writes=(), dma_sem=None):
        o = Op(eng, fn, dma_sem)
        deps = []
        for b in list(reads) + list(writes):
            if b not in self.state:
                self.state[b] = [list(self.inherit.get(_base(b), [])), []]
        for b in reads:
            deps.extend(self.state[b][0])
            if _base(b) == "ps":
                deps.extend(r for r in self.state[b][1] if r.eng != eng)
        for b in writes:
            st = self.state[b]
            deps.extend(st[0])
            deps.extend(st[1])
        seen = set()
        for d in deps:
            if id(d) in seen or d is o:
                continue
            seen.add(id(d))
            if d.eng == "pe" and eng == "pe" and d.dma_sem is None and dma_sem is None:
                continue
            if d.dma_sem is not None:
                d = self.last_dma[d.dma_sem]
            d.needed = True
            o.deps.append(d)
        for b in reads:
            rl = self.state[b][1]
            if dma_sem is None:
                for i_ in range(len(rl)):
                    if rl[i_].eng == eng and rl[i_].dma_sem is None:
                        del rl[i_]
                        break
            rl.append(o)
        for b in writes:
            self.state[b] = [[o], []]
        tag = ("d", dma_sem) if dma_sem is not None else ("e", eng)
        for b in list(reads) + list(writes):
            self.touch.setdefault(_base(b), {})[tag] = o
        if dma_sem is not None:
            self.last_dma[dma_sem] = o
        self.ops[eng].append(o)
        return o

    def emit(self):
        nc = self.nc
        sem_names = set()
        for e in ENGS:
            for o in self.ops[e]:
                if o.dma_sem is not None:
                    sem_names.add("d_" + o.dma_sem)
        sem_names = sorted(sem_names)
        all_names = ["e_" + e for e in ENGS] + sem_names
        with contextlib.ExitStack() as st:
            sems = {n: st.enter_context(nc.semaphore(n)) for n in all_names}
            cnt = {n: 0 for n in all_names}
            for e in ENGS:
                for o in self.ops[e]:
                    if o.dma_sem is not None:
                        n = "d_" + o.dma_sem
                        cnt[n] += 16
                        o.count = (n, cnt[n])
                    elif o.needed:
                        n = "e_" + e
                        cnt[n] += 1
                        o.count = (n, cnt[n])
            blk = st.enter_context(nc.Block())

            def run(engobj, e):
                waited = {}
                for o in self.ops[e]:
                    need = {}
                    for d in o.deps:
                        n, v = d.count
                        if v > need.get(n, 0):
                            need[n] = v
                    for n, v in need.items():
                        if waited.get(n, 0) >= v:
                            continue
                        waited[n] = v
                        engobj.wait_ge(sems[n], v)
                    ins = o.fn(engobj)
                    if o.dma_sem is not None:
                        ins.then_inc(sems[o.count[0]], 16)
                    elif o.needed:
                        ins.then_inc(sems[o.count[0]], 1)
                if e == "sp":
                    for n in sem_names:
                        if cnt[n] > 0:
                            engobj.wait_ge(sems[n], cnt[n])

            blk.tensor(lambda t: run(t, "pe"))
            blk.scalar(lambda t: run(t, "act"))
            blk.vector(lambda t: run(t, "dve"))
            blk.gpsimd(lambda t: run(t, "pool"))
            blk.sync(lambda t: run(t, "sp"))


_DTSZ = {F32: 4, BF16: 2, I32: 4, U8: 1}


class Arena:
    def __init__(self, P, arena_ap):
        self.P = P
        self.a = arena_ap
        self.top = 0
        self.live = []
        self.dead = []
        self.uid = 0
        self.peak = 0

    def alloc(self, name, shape, dt, parts=128):
        self.uid += 1
        name = "%s#%d" % (name, self.uid)
        n = 1
        for s in shape:
            n *= s
        size = (n * _DTSZ[dt] + 63) // 64 * 64
        off = self.top
        self.top += size
        self.peak = max(self.peak, self.top)
        assert self.top <= ARENA_BYTES, ("SBUF arena overflow", name, self.top)
        inh = []
        for (nm, o, s) in self.dead:
            if o < off + size and off < o + s:
                inh.extend(self.P.touch.get(nm, {}).values())
        self.P.inherit[name] = inh
        self.live.append((name, off, size))
        v = self.a[0:parts, off:off + n * _DTSZ[dt]].bitcast(dt)
        if len(shape) == 2:
            v = v.rearrange("p (a b) -> p a b", a=shape[0])
        elif len(shape) == 3:
            v = v.rearrange("p (a b c) -> p a b c", a=shape[0], b=shape[1])
        return name, v

    def mark(self):
        return (self.top, len(self.live))

    def release(self, mark):
        top, nl = mark
        self.dead.extend(self.live[nl:])
        del self.live[nl:]
        self.top = top


def build_program():
    nc = bass.Bass("TRN2", target_bir_lowering=False)

    def din(name, shape, dt=F32):
        return nc.dram_tensor(name, list(shape), dt, kind="ExternalInput").ap()

    big2 = STOP >= 2
    big4 = STOP >= 4
    x_nat = din("x_nat", [S, D] if big2 else [128, D])
    x_own = din("x_own", [S // 2, D])
    posN_d = din("pos_nat", [128, NB], I32)
    posO_d = din("pos_own", [128, NO], I32)
    w_in = din("w_in", [D, 3912])
    w_uq = din("w_uq", [512, 1536])
    w_ukv = din("w_ukv", [256, 2048])
    w_out = din("w_out", [D, D] if big4 else [128, D])
    w_gate = din("w_gate", [D, FF] if big4 else [128, FF])
    w_up = din("w_up", [D, FF] if big4 else [128, FF])
    w_down = din("w_down", [FF, D] if big4 else [128, D])
    g_attn = din("g_attn", [D])
    g_q = din("g_q", [512])
    g_kv = din("g_kv", [256])
    g_out = din("g_out", [D])
    g_ffn = din("g_ffn", [D])
    g_fin = din("g_fin", [D])
    b_fg = din("b_fg", [8])
    ident_d = din("c_ident", [128, 128])
    tri_d = din("c_tri", [128, 128])
    ones_d = din("c_ones", [128, 128])
    invf_d = din("c_invf", [128, 32])
    maskM_d = din("c_maskM", [128, 1024])
    maskF_d = din("c_maskF", [128, 1024])
    sel_d = din("c_sel", [128, NO * NB])
    out_d = nc.dram_tensor("out", [S // 2, D], F32, kind="ExternalOutput").ap()

    skind = "ExternalOutput" if DEBUG else "Internal"

    def scratch(name, shape):
        return nc.dram_tensor(name, list(shape), BF16, kind=skind).ap()

    KTM = scratch("s_ktm", [8, 128, S])
    KTF = scratch("s_ktf", [8, 128, S])
    VM = scratch("s_vm", [S, 1024])
    VF = scratch("s_vf", [S, 1024])
    QTN = scratch("s_qtn", [8, 128, S // 2])
    QTP = scratch("s_qtp", [8, 64, S // 2])
    QTF = scratch("s_qtf", [8, 128, S // 2])
    OTS = scratch("s_ots", [16, 128, S // 2])

    with contextlib.ExitStack() as es:
        arena_t = es.enter_context(nc.sbuf_tensor("arena", [128, ARENA_BYTES], U8))
        banks = [es.enter_context(nc.psum_tensor("bank%d" % i, [128, 512], F32)) for i in range(8)]
        P = Prog(nc)
        A = Arena(P, arena_t[:])
        BK = [("ps", i) for i in range(8)]

        def bf_bank(i):
            return banks[i][:].bitcast(BF16).rearrange("p (a b) -> p a b", a=8)

        def dma(q, out, in_, reads, writes, sem):
            return P.op(q, lambda e: e.dma_start(out=out, in_=in_), reads=reads, writes=writes, dma_sem=sem)

        def mm(out, lhsT, rhs, start, stop, reads, writes):
            return P.op("pe", lambda e: e.matmul(out, lhsT=lhsT, rhs=rhs, start=start, stop=stop), reads=reads, writes=writes)

        def tr(out, in_, reads, writes):
            return P.op("pe", lambda e: e.transpose(out=out, in_=in_, identity=identb), reads=list(reads) + [k_identb], writes=writes)

        def act(out, in_, func, reads, writes, bias=None, scale=None, accum=None):
            kw = {}
            if bias is not None:
                kw["bias"] = bias
            if scale is not None:
                kw["scale"] = scale
            if accum is not None:
                kw["accum_out"] = accum
            return P.op("act", lambda e: e.activation(out=out, in_=in_, func=func, **kw), reads=reads, writes=writes)

        def amul(out, in_, c, reads, writes):
            return P.op("act", lambda e: e.mul(out=out, in_=in_, mul=c), reads=reads, writes=writes)

        def tcopy(eng, out, in_, reads, writes):
            if eng == "act":
                return P.op("act", lambda e: e.copy(out=out, in_=in_), reads=reads, writes=writes)
            return P.op(eng, lambda e: e.tensor_copy(out=out, in_=in_), reads=reads, writes=writes)

        def tt(eng, out, in0, in1, op, reads, writes):
            return P.op(eng, lambda e: e.tensor_tensor(out=out, in0=in0, in1=in1, op=op), reads=reads, writes=writes)

        def ts(eng, out, in0, s1, s2, op0, op1, reads, writes):
            if s2 is None:
                return P.op(eng, lambda e: e.tensor_scalar(out=out, in0=in0, scalar1=s1, scalar2=None, op0=op0), reads=reads, writes=writes)
            return P.op(eng, lambda e: e.tensor_scalar(out=out, in0=in0, scalar1=s1, scalar2=s2, op0=op0, op1=op1), reads=reads, writes=writes)

        def stt(eng, out, in0, scalar, in1, op0, op1, reads, writes):
            return P.op(eng, lambda e: e.scalar_tensor_tensor(out=out, in0=in0, scalar=scalar, in1=in1, op0=op0, op1=op1), reads=reads, writes=writes)

        def memset(eng, ap, val, writes):
            return P.op(eng, lambda e: e.memset(ap, val), writes=writes)

        def finish():
            P.emit()
            build_program.peak = A.peak
            return nc

        k_identf, identf = A.alloc("identf", [128], F32)
        k_identb, identb = A.alloc("identb", [128], BF16)
        k_epsb, epsb = A.alloc("epsb", [1], F32)
        k_oneb, oneb = A.alloc("oneb", [1], F32)
        k_hpib, hpib = A.alloc("hpib", [1], F32)
        k_rsa, rstd_att = A.alloc("rstd_att", [NO * 2], F32)
        k_ssa, ssq_att = A.alloc("ssq_att", [NO, 16], F32)
        k_recs, recs = A.alloc("recs", [NO, 16], F32)
        dma("sp", identf, ident_d, [], [k_identf], "c")
        memset("dve", epsb, EPS, [k_epsb])
        memset("dve", oneb, 1.0, [k_oneb])
        memset("dve", hpib, PI / 2, [k_hpib])
        cnt_sm = [0]

        k_lnv, lnv_all = A.alloc("lnv", [64], F32)

        def rsqrt_mean(ssq_ap, k_ssq, n_feat, out_ap, k_out, width):
            o = cnt_sm[0] % 2
            cnt_sm[0] += 1
            lnv = lnv_all[:, o * 32:o * 32 + width]
            act(lnv, ssq_ap, AF.Ln, [k_ssq, k_epsb], [(k_lnv, o)], bias=epsb, scale=1.0 / n_feat)
            act(out_ap, lnv, AF.Exp, [(k_lnv, o)], [k_out], scale=-0.5)

        attn_mark = A.mark()
        k_trif, trif = A.alloc("trif", [128], F32)
        k_onesf, onesf = A.alloc("onesf", [128], F32)
        k_maskM, maskM = A.alloc("maskM", [1024], F32)
        k_maskF, maskF = A.alloc("maskF", [1024], F32)
        k_sel, sel = A.alloc("sel", [NO, NB], F32)
        k_cosN, cosN = A.alloc("cosN", [NB, 32], F32)
        k_sinN, sinN = A.alloc("sinN", [NB, 32], F32)
        k_cosO, cosO = A.alloc("cosO", [NO, 32], F32)
        k_sinO, sinO = A.alloc("sinO", [NO, 32], F32)
        k_flog, flog = A.alloc("flog", [NB, 8], F32)
        k_negc, negc = A.alloc("negc", [NB, 8], F32)
        k_cown, cown = A.alloc("cown", [NO, 8], F32)
        k_bfg, bfg = A.alloc("bfg", [8], F32)
        k_kpeT, kpeT = A.alloc("kpeT", [S], BF16)
        dma("sp", trif, tri_d, [], [k_trif], "c")
        dma("sp", onesf, ones_d, [], [k_onesf], "c")
        dma("sp", maskM, maskM_d, [], [k_maskM], "c")
        dma("sp", maskF, maskF_d, [], [k_maskF], "c")
        dma("sp", sel, sel_d.rearrange("p (a n) -> p a n", a=NO), [], [k_sel], "c")
        dma("sp", bfg, b_fg.partition_broadcast(128), [], [k_bfg], "c")

        k_piN, pos_iN = A.alloc("pos_iN", [NB], I32)
        k_piO, pos_iO = A.alloc("pos_iO", [NO], I32)
        k_if, invf = A.alloc("invf", [32], F32)
        dma("sp", pos_iN, posN_d, [], [k_piN], "c")
        dma("sp", pos_iO, posO_d, [], [k_piO], "c")
        dma("sp", invf, invf_d, [], [k_if], "c")
        k_join, joinb = A.alloc("joinb", [1], F32)
        memset("dve", joinb, 0.0, [k_join, k_identf, k_trif, k_onesf, k_maskM, k_maskF, k_sel, k_bfg, k_piN, k_piO, k_if])
        tcopy("dve", identb, identf, [k_identf], [k_identb])

        def make_tables(pos_i, k_pi, nb, cos_t, k_cos, sin_t, k_sin, scale, tag):
            m = A.mark()
            k_pf, pos_f = A.alloc("pos_f", [nb], F32)
            k_ang, ang = A.alloc("ang", [nb, 32], F32)
            k_t1, t1 = A.alloc("t1", [nb, 32], F32)
            k_kk, kk = A.alloc("kk", [nb, 32], F32)
            k_rr, rr = A.alloc("rr", [nb, 32], F32)
            k_ra, ra = A.alloc("ra", [nb, 32], F32)
            k_bb, bb = A.alloc("bb", [nb, 32], F32)
            k_sg, sg = A.alloc("sg", [nb, 32], F32)
            tcopy("dve", pos_f, pos_i, [k_pi], [k_pf])
            tt("dve", ang, pos_f.unsqueeze(2).broadcast_to([128, nb, 32]), invf.unsqueeze(1).broadcast_to([128, nb, 32]), ALU.mult, [k_pf, k_if], [k_ang])
            MAGIC = 12582912.0
            C1 = 6.28125
            C2 = 2 * PI - 6.28125
            ts("dve", t1, ang, 1.0 / (2 * PI), MAGIC, ALU.mult, ALU.add, [k_ang], [k_t1])
            ts("dve", kk, t1, -MAGIC, None, ALU.add, None, [k_t1], [k_kk])
            stt("dve", t1, kk, -C1, ang, ALU.mult, ALU.add, [k_kk, k_ang], [k_t1])
            stt("dve", rr, kk, -C2, t1, ALU.mult, ALU.add, [k_kk, k_t1], [k_rr])
            ts("dve", rr, rr, -3.1415925, 3.1415925, ALU.max, ALU.min, [k_rr], [k_rr])
            stt("dve", ra, rr, -1.0, rr, ALU.mult, ALU.max, [k_rr], [k_ra])
            act(cos_t, ra, AF.Sin, [k_ra, k_hpib], [k_cos], bias=hpib, scale=-1.0)
            ts("dve", t1, ra, -PI / 2, None, ALU.add, None, [k_ra], [k_t1])
            stt("dve", bb, t1, -1.0, t1, ALU.mult, ALU.max, [k_t1], [k_bb])
            act(sin_t, bb, AF.Sin, [k_bb, k_hpib], [k_sin], bias=hpib, scale=-1.0)
            act(sg, rr, AF.Sign, [k_rr], [k_sg])
            tt("dve", sin_t, sin_t, sg, ALU.mult, [k_sin, k_sg], [k_sin])
            if scale != 1.0:
                ts("dve", cos_t, cos_t, scale, None, ALU.mult, None, [k_cos], [k_cos])
                ts("dve", sin_t, sin_t, scale, None, ALU.mult, None, [k_sin], [k_sin])
            A.release(m)

        make_tables(pos_iN, k_piN, NB, cosN, k_cosN, sinN, k_sinN, 1.0, "n")
        make_tables(pos_iO, k_piO, NO, cosO, k_cosO, sinO, k_sinO, QSCALE, "o")

        if STOP == 0:
            return finish()

        def wload(dst, src, reads_none, k_dst, sem, pieces, axis_len):
            step = axis_len // pieces
            for i in range(pieces):
                dma("pool", dst[:, i * step:(i + 1) * step], src[:, i * step:(i + 1) * step], [], [(k_dst, i)], sem)

        def wkeys(k, pieces):
            return [(k, i) for i in range(pieces)]

        def norm_part(x_rows, bufs, s, g_b, k_g):
            (k_xt, xt), (k_hn, hn), (k_sq, sqj), (k_ss, ssq), (k_rs, rstd) = bufs
            dma("sp", xt[:, s], x_rows, [], [(k_xt, s)], "xt%d" % s)
            act(sqj, xt[:, s], AF.Square, [(k_xt, s)], [k_sq, (k_ss, s)], accum=ssq[:, s:s + 1])
            rsqrt_mean(ssq[:, s:s + 1], (k_ss, s), D, rstd[:, s:s + 1], (k_rs, s), 1)
            stt("dve", hn[:, s], xt[:, s], rstd[:, s:s + 1], g_b, ALU.mult, ALU.mult, [(k_xt, s), (k_rs, s), k_g], [(k_hn, s)])

        def transpose_part(bufs, s, hT, k_hT, col0, tb):
            (k_xt, xt), (k_hn, hn), (k_sq, sqj), (k_ss, ssq), (k_rs, rstd) = bufs
            for half in range(2):
                bv = bf_bank(tb[half])
                for kq in range(8):
                    k = half * 8 + kq
                    tr(bv[:, kq, :], hn[:, s, k * 128:(k + 1) * 128], [(k_hn, s)], [BK[tb[half]]])
                tcopy("act" if half == 0 else "dve", hT[:, half * 8:(half + 1) * 8, col0:col0 + 128], bv, [BK[tb[half]]], [(k_hT, col0 // 128, half)])

        def hT_keys(k_hT, blks):
            return [(k_hT, b, h) for b in blks for h in range(2)]

        p1_mark = A.mark()
        k_gattn, gattn = A.alloc("gattn", [D], F32)
        dma("sp", gattn, g_attn.partition_broadcast(128), [], [k_gattn], "c9")
        nb_bufs = (A.alloc("xt", [2, D], F32), A.alloc("hn", [2, D], BF16), A.alloc("sqj", [D], BF16),
                   A.alloc("ssq", [2], F32), A.alloc("rstd", [2], F32))
        k_hT, hT = A.alloc("hT", [16, 512], BF16)
        pO_mark = A.mark()
        k_gq, gq = A.alloc("gq", [512], F32)
        dma("sp", gq, g_q.partition_broadcast(128), [], [k_gq], "c10")
        k_wq, wq = A.alloc("wq", [16, 512], BF16)
        k_wfq, wfq = A.alloc("wfq", [16, 1024], BF16)
        k_wuqn, wuqn = A.alloc("wuqn", [4, 1024], BF16)
        k_wuqp, wuqp = A.alloc("wuqp", [4, 512], BF16)
        w_in_v = w_in.rearrange("(k p) n -> p k n", p=128)
        wload(wq, w_in_v[:, :, 0:512], None, k_wq, "wq", 4, 16)
        w_uq_v = w_uq.rearrange("(k p) (h d) -> p k h d", p=128, d=192)
        for k in range(4):
            dma("pool", wuqn[:, k, :].rearrange("p (h d) -> p h d", d=128), w_uq_v[:, k, :, 0:128], [], [(k_wuqn, k)], "wuqn")
            dma("pool", wuqp[:, k, :].rearrange("p (h d) -> p h d", d=64), w_uq_v[:, k, :, 128:192], [], [(k_wuqp, k)], "wuqp")
        wload(wfq, w_in_v[:, :, 832:1856], None, k_wfq, "wfq", 8, 16)
        k_ssq_q, ssq_q = A.alloc("ssq_q", [1], F32)
        k_rs_q, rs_q = A.alloc("rs_q", [1], F32)
        k_sq2, sq2 = A.alloc("sq2", [512], BF16)
        k_qn, qn = A.alloc("qn", [512], BF16)
        k_qnT, qnT = A.alloc("qnT", [4, 128], BF16)
        k_qnb, qnb = A.alloc("qnb", [1024], BF16)
        k_qpb, qpb = A.alloc("qpb", [8, 64], BF16)
        rt = [A.alloc("rt%d" % i, [8, 32], F32) for i in range(4)]
        k_qtn, qtn_st = A.alloc("qtn_st", [2, 8, 128], BF16)
        k_qtp, qtp_st = A.alloc("qtp_st", [2, 8, 128], BF16)
        k_qf, qf_st = A.alloc("qf_st", [2, 512], BF16)

        QTN_v = QTN.rearrange("h d t -> d h t")
        QTP_v = QTP.rearrange("h d t -> d h t")
        norm_part(x_own[0:128, :], nb_bufs, 0, gattn, k_gattn)
        for ot in range(4):
            for blk in range(4):
                a = ot * 4 + blk
                s = a % 2
                if a + 1 < NO:
                    norm_part(x_own[(a + 1) * 128:(a + 2) * 128, :], nb_bufs, (a + 1) % 2, gattn, k_gattn)
                transpose_part(nb_bufs, s, hT, k_hT, blk * 128, (0, 1))
                hk = hT_keys(k_hT, [blk])
                for k in range(16):
                    mm(banks[2][:, 0:512], hT[:, k, blk * 128:(blk + 1) * 128], wq[:, k, :], k == 0, k == 15, hk + wkeys(k_wq, 4), [BK[2]])
                act(sq2, banks[2][:, 0:512], AF.Square, [BK[2]], [k_sq2, k_ssq_q], accum=ssq_q)
                rsqrt_mean(ssq_q, k_ssq_q, 512, rs_q, k_rs_q, 1)
                stt("dve", qn, banks[2][:, 0:512], rs_q, gq, ALU.mult, ALU.mult, [BK[2], k_rs_q, k_gq], [k_qn])
                b3 = bf_bank(3)
                for k in range(4):
                    tr(b3[:, k, :], qn[:, k * 128:(k + 1) * 128], [k_qn], [BK[3]])
                tcopy("dve", qnT, b3[:, 0:4, :], [BK[3]], [k_qnT])
                for (bk, wsrc, c0, kw) in ((4, wuqn, 0, wkeys(k_wuqn, 4)), (5, wuqn, 512, wkeys(k_wuqn, 4)), (6, wuqp, 0, wkeys(k_wuqp, 4))):
                    for k in range(4):
                        mm(banks[bk][:, 0:512], qnT[:, k, :], wsrc[:, k, c0:c0 + 512], k == 0, k == 3, [k_qnT] + kw, [BK[bk]])
                amul(qnb[:, 0:512], banks[4][:, 0:512], QSCALE, [BK[4]], [(k_qnb, 0)])
                amul(qnb[:, 512:1024], banks[5][:, 0:512], QSCALE, [BK[5]], [(k_qnb, 1)])
                b6 = banks[6][:, 0:512].rearrange("p (h d) -> p h d", d=64)
                x1v, x2v = b6[:, :, 0:32], b6[:, :, 32:64]
                cb_ = cosO[:, a, :].unsqueeze(1).broadcast_to([128, 8, 32])
                sb_ = sinO[:, a, :].unsqueeze(1).broadcast_to([128, 8, 32])
                tt("dve", rt[0][1], x1v, cb_, ALU.mult, [BK[6], k_cosO], [rt[0][0]])
                tt("dve", rt[1][1], x2v, sb_, ALU.mult, [BK[6], k_sinO], [rt[1][0]])
                tt("dve", rt[2][1], x2v, cb_, ALU.mult, [BK[6], k_cosO], [rt[2][0]])
                tt("dve", rt[3][1], x1v, sb_, ALU.mult, [BK[6], k_sinO], [rt[3][0]])
                tt("dve", qpb[:, :, 0:32], rt[0][1], rt[1][1], ALU.subtract, [rt[0][0], rt[1][0]], [(k_qpb, 0)])
                tt("dve", qpb[:, :, 32:64], rt[2][1], rt[3][1], ALU.add, [rt[2][0], rt[3][0]], [(k_qpb, 1)])
                b7 = bf_bank(7)
                for h in range(8):
                    tr(b7[:, h, :], qnb[:, h * 128:(h + 1) * 128], [(k_qnb, h // 4)], [BK[7]])
                for h in range(8):
                    tr(b3[0:64, h, :], qpb[:, h, :], [(k_qpb, 0), (k_qpb, 1)], [BK[3]])
                tcopy("act", qtn_st[:, s], b7, [BK[7]], [(k_qtn, s)])
                tcopy("dve", qtp_st[0:64, s], b3[0:64], [BK[3]], [(k_qtp, s)])
                dma("sp", QTN_v[:, :, a * 128:(a + 1) * 128], qtn_st[:, s], [(k_qtn, s)], [("QTN", a)], "qtn%d" % s)
                dma("sp", QTP_v[:, :, a * 128:(a + 1) * 128], qtp_st[0:64, s], [(k_qtp, s)], [("QTP", a)], "qtp%d" % s)
            hk = hT_keys(k_hT, range(4))
            for h in range(8):
                bk = 4 + (h % 2)
                for k in range(16):
                    mm(banks[bk][:, 0:512], wfq[:, k, h * 128:(h + 1) * 128], hT[:, k, :], k == 0, k == 15, hk + wkeys(k_wfq, 8), [BK[bk]])
                s2 = h % 2
                if h % 2 == 0:
                    amul(qf_st[:, s2], banks[bk][:, 0:512], FSCALE, [BK[bk]], [(k_qf, s2)])
                else:
                    ts("dve", qf_st[:, s2], banks[bk][:, 0:512], FSCALE, None, ALU.mult, None, [BK[bk]], [(k_qf, s2)])
                dma("sp", QTF[h][:, ot * 512:(ot + 1) * 512], qf_st[:, s2], [(k_qf, s2)], [("QTF", h, ot)], "qf%d" % s2)
        A.release(pO_mark)
        if STOP == 1:
            return finish()

        k_gkv, gkv = A.alloc("gkv", [256], F32)
        dma("sp", gkv, g_kv.partition_broadcast(128), [], [k_gkv], "c11")
        k_wlat, wlat = A.alloc("wlat", [16, 328], BF16)
        k_wfk, wfk = A.alloc("wfk", [16, 1024], BF16)
        k_wfv, wfv = A.alloc("wfv", [16, 1024], BF16)
        k_wkn, wkn = A.alloc("wkn", [2, 1024], BF16)
        k_wkv, wkv = A.alloc("wkv", [2, 1024], BF16)
        wload(wlat[:, :, 0:320], w_in_v[:, :, 512:832], None, k_wlat, "wlat", 2, 16)
        dma("pool", wlat[:, :, 320:328], w_in_v[:, :, 3904:3912], [], [(k_wlat, 2)], "wlat2")
        w_ukv_v = w_ukv.rearrange("(k p) (h t d) -> p k t h d", p=128, t=2, d=128)
        for k in range(2):
            dma("pool", wkn[:, k, :].rearrange("p (h d) -> p h d", d=128), w_ukv_v[:, k, 0], [], [(k_wkn, k)], "wkn")
            dma("pool", wkv[:, k, :].rearrange("p (h d) -> p h d", d=128), w_ukv_v[:, k, 1], [], [(k_wkv, k)], "wkv")
        wload(wfv, w_in_v[:, :, 2880:3904], None, k_wfv, "wfv", 8, 16)
        wload(wfk, w_in_v[:, :, 1856:2880], None, k_wfk, "wfk", 8, 16)
        k_ssq_k, ssq_k = A.alloc("ssq_k", [1], F32)
        k_rs_k, rs_k = A.alloc("rs_k", [1], F32)
        k_sq3, sq3 = A.alloc("sq3", [256], BF16)
        k_kvn, kvn = A.alloc("kvn", [256], BF16)
        k_kvnT, kvnT = A.alloc("kvnT", [2, 512], BF16)
        k_kpb, kpb = A.alloc("kpb", [64], BF16)
        rk = [A.alloc("rk%d" % i, [32], F32) for i in range(4)]
        k_vms, vm_st = A.alloc("vm_st", [2, 1024], BF16)
        k_vfs, vf_st = A.alloc("vf_st", [2, 1024], BF16)
        k_kst, kst = A.alloc("kst", [4, 512], BF16)
        wlat_keys = [(k_wlat, 0), (k_wlat, 1), (k_wlat, 2)]
        def b_T(n):
            transpose_part(nb_bufs, n % 2, hT, k_hT, (n % 4) * 128, (0, 1))

        def b_L(n):
            blk = n % 4
            hk = hT_keys(k_hT, [blk])
            for k in range(16):
                mm(banks[2][:, 0:328], hT[:, k, blk * 128:(blk + 1) * 128], wlat[:, k, :], k == 0, k == 15, hk + wlat_keys, [BK[2]])
            act(sq3, banks[2][:, 0:256], AF.Square, [BK[2]], [k_sq3, k_ssq_k], accum=ssq_k)
            rsqrt_mean(ssq_k, k_ssq_k, 256, rs_k, k_rs_k, 1)
            stt("dve", kvn, banks[2][:, 0:256], rs_k, gkv, ALU.mult, ALU.mult, [BK[2], k_rs_k, k_gkv], [k_kvn])
            x1v, x2v = banks[2][:, 256:288], banks[2][:, 288:320]
            cb_, sb_ = cosN[:, n, :], sinN[:, n, :]
            tt("dve", rk[0][1], x1v, cb_, ALU.mult, [BK[2], k_cosN], [rk[0][0]])
            tt("dve", rk[1][1], x2v, sb_, ALU.mult, [BK[2], k_sinN], [rk[1][0]])
            tt("dve", rk[2][1], x2v, cb_, ALU.mult, [BK[2], k_cosN], [rk[2][0]])
            tt("dve", rk[3][1], x1v, sb_, ALU.mult, [BK[2], k_sinN], [rk[3][0]])
            tt("dve", kpb[:, 0:32], rk[0][1], rk[1][1], ALU.subtract, [rk[0][0], rk[1][0]], [(k_kpb, 0)])
            tt("dve", kpb[:, 32:64], rk[2][1], rk[3][1], ALU.add, [rk[2][0], rk[3][0]], [(k_kpb, 1)])
            tcopy("dve", flog[:, n, :], banks[2][:, 320:328], [BK[2]], [(k_flog, n)])

        def b_t(n):
            blk = n % 4
            b3 = bf_bank(3)
            for k in range(2):
                tr(b3[:, k, :], kvn[:, k * 128:(k + 1) * 128], [k_kvn], [BK[3]])
            tr(b3[0:64, 2, :], kpb, [(k_kpb, 0), (k_kpb, 1)], [BK[3]])
            tcopy("act", kvnT[:, :, blk * 128:(blk + 1) * 128], b3[:, 0:2, :], [BK[3]], [(k_kvnT, blk)])
            tcopy("act", kpeT[0:64, n * 128:(n + 1) * 128], b3[0:64, 2, :], [BK[3]], [(k_kpeT, n)])

        def b_VM(n):
            blk = n % 4
            s = n % 2
            for half in range(2):
                bk = 4 + half
                for k in range(2):
                    mm(banks[bk][:, 0:512], kvnT[:, k, blk * 128:(blk + 1) * 128], wkv[:, k, half * 512:(half + 1) * 512], k == 0, k == 1, [(k_kvnT, blk)] + wkeys(k_wkv, 2), [BK[bk]])
                tcopy("act" if half == 0 else "dve", vm_st[:, s, half * 512:(half + 1) * 512], banks[bk][:, 0:512], [BK[bk]], [(k_vms, s, half)])
            dma("sp", VM[n * 128:(n + 1) * 128, :], vm_st[:, s], [(k_vms, s, 0), (k_vms, s, 1)], [("VM", n)], "vm%d" % s)

        def b_VF(n, half):
            blk = n % 4
            s = n % 2
            hk = hT_keys(k_hT, [blk])
            bk = 6 + half
            for k in range(16):
                mm(banks[bk][:, 0:512], hT[:, k, blk * 128:(blk + 1) * 128], wfv[:, k, half * 512:(half + 1) * 512], k == 0, k == 15, hk + wkeys(k_wfv, 8), [BK[bk]])
            tcopy("act" if half == 0 else "dve", vf_st[:, s, half * 512:(half + 1) * 512], banks[bk][:, 0:512], [BK[bk]], [(k_vfs, s, half)])
            if half == 1:
                dma("sp", VF[n * 128:(n + 1) * 128, :], vf_st[:, s], [(k_vfs, s, 0), (k_vfs, s, 1)], [("VF", n)], "vf%d" % s)

        kc_ = [0]

        def b_K(nt):
            hk = hT_keys(k_hT, range(4))
            for h in range(8):
                bk = 4 + (h % 2)
                for k in range(16):
                    mm(banks[bk][:, 0:512], wfk[:, k, h * 128:(h + 1) * 128], hT[:, k, :], k == 0, k == 15, hk + wkeys(k_wfk, 8), [BK[bk]])
                s2 = kc_[0] % 4
                kc_[0] += 1
                tcopy("act" if h % 2 == 0 else "dve", kst[:, s2], banks[bk][:, 0:512], [BK[bk]], [(k_kst, s2)])
                dma("sp", KTF[h][:, nt * 512:(nt + 1) * 512], kst[:, s2], [(k_kst, s2)], [("KTF", h, nt)], "kst%d" % s2)
            kT_keys = [(k_kvnT, b) for b in range(4)]
            for h in range(8):
                bk = 6 + (h % 2)
                for k in range(2):
                    mm(banks[bk][:, 0:512], wkn[:, k, h * 128:(h + 1) * 128], kvnT[:, k, :], k == 0, k == 1, kT_keys + wkeys(k_wkn, 2), [BK[bk]])
                s2 = kc_[0] % 4
                kc_[0] += 1
                tcopy("act" if h % 2 == 0 else "dve", kst[:, s2], banks[bk][:, 0:512], [BK[bk]], [(k_kst, s2)])
                dma("sp", KTM[h][:, nt * 512:(nt + 1) * 512], kst[:, s2], [(k_kst, s2)], [("KTM", h, nt)], "kst%d" % s2)

        NBLK = NT_LIMIT * 4
        norm_part(x_nat[0:128, :], nb_bufs, 0, gattn, k_gattn)
        for n in range(NBLK):
            if n + 1 < NBLK:
                norm_part(x_nat[(n + 1) * 128:(n + 2) * 128, :], nb_bufs, (n + 1) % 2, gattn, k_gattn)
            b_T(n)
            if n >= 1:
                b_VF(n - 1, 0)
            b_L(n)
            if n >= 1:
                b_VM(n - 1)
                b_VF(n - 1, 1)
            b_t(n)
            if n % 4 == 3:
                b_K(n // 4)
        b_VF(NBLK - 1, 0)
        b_VM(NBLK - 1)
        b_VF(NBLK - 1, 1)
        A.release(p1_mark)

        if "cum" in SKIP:
            return finish()
        m = A.mark()
        flog_keys = [(k_flog, n) for n in range(NB)]
        k_z, z = A.alloc("z", [NB, 8], F32)
        k_l, lpos = A.alloc("lpos", [NB, 8], F32)
        k_tb, tbc = A.alloc("tbc", [NB, 8], F32)
        k_pre, pre = A.alloc("pre", [NB, 8], F32)
        k_t4, t4 = A.alloc("t4", [NO, 8, NB], F32)
        tt("dve", z, flog, bfg.unsqueeze(1).broadcast_to([128, NB, 8]), ALU.add, flog_keys + [k_bfg], [k_z])
        ts("dve", z, z, -80.0, None, ALU.max, None, [k_z], [k_z])
        act(lpos, z, AF.Exp, [k_z], [k_l], scale=-1.0)
        act(lpos, lpos, AF.Ln, [k_l, k_oneb], [k_l], bias=oneb, scale=1.0)
        lflat = lpos.rearrange("p n h -> p (n h)")
        mm(banks[0][:, 0:256], trif, lflat, True, True, [k_trif, k_l], [BK[0]])
        mm(banks[1][:, 0:256], onesf, lflat, True, True, [k_onesf, k_l], [BK[1]])
        tcopy("dve", tbc.rearrange("p n h -> p (n h)"), banks[1][:, 0:256], [BK[1]], [k_tb])
        memset("dve", pre[:, 0, :], 0.0, [(k_pre, 0)])
        for n in range(1, NB):
            tt("dve", pre[:, n, :], pre[:, n - 1, :], tbc[:, n - 1, :], ALU.add, [(k_pre, n - 1), k_tb], [(k_pre, n)])
        tt("dve", negc.rearrange("p n h -> p (n h)"), banks[0][:, 0:256], pre.rearrange("p n h -> p (n h)"), ALU.add,
           [BK[0]] + [(k_pre, n) for n in range(NB)], [k_negc])
        tt("dve", t4, sel.unsqueeze(2).broadcast_to([128, NO, 8, NB]),
           negc.rearrange("p n h -> p h n").unsqueeze(1).broadcast_to([128, NO, 8, NB]), ALU.mult, [k_sel, k_negc], [k_t4])
        P.op("dve", lambda e: e.tensor_reduce(out=cown, in_=t4, axis=AX.X, op=ALU.add), reads=[k_t4], writes=[k_cown])
        A.release(m)

        if STOP == 2:
            return finish()

        p2_mark = A.mark()
        k_gout, gout = A.alloc("gout", [D], F32)
        dma("sp", gout, g_out.partition_broadcast(128), [], [k_gout], "c12")
        k_KT, KT = A.alloc("KT", [2, S], BF16)
        k_V, V = A.alloc("V", [2, NB, 130], BF16)
        k_QT, QT = A.alloc("QT", [2, S // 2], BF16)
        k_QP, QP = A.alloc("QP", [2, S // 2], BF16)
        k_cb, cb = A.alloc("cb", [8, S // 2], F32)
        k_dg, dg = A.alloc("dg", [4, 128], F32)
        k_PT, PT = A.alloc("PT", [3, 512], BF16)
        k_tmp, tmpF = A.alloc("tmpF", [3, 512], F32)
        k_osq, osq = A.alloc("osq", [128], F32)
        k_obf, obf = A.alloc("obf", [4, 128], BF16)
        k_of32, of32 = A.alloc("of32", [4, 128], F32)
        k_OTh, OTh = A.alloc("OTh", [2, S // 2], BF16)
        for sl in range(2):
            memset("dve", V[:, sl, :, 128:130], 1.0, [(k_V, sl, "ones")])
        VM_v = VM.rearrange("(n p) c -> p n c", p=128)
        VF_v = VF.rearrange("(n p) c -> p n c", p=128)

        def head_loads(hh):
            fox = hh >= 8
            h = hh % 8
            sl = hh % 2
            KTsrc, Vsrc, Qsrc = (KTF, VF_v, QTF) if fox else (KTM, VM_v, QTN)
            ktag = "KTF" if fox else "KTM"
            for i in range(4):
                dma("sp", KT[:, sl, i * 1024:(i + 1) * 1024], KTsrc[h][:, i * 1024:(i + 1) * 1024],
                    [(ktag, h, 2 * i), (ktag, h, 2 * i + 1)], [(k_KT, sl, i)], "KT%d_%d" % (sl, i))
            vtag = "VF" if fox else "VM"
            for i in range(4):
                dma("sp", V[:, sl, i * 8:(i + 1) * 8, 0:128], Vsrc[:, i * 8:(i + 1) * 8, h * 128:(h + 1) * 128],
                    [(vtag, n) for n in range(i * 8, (i + 1) * 8)], [(k_V, sl, i)], "V%d" % sl)
            if fox:
                dma("sp", QT[:, sl, :], Qsrc[h], [("QTF", h, ot) for ot in range(4)], [(k_QT, sl)], "QT%d" % sl)
            else:
                dma("sp", QT[:, sl, :], Qsrc[h], [("QTN", a) for a in range(NO)], [(k_QT, sl)], "QT%d" % sl)
                dma("sp", QP[0:64, sl, :], QTP[h], [("QTP", a) for a in range(NO)], [(k_QP, sl)], "QP%d" % sl)

        head_loads(0)
        head_loads(1)
        ci = 0
        for h in range(8):
            for a in range(NO):
                ds_ = ci % 4
                bkc = 6 + (ci // 4) % 2
                ts("dve", dg[:, ds_], identf, cown[:, a, h:h + 1], -1.0, ALU.mult, ALU.mult, [k_identf, k_cown], [(k_dg, ds_)])
                mm(banks[bkc][:, (a % 4) * 128:(a % 4 + 1) * 128], onesf, dg[:, ds_], True, True, [k_onesf, (k_dg, ds_)], [BK[bkc]])
                if a % 4 == 3:
                    tcopy("dve", cb[:, h, (a - 3) * 128:(a + 1) * 128], banks[bkc][:, 0:512], [BK[bkc]], [(k_cb, h, a // 4)])
                ci += 1

        SBK = (0, 1, 2)
        OB = ((3, 4), (5, 6))
        LAG = 2
        tasks = []
        gidx = 0
        for hh in range(16):
            for j in range(8):
                npairs = 2 * (j + 1)
                for pp in range(npairs):
                    tasks.append((hh, j, pp, gidx, pp == npairs - 1))
                gidx += 1
        pending_tr = []
        tcount = [0]

        def t_qk(i):
            hh, j, pp, g, last = tasks[i]
            fox = hh >= 8
            sl = hh % 2
            jq = j * 256
            sbk = SBK[i % 3]
            for ii in range(2):
                n = 2 * pp + ii
                mm(banks[sbk][:, ii * 256:(ii + 1) * 256], KT[:, sl, n * 128:(n + 1) * 128], QT[:, sl, jq:jq + 256],
                   True, fox, [(k_KT, sl, n // 8), (k_QT, sl)], [BK[sbk]])
                if not fox:
                    mm(banks[sbk][:, ii * 256:(ii + 1) * 256], kpeT[0:64, n * 128:(n + 1) * 128], QP[0:64, sl, jq:jq + 256],
                       False, True, [(k_kpeT, n), (k_QP, sl)], [BK[sbk]])

        def t_sm(i):
            hh, j, pp, g, last = tasks[i]
            fox = hh >= 8
            h = hh % 8
            jq = j * 256
            sbk = SBK[i % 3]
            ps_ = i % 3
            n0 = 2 * pp
            ingroup = n0 >= 4 * j
            kbl = n0 - 4 * j
            if not fox:
                if ingroup:
                    tt("dve", tmpF[:, ps_], banks[sbk][:, 0:512], maskM[:, kbl * 256:(kbl + 2) * 256], ALU.add, [BK[sbk], k_maskM], [(k_tmp, ps_)])
                    act(PT[:, ps_], tmpF[:, ps_], AF.Exp, [(k_tmp, ps_)], [(k_PT, ps_)])
                else:
                    act(PT[:, ps_], banks[sbk][:, 0:512], AF.Exp, [BK[sbk]], [(k_PT, ps_)])
            else:
                for ii in range(2):
                    n = n0 + ii
                    stt("dve", tmpF[:, ps_, ii * 256:(ii + 1) * 256], banks[sbk][:, ii * 256:(ii + 1) * 256], negc[:, n, h:h + 1],
                        cb[:, h, jq:jq + 256], ALU.add, ALU.add, [BK[sbk], k_negc, (k_cb, h, j // 2)], [(k_tmp, ps_, ii)])
                tk = [(k_tmp, ps_, 0), (k_tmp, ps_, 1)]
                if ingroup:
                    tt("dve", tmpF[:, ps_], tmpF[:, ps_], maskF[:, kbl * 256:(kbl + 2) * 256], ALU.add, tk + [k_maskF], tk)
                act(PT[:, ps_], tmpF[:, ps_], AF.Exp, tk, [(k_PT, ps_)])

        def t_pv(i, step):
            hh, j, pp, g, last = tasks[i]
            fox = hh >= 8
            sl = hh % 2
            ps_ = i % 3
            ob = OB[g % 2]
            nkb = 4 * (j + 1)
            V_keys = [(k_V, sl, q) for q in range(4)] + [(k_V, sl, "ones")]
            for ii in range(2):
                n = 2 * pp + ii
                for al in range(2):
                    mm(banks[ob[al]][:, 0:129], PT[:, ps_, ii * 256 + al * 128:ii * 256 + (al + 1) * 128], V[:, sl, n, 0:129],
                       n == 0, n == nkb - 1, [(k_PT, ps_)] + V_keys, [BK[ob[al]]])
            if not last:
                return
            for al in range(2):
                a = 2 * j + al
                r_ = (g % 2) * 2 + al
                obk = banks[ob[al]]
                rc = recs[:, a, hh:hh + 1]
                P.op("dve", lambda e, o=rc, i_=obk[:, 128:129]: e.reciprocal(out=o, in_=i_), reads=[BK[ob[al]]], writes=[(k_recs, a, hh)])
                ts("dve", of32[:, r_], obk[:, 0:128], rc, None, ALU.mult, None, [BK[ob[al]], (k_recs, a, hh)], [(k_of32, r_)])
                P.op("dve", lambda e, o_=of32[:, r_], acc=ssq_att[:, a, hh:hh + 1]: e.scalar_tensor_tensor(
                    out=osq, in0=o_, scalar=1.0, in1=o_, op0=ALU.mult, op1=ALU.mult, accum_out=acc),
                    reads=[(k_of32, r_)], writes=[k_osq, (k_ssa, a, hh)])
                tt("dve", obf[:, r_], of32[:, r_], gout[:, hh * 128:(hh + 1) * 128], ALU.mult, [(k_of32, r_), k_gout], [(k_obf, r_)])

                def do_tr(a=a, r_=r_, sl=sl, fox=fox):
                    b7 = bf_bank(7)
                    tsl = tcount[0] % 8
                    tcount[0] += 1
                    tr(b7[:, tsl, :], obf[:, r_], [(k_obf, r_)], [BK[7]])
                    tcopy("act" if fox else "dve", OTh[:, sl, a * 128:(a + 1) * 128], b7[:, tsl, :], [BK[7]], [(k_OTh, sl, a)])
                pending_tr.append((step + 2, do_tr))
            if j == 7:
                def do_store(hh=hh, sl=sl):
                    dma("sp", OTS[hh], OTh[:, sl], [(k_OTh, sl, a) for a in range(NO)], [("OTS", hh)], "OTh%d" % sl)
                    if hh + 2 < 16:
                        head_loads(hh + 2)
                pending_tr.append((step + 2, do_store))

        T = len(tasks)
        for step in range(T + LAG + 3):
            if step < T:
                t_qk(step)
                t_sm(step)
            if 0 <= step - LAG < T:
                t_pv(step - LAG, step)
            while pending_tr and pending_tr[0][0] <= step:
                pending_tr.pop(0)[1]()
        assert not pending_tr
        A.release(p2_mark)
        A.release(attn_mark)
        if STOP == 3:
            return finish()

        k_ssum, ssum = A.alloc("ssum", [NO * 2], F32)
        ssa_keys = [(k_ssa, a, hh) for a in range(NO) for hh in range(16)]
        P.op("dve", lambda e: e.tensor_reduce(out=ssum, in_=ssq_att.rearrange("p a (g h) -> p (a g) h", g=2), axis=AX.X, op=ALU.add),
             reads=ssa_keys, writes=[k_ssum])
        rsqrt_mean(ssum, k_ssum, 1024, rstd_att, k_rsa, NO * 2)
        k_gffn, gffn = A.alloc("gffn", [D], F32)
        k_gfin, gfin = A.alloc("gfin", [D], F32)
        dma("sp", gffn, g_ffn.partition_broadcast(128), [], [k_gffn], "c13")
        dma("sp", gfin, g_fin.partition_broadcast(128), [], [k_gfin], "c14")
        k_x1, x1 = A.alloc("x1", [8, D], F32)
        k_h2T, h2T = A.alloc("h2T", [16, 1024], BF16)
        k_ss2, ss2 = A.alloc("ss2", [8], F32)
        k_rs2, rs2 = A.alloc("rs2", [8], F32)
        k_ss3, ss3 = A.alloc("ss3", [8], F32)
        k_rs3, rs3 = A.alloc("rs3", [8], F32)
        OTS_v = OTS.rearrange("c d t -> d c t")
        w_out_v = w_out.rearrange("(c p) n -> p c n", p=128)
        w_gate_v = w_gate.rearrange("(k p) n -> p k n", p=128)
        w_up_v = w_up.rearrange("(k p) n -> p k n", p=128)
        for tg in range(2):
            mo = A.mark()
            k_OTg, OTg = A.alloc("OTg", [16, 1024], BF16)
            k_wo, wo = A.alloc("wo", [2, 16, 512], BF16)
            k_hn2, hn2 = A.alloc("hn2", [2, D], BF16)
            k_sq4, sq4 = A.alloc("sq4", [D], BF16)
            for i in range(4):
                dma("sp", OTg[:, i * 4:(i + 1) * 4, :], OTS_v[:, i * 4:(i + 1) * 4, tg * 1024:(tg + 1) * 1024],
                    [("OTS", c) for c in range(i * 4, (i + 1) * 4)], [(k_OTg, i)], "OTg%d" % i)
            for i in range(8):
                a = tg * 8 + i
                dma("sp", x1[:, i, :], x_own[a * 128:(a + 1) * 128, :], [], [(k_x1, i, ct) for ct in range(4)], "xo%d" % i)
            uc = 0
            for ct in range(4):
                ws = ct % 2
                for q in range(4):
                    dma("pool", wo[:, ws, q * 4:(q + 1) * 4, :], w_out_v[:, q * 4:(q + 1) * 4, ct * 512:(ct + 1) * 512], [], [(k_wo, ws, q)], "wo%d_%d" % (ws, q))
                for i in range(8):
                    a = tg * 8 + i
                    bm, bf_ = ((0, 1), (2, 3), (4, 5))[uc % 3]
                    uc += 1
                    for c in range(8):
                        mm(banks[bm][:, 0:512], OTg[:, c, i * 128:(i + 1) * 128], wo[:, ws, c, :], c == 0, c == 7, [(k_OTg, c // 4), (k_wo, ws, c // 4)], [BK[bm]])
                    for c in range(8, 16):
                        mm(banks[bf_][:, 0:512], OTg[:, c, i * 128:(i + 1) * 128], wo[:, ws, c, :], c == 8, c == 15, [(k_OTg, c // 4), (k_wo, ws, c // 4)], [BK[bf_]])
                    xs = x1[:, i, ct * 512:(ct + 1) * 512]
                    stt("dve", xs, banks[bm][:, 0:512], rstd_att[:, 2 * a:2 * a + 1], xs, ALU.mult, ALU.add, [BK[bm], k_rsa, (k_x1, i, ct)], [(k_x1, i, ct)])
                    stt("dve", xs, banks[bf_][:, 0:512], rstd_att[:, 2 * a + 1:2 * a + 2], xs, ALU.mult, ALU.add, [BK[bf_], k_rsa, (k_x1, i, ct)], [(k_x1, i, ct)])
            for i in range(8):
                act(sq4, x1[:, i, :], AF.Square, [(k_x1, i, ct) for ct in range(4)], [k_sq4, (k_ss2, i)], accum=ss2[:, i:i + 1])
            k_l2, lnv2 = A.alloc("lnv2", [8], F32)
            act(lnv2, ss2, AF.Ln, [(k_ss2, i) for i in range(8)] + [k_epsb], [k_l2], bias=epsb, scale=1.0 / D)
            act(rs2, lnv2, AF.Exp, [k_l2], [k_rs2], scale=-0.5)
            for i in range(8):
                s = i % 2
                stt("dve", hn2[:, s], x1[:, i, :], rs2[:, i:i + 1], gffn, ALU.mult, ALU.mult, [(k_x1, i, ct) for ct in range(4)] + [k_rs2, k_gffn], [(k_hn2, s)])
                for half in range(2):
                    tb = 6 + half
                    bv = bf_bank(tb)
                    for kq in range(8):
                        k = half * 8 + kq
                        tr(bv[:, kq, :], hn2[:, s, k * 128:(k + 1) * 128], [(k_hn2, s)], [BK[tb]])
                    tcopy("act" if half == 0 else "dve", h2T[:, half * 8:(half + 1) * 8, i * 128:(i + 1) * 128], bv, [BK[tb]], [(k_h2T, i, half)])
            A.release(mo)
            mf = A.mark()
            k_wg, wg = A.alloc("wg", [3, 16, 128], BF16)
            k_wu, wu = A.alloc("wu", [3, 16, 128], BF16)
            k_wd, wd = A.alloc("wd", [8, D], BF16)
            k_aT, aT = A.alloc("aT", [2, 4, 1024], BF16)
            k_sg, sgt = A.alloc("sgt", [2, 512], F32)
            h2keys = [[(k_h2T, i, hf) for i in range(tt_ * 4, tt_ * 4 + 4) for hf in range(2)] for tt_ in range(2)]
            ucnt = [0]
            dcnt = [0]

            def ffn_units(g):
                for cg in range(4):
                    c = g * 4 + cg
                    w3 = c % 3
                    dma("pool", wg[:, w3], w_gate_v[:, :, c * 128:(c + 1) * 128], [], [(k_wg, w3)], "wg%d" % w3)
                    dma("pool", wu[:, w3], w_up_v[:, :, c * 128:(c + 1) * 128], [], [(k_wu, w3)], "wu%d" % w3)
                    dma("pool", wd[:, c % 8, :], w_down[c * 128:(c + 1) * 128, :], [], [(k_wd, c % 8)], "wd%d" % (c % 8))
                    for tt_ in range(2):
                        bg, bu = ((0, 1), (2, 3), (4, 5))[ucnt[0] % 3]
                        ss = ucnt[0] % 2
                        ucnt[0] += 1
                        for k in range(16):
                            mm(banks[bg][:, 0:512], wg[:, w3, k, :], h2T[:, k, tt_ * 512:(tt_ + 1) * 512], k == 0, k == 15, [(k_wg, w3)] + h2keys[tt_], [BK[bg]])
                        for k in range(16):
                            mm(banks[bu][:, 0:512], wu[:, w3, k, :], h2T[:, k, tt_ * 512:(tt_ + 1) * 512], k == 0, k == 15, [(k_wu, w3)] + h2keys[tt_], [BK[bu]])
                        act(sgt[:, ss], banks[bg][:, 0:512], AF.Silu, [BK[bg]], [(k_sg, ss)])
                        tt("dve", aT[:, g % 2, cg, tt_ * 512:(tt_ + 1) * 512], sgt[:, ss], banks[bu][:, 0:512], ALU.mult, [(k_sg, ss), BK[bu]], [(k_aT, g % 2, cg, tt_)])

            def ffn_down(g):
                for i in range(8):
                    for ct in range(4):
                        bk = 6 + dcnt[0] % 2
                        dcnt[0] += 1
                        for cg in range(4):
                            c = g * 4 + cg
                            mm(banks[bk][:, 0:512], aT[:, g % 2, cg, i * 128:(i + 1) * 128], wd[:, c % 8, ct * 512:(ct + 1) * 512], cg == 0, cg == 3,
                               [(k_aT, g % 2, cg, i // 4), (k_wd, c % 8)], [BK[bk]])
                        xs = x1[:, i, ct * 512:(ct + 1) * 512]
                        tt("dve", xs, banks[bk][:, 0:512], xs, ALU.add, [BK[bk], (k_x1, i, ct)], [(k_x1, i, ct)])

            NG = NCH // 4
            for g in range(NG + 1):
                if g < NG:
                    ffn_units(g)
                if g >= 1:
                    ffn_down(g - 1)
            k_sq5, sq5 = A.alloc("sq5", [D], BF16)
            for i in range(8):
                act(sq5, x1[:, i, :], AF.Square, [(k_x1, i, ct) for ct in range(4)], [k_sq5, (k_ss3, i)], accum=ss3[:, i:i + 1])
            k_l3, lnv3 = A.alloc("lnv3", [8], F32)
            act(lnv3, ss3, AF.Ln, [(k_ss3, i) for i in range(8)] + [k_epsb], [k_l3], bias=epsb, scale=1.0 / D)
            act(rs3, lnv3, AF.Exp, [k_l3], [k_rs3], scale=-0.5)
            for i in range(8):
                a = tg * 8 + i
                xk = [(k_x1, i, ct) for ct in range(4)]
                stt("dve", x1[:, i, :], x1[:, i, :], rs3[:, i:i + 1], gfin, ALU.mult, ALU.mult, xk + [k_rs3, k_gfin], xk)
                dma("sp", out_d[a * 128:(a + 1) * 128, :], x1[:, i, :], xk, [("out", a)], "xo%d" % i)
            A.release(mf)
        return finish()


def own_blocks(par):
    r = []
    for j in range(8):
        r += [4 * j, 4 * j + 3] if par == 0 else [4 * j + 1, 4 * j + 2]
    return r


def _consts(par):
    ob = own_blocks(par)
    p = np.arange(128)
    maskM = np.zeros((128, 4, 2, 128), np.float32)
    maskF = np.zeros((128, 4, 2, 128), np.float32)
    for kbl in range(4):
        for al in range(2):
            n = kbl
            ia = ob[al]
            s_idx = n * 128 + p[:, None]
            t_idx = ia * 128 + p[None, :]
            maskM[:, kbl, al, :] = np.where((s_idx // 64) <= (t_idx // 64), 0.0, NEG)
            maskF[:, kbl, al, :] = np.where(s_idx <= t_idx, 0.0, NEG)
    sel = np.zeros((128, NO, NB), np.float32)
    for a, ia in enumerate(ob):
        sel[:, a, ia] = 1.0
    invf = (10000.0 ** (-np.arange(0, 64, 2, dtype=np.float32) / 64)).astype(np.float32)
    return {
        "c_ident": np.eye(128, dtype=np.float32),
        "c_tri": np.triu(np.ones((128, 128), np.float32)),
        "c_ones": np.ones((128, 128), np.float32),
        "c_invf": np.ascontiguousarray(np.broadcast_to(invf, (128, 32))),
        "c_maskM": maskM.reshape(128, 1024),
        "c_maskF": maskF.reshape(128, 1024),
        "c_sel": sel.reshape(128, NO * NB),
    }


_NC_CACHE = {}


def kernel(x, positions, g_attn_norm, w_in, b_forget, g_q_lat, w_uq, g_kv_lat, w_ukv, g_out_mla, g_out_fox,
           w_out, g_ffn_norm, w_gate, w_up, w_down, g_final_norm):
    f = lambda a: np.ascontiguousarray(np.asarray(a, dtype=np.float32))
    x = f(x)
    positions = np.asarray(positions).astype(np.int32)
    shared = {
        "w_in": f(w_in)[0], "w_uq": f(w_uq)[0], "w_ukv": f(w_ukv)[0], "w_out": f(w_out)[0],
        "w_gate": f(w_gate)[0], "w_up": f(w_up)[0], "w_down": f(w_down)[0],
        "g_attn": f(g_attn_norm)[0], "g_q": f(g_q_lat)[0], "g_kv": f(g_kv_lat)[0],
        "g_out": np.ascontiguousarray(np.concatenate([f(g_out_mla)[0], f(g_out_fox)[0]])),
        "g_ffn": f(g_ffn_norm)[0], "g_fin": f(g_final_norm), "b_fg": f(b_forget)[0],
    }
    consts = [_consts(0), _consts(1)]
    in_maps = []
    for c in range(8):
        b, par = c // 2, c % 2
        ob = own_blocks(par)
        xb = x[b].reshape(NB, 128, D)
        pb = positions[b].reshape(NB, 128)
        m = dict(shared)
        m.update(consts[par])
        m["x_nat"] = x[b]
        m["x_own"] = np.ascontiguousarray(xb[ob].reshape(NO * 128, D))
        m["pos_nat"] = np.ascontiguousarray(pb.T)
        m["pos_own"] = np.ascontiguousarray(pb[ob].T)
        in_maps.append(m)
    if STOP < 4:
        for m in in_maps:
            for k in ("w_out", "w_gate", "w_up", "w_down"):
                m[k] = np.ascontiguousarray(m[k][0:128])
            if STOP < 2:
                m["x_nat"] = np.ascontiguousarray(m["x_nat"][0:128])
    if "nc" not in _NC_CACHE:
        _NC_CACHE["nc"] = build_program()
    res = run_bass_kernel_spmd(_NC_CACHE["nc"], in_maps[:NCORES], core_ids=list(range(NCORES)))
    kernel.last_results = res
    out = np.zeros((4, S, D), np.float32) if NCORES < 8 else np.empty((4, S, D), np.float32)
    for c in range(NCORES):
        b, par = c // 2, c % 2
        ob = own_blocks(par)
        o = res.results[c]["out"].reshape(NO, 128, D)
        out[b].reshape(NB, 128, D)[ob] = o
    return out
```

```python
import contextlib
import math
import numpy as np
import concourse.bass as bass
import concourse.mybir as mybir
from concourse.bass_utils import run_bass_kernel_spmd

F32 = mybir.dt.float32
BF16 = mybir.dt.bfloat16
I32 = mybir.dt.int32
U8 = mybir.dt.uint8
AF = mybir.ActivationFunctionType
ALU = mybir.AluOpType
AX = mybir.AxisListType

D = 2048
S = 4096
NB = 32
NO = 16
FF = 5632
NCH = FF // 128
EPS = 1e-6
QSCALE = 1.0 / math.sqrt(192.0)
FSCALE = 1.0 / math.sqrt(128.0)
PI = float(np.pi)
NEG = -1.0e30
ARENA_BYTES = 211968

DEBUG = False
STOP = 99
NCORES = 8
SKIP = set()
NT_LIMIT = 8
LATSTEP = 99
ENGS = ("pe", "act", "dve", "pool", "sp")


class Op:
    __slots__ = ("eng", "fn", "deps", "needed", "count", "dma_sem")

    def __init__(self, eng, fn, dma_sem):
        self.eng = eng
        self.fn = fn
        self.deps = []
        self.needed = False
        self.count = None
        self.dma_sem = dma_sem


def _base(k):
    return k[0] if isinstance(k, tuple) else k


class Prog:
    def __init__(self, nc):
        self.nc = nc
        self.ops = {e: [] for e in ENGS}
        self.state = {}
        self.inherit = {}
        self.touch = {}
        self.last_dma = {}

    def op(self, eng, fn, reads=(), writes=(), dma_sem=None):
        o = Op(eng, fn, dma_sem)
        deps = []
        for b in list(reads) + list(writes):
            if b not in self.state:
                self.state[b] = [list(self.inherit.get(_base(b), [])), []]
        for b in reads:
            deps.extend(self.state[b][0])
            if _base(b) == "ps":
                deps.extend(r for r in self.state[b][1] if r.eng != eng)
        for b in writes:
            st = self.state[b]
            deps.extend(st[0])
            deps.extend(st[1])
        seen = set()
        for d in deps:
            if id(d) in seen or d is o:
                continue
            seen.add(id(d))
            if d.eng == "pe" and eng == "pe" and d.dma_sem is None and dma_sem is None:
                continue
            if d.dma_sem is not None:
                d = self.last_dma[d.dma_sem]
            d.needed = True
            o.deps.append(d)
        for b in reads:
            rl = self.state[b][1]
            if dma_sem is None:
                for i_ in range(len(rl)):
                    if rl[i_].eng == eng and rl[i_].dma_sem is None:
                        del rl[i_]
                        break
            rl.append(o)
        for b in writes:
            self.state[b] = [[o], []]
        tag = ("d", dma_sem) if dma_sem is not None else ("e", eng)
        for b in list(reads) + list(writes):
            self.touch.setdefault(_base(b), {})[tag] = o
        if dma_sem is not None:
            self.last_dma[dma_sem] = o
        self.ops[eng].append(o)
        return o

    def emit(self):
        nc = self.nc
        sem_names = set()
        for e in ENGS:
            for o in self.ops[e]:
                if o.dma_sem is not None:
                    sem_names.add("d_" + o.dma_sem)
        sem_names = sorted(sem_names)
        all_names = ["e_" + e for e in ENGS] + sem_names
        with contextlib.ExitStack() as st:
            sems = {n: st.enter_context(nc.semaphore(n)) for n in all_names}
            cnt = {n: 0 for n in all_names}
            for e in ENGS:
                for o in self.ops[e]:
                    if o.dma_sem is not None:
                        n = "d_" + o.dma_sem
                        cnt[n] += 16
                        o.count = (n, cnt[n])
                    elif o.needed:
                        n = "e_" + e
                        cnt[n] += 1
                        o.count = (n, cnt[n])
            blk = st.enter_context(nc.Block())

            def run(engobj, e):
                waited = {}
                for o in self.ops[e]:
                    need = {}
                    for d in o.deps:
                        n, v = d.count
                        if v > need.get(n, 0):
                            need[n] = v
                    for n, v in need.items():
                        if waited.get(n, 0) >= v:
                            continue
                        waited[n] = v
                        engobj.wait_ge(sems[n], v)
                    ins = o.fn(engobj)
                    if o.dma_sem is not None:
                        ins.then_inc(sems[o.count[0]], 16)
                    elif o.needed:
                        ins.then_inc(sems[o.count[0]], 1)
                if e == "sp":
                    for n in sem_names:
                        if cnt[n] > 0:
                            engobj.wait_ge(sems[n], cnt[n])

            blk.tensor(lambda t: run(t, "pe"))
            blk.scalar(lambda t: run(t, "act"))
            blk.vector(lambda t: run(t, "dve"))
            blk.gpsimd(lambda t: run(t, "pool"))
            blk.sync(lambda t: run(t, "sp"))


_DTSZ = {F32: 4, BF16: 2, I32: 4, U8: 1}


class Arena:
    def __init__(self, P, arena_ap):
        self.P = P
        self.a = arena_ap
        self.top = 0
        self.live = []
        self.dead = []
        self.uid = 0
        self.peak = 0

    def alloc(self, name, shape, dt, parts=128):
        self.uid += 1
        name = "%s#%d" % (name, self.uid)
        n = 1
        for s in shape:
            n *= s
        size = (n * _DTSZ[dt] + 63) // 64 * 64
        off = self.top
        self.top += size
        self.peak = max(self.peak, self.top)
        assert self.top <= ARENA_BYTES, ("SBUF arena overflow", name, self.top)
        inh = []
        for (nm, o, s) in self.dead:
            if o < off + size and off < o + s:
                inh.extend(self.P.touch.get(nm, {}).values())
        self.P.inherit[name] = inh
        self.live.append((name, off, size))
        v = self.a[0:parts, off:off + n * _DTSZ[dt]].bitcast(dt)
        if len(shape) == 2:
            v = v.rearrange("p (a b) -> p a b", a=shape[0])
        elif len(shape) == 3:
            v = v.rearrange("p (a b c) -> p a b c", a=shape[0], b=shape[1])
        return name, v

    def mark(self):
        return (self.top, len(self.live))

    def release(self, mark):
        top, nl = mark
        self.dead.extend(self.live[nl:])
        del self.live[nl:]
        self.top = top


def build_program():
    nc = bass.Bass("TRN2", target_bir_lowering=False)

    def din(name, shape, dt=F32):
        return nc.dram_tensor(name, list(shape), dt, kind="ExternalInput").ap()

    big2 = STOP >= 2
    big4 = STOP >= 4
    x_nat = din("x_nat", [S, D] if big2 else [128, D])
    x_own = din("x_own", [S // 2, D])
    posN_d = din("pos_nat", [128, NB], I32)
    posO_d = din("pos_own", [128, NO], I32)
    w_in = din("w_in", [D, 3912])
    w_uq = din("w_uq", [512, 1536])
    w_ukv = din("w_ukv", [256, 2048])
    w_out = din("w_out", [D, D] if big4 else [128, D])
    w_gate = din("w_gate", [D, FF] if big4 else [128, FF])
    w_up = din("w_up", [D, FF] if big4 else [128, FF])
    w_down = din("w_down", [FF, D] if big4 else [128, D])
    g_attn = din("g_attn", [D])
    g_q = din("g_q", [512])
    g_kv = din("g_kv", [256])
    g_out = din("g_out", [D])
    g_ffn = din("g_ffn", [D])
    g_fin = din("g_fin", [D])
    b_fg = din("b_fg", [8])
    ident_d = din("c_ident", [128, 128])
    tri_d = din("c_tri", [128, 128])
    ones_d = din("c_ones", [128, 128])
    invf_d = din("c_invf", [128, 32])
    maskM_d = din("c_maskM", [128, 1024])
    maskF_d = din("c_maskF", [128, 1024])
    sel_d = din("c_sel", [128, NO * NB])
    out_d = nc.dram_tensor("out", [S // 2, D], F32, kind="ExternalOutput").ap()

    skind = "ExternalOutput" if DEBUG else "Internal"

    def scratch(name, shape):
        return nc.dram_tensor(name, list(shape), BF16, kind=skind).ap()

    KTM = scratch("s_ktm", [8, 128, S])
    KTF = scratch("s_ktf", [8, 128, S])
    VM = scratch("s_vm", [S, 1024])
    VF = scratch("s_vf", [S, 1024])
    QTN = scratch("s_qtn", [8, 128, S // 2])
    QTP = scratch("s_qtp", [8, 64, S // 2])
    QTF = scratch("s_qtf", [8, 128, S // 2])
    OTS = scratch("s_ots", [16, 128, S // 2])

    with contextlib.ExitStack() as es:
        arena_t = es.enter_context(nc.sbuf_tensor("arena", [128, ARENA_BYTES], U8))
        banks = [es.enter_context(nc.psum_tensor("bank%d" % i, [128, 512], F32)) for i in range(8)]
        P = Prog(nc)
        A = Arena(P, arena_t[:])
        BK = [("ps", i) for i in range(8)]

        def bf_bank(i):
            return banks[i][:].bitcast(BF16).rearrange("p (a b) -> p a b", a=8)

        def dma(q, out, in_, reads, writes, sem):
            return P.op(q, lambda e: e.dma_start(out=out, in_=in_), reads=reads, writes=writes, dma_sem=sem)

        def mm(out, lhsT, rhs, start, stop, reads, writes):
            return P.op("pe", lambda e: e.matmul(out, lhsT=lhsT, rhs=rhs, start=start, stop=stop), reads=reads, writes=writes)

        def tr(out, in_, reads, writes):
            return P.op("pe", lambda e: e.transpose(out=out, in_=in_, identity=identb), reads=list(reads) + [k_identb], writes=writes)

        def act(out, in_, func, reads, writes, bias=None, scale=None, accum=None):
            kw = {}
            if bias is not None:
                kw["bias"] = bias
            if scale is not None:
                kw["scale"] = scale
            if accum is not None:
                kw["accum_out"] = accum
            return P.op("act", lambda e: e.activation(out=out, in_=in_, func=func, **kw), reads=reads, writes=writes)

        def amul(out, in_, c, reads, writes):
            return P.op("act", lambda e: e.mul(out=out, in_=in_, mul=c), reads=reads, writes=writes)

        def tcopy(eng, out, in_, reads, writes):
            if eng == "act":
                return P.op("act", lambda e: e.copy(out=out, in_=in_), reads=reads, writes=writes)
            return P.op(eng, lambda e: e.tensor_copy(out=out, in_=in_), reads=reads, writes=writes)

        def tt(eng, out, in0, in1, op, reads, writes):
            return P.op(eng, lambda e: e.tensor_tensor(out=out, in0=in0, in1=in1, op=op), reads=reads, writes=writes)

        def ts(eng, out, in0, s1, s2, op0, op1, reads, writes):
            if s2 is None:
                return P.op(eng, lambda e: e.tensor_scalar(out=out, in0=in0, scalar1=s1, scalar2=None, op0=op0), reads=reads, writes=writes)
            return P.op(eng, lambda e: e.tensor_scalar(out=out, in0=in0, scalar1=s1, scalar2=s2, op0=op0, op1=op1), reads=reads, writes=writes)

        def stt(eng, out, in0, scalar, in1, op0, op1, reads, writes):
            return P.op(eng, lambda e: e.scalar_tensor_tensor(out=out, in0=in0, scalar=scalar, in1=in1, op0=op0, op1=op1), reads=reads, writes=writes)

        def memset(eng, ap, val, writes):
            return P.op(eng, lambda e: e.memset(ap, val), writes=writes)

        def finish():
            P.emit()
            build_program.peak = A.peak
            return nc

        k_identf, identf = A.alloc("identf", [128], F32)
        k_identb, identb = A.alloc("identb", [128], BF16)
        k_epsb, epsb = A.alloc("epsb", [1], F32)
        k_oneb, oneb = A.alloc("oneb", [1], F32)
        k_hpib, hpib = A.alloc("hpib", [1], F32)
        k_rsa, rstd_att = A.alloc("rstd_att", [NO * 2], F32)
        k_ssa, ssq_att = A.alloc("ssq_att", [NO, 16], F32)
        k_recs, recs = A.alloc("recs", [NO, 16], F32)
        dma("sp", identf, ident_d, [], [k_identf], "c")
        memset("dve", epsb, EPS, [k_epsb])
        memset("dve", oneb, 1.0, [k_oneb])
        memset("dve", hpib, PI / 2, [k_hpib])
        cnt_sm = [0]

        k_lnv, lnv_all = A.alloc("lnv", [64], F32)

        def rsqrt_mean(ssq_ap, k_ssq, n_feat, out_ap, k_out, width):
            o = cnt_sm[0] % 2
            cnt_sm[0] += 1
            lnv = lnv_all[:, o * 32:o * 32 + width]
            act(lnv, ssq_ap, AF.Ln, [k_ssq, k_epsb], [(k_lnv, o)], bias=epsb, scale=1.0 / n_feat)
            act(out_ap, lnv, AF.Exp, [(k_lnv, o)], [k_out], scale=-0.5)

        attn_mark = A.mark()
        k_trif, trif = A.alloc("trif", [128], F32)
        k_onesf, onesf = A.alloc("onesf", [128], F32)
        k_maskM, maskM = A.alloc("maskM", [1024], F32)
        k_maskF, maskF = A.alloc("maskF", [1024], F32)
        k_sel, sel = A.alloc("sel", [NO, NB], F32)
        k_cosN, cosN = A.alloc("cosN", [NB, 32], F32)
        k_sinN, sinN = A.alloc("sinN", [NB, 32], F32)
        k_cosO, cosO = A.alloc("cosO", [NO, 32], F32)
        k_sinO, sinO = A.alloc("sinO", [NO, 32], F32)
        k_flog, flog = A.alloc("flog", [NB, 8], F32)
        k_negc, negc = A.alloc("negc", [NB, 8], F32)
        k_cown, cown = A.alloc("cown", [NO, 8], F32)
        k_bfg, bfg = A.alloc("bfg", [8], F32)
        k_kpeT, kpeT = A.alloc("kpeT", [S], BF16)
        memset("dve", kpeT[64:128, :], 0.0, [(k_kpeT, "z")])
        dma("sp", trif, tri_d, [], [k_trif], "c")
        dma("sp", onesf, ones_d, [], [k_onesf], "c")
        dma("sp", maskM, maskM_d, [], [k_maskM], "c")
        dma("sp", maskF, maskF_d, [], [k_maskF], "c")
        dma("sp", sel, sel_d.rearrange("p (a n) -> p a n", a=NO), [], [k_sel], "c")
        dma("sp", bfg, b_fg.partition_broadcast(128), [], [k_bfg], "c")

        k_piN, pos_iN = A.alloc("pos_iN", [NB], I32)
        k_piO, pos_iO = A.alloc("pos_iO", [NO], I32)
        k_if, invf = A.alloc("invf", [32], F32)
        dma("sp", pos_iN, posN_d, [], [k_piN], "c")
        dma("sp", pos_iO, posO_d, [], [k_piO], "c")
        dma("sp", invf, invf_d, [], [k_if], "c")
        k_join, joinb = A.alloc("joinb", [1], F32)
        memset("dve", joinb, 0.0, [k_join, k_identf, k_trif, k_onesf, k_maskM, k_maskF, k_sel, k_bfg, k_piN, k_piO, k_if])
        tcopy("dve", identb, identf, [k_identf], [k_identb])

        def make_tables(pos_i, k_pi, nb, cos_t, k_cos, sin_t, k_sin, scale, tag):
            m = A.mark()
            k_pf, pos_f = A.alloc("pos_f", [nb], F32)
            k_ang, ang = A.alloc("ang", [nb, 32], F32)
            k_t1, t1 = A.alloc("t1", [nb, 32], F32)
            k_kk, kk = A.alloc("kk", [nb, 32], F32)
            k_rr, rr = A.alloc("rr", [nb, 32], F32)
            k_ra, ra = A.alloc("ra", [nb, 32], F32)
            k_bb, bb = A.alloc("bb", [nb, 32], F32)
            k_sg, sg = A.alloc("sg", [nb, 32], F32)
            tcopy("dve", pos_f, pos_i, [k_pi], [k_pf])
            tt("dve", ang, pos_f.unsqueeze(2).broadcast_to([128, nb, 32]), invf.unsqueeze(1).broadcast_to([128, nb, 32]), ALU.mult, [k_pf, k_if], [k_ang])
            MAGIC = 12582912.0
            C1 = 6.28125
            C2 = 2 * PI - 6.28125
            ts("dve", t1, ang, 1.0 / (2 * PI), MAGIC, ALU.mult, ALU.add, [k_ang], [k_t1])
            ts("dve", kk, t1, -MAGIC, None, ALU.add, None, [k_t1], [k_kk])
            stt("dve", t1, kk, -C1, ang, ALU.mult, ALU.add, [k_kk, k_ang], [k_t1])
            stt("dve", rr, kk, -C2, t1, ALU.mult, ALU.add, [k_kk, k_t1], [k_rr])
            ts("dve", rr, rr, -3.1415925, 3.1415925, ALU.max, ALU.min, [k_rr], [k_rr])
            stt("dve", ra, rr, -1.0, rr, ALU.mult, ALU.max, [k_rr], [k_ra])
            act(cos_t, ra, AF.Sin, [k_ra, k_hpib], [k_cos], bias=hpib, scale=-1.0)
            ts("dve", t1, ra, -PI / 2, None, ALU.add, None, [k_ra], [k_t1])
            stt("dve", bb, t1, -1.0, t1, ALU.mult, ALU.max, [k_t1], [k_bb])
            act(sin_t, bb, AF.Sin, [k_bb, k_hpib], [k_sin], bias=hpib, scale=-1.0)
            act(sg, rr, AF.Sign, [k_rr], [k_sg])
            tt("dve", sin_t, sin_t, sg, ALU.mult, [k_sin, k_sg], [k_sin])
            if scale != 1.0:
                ts("dve", cos_t, cos_t, scale, None, ALU.mult, None, [k_cos], [k_cos])
                ts("dve", sin_t, sin_t, scale, None, ALU.mult, None, [k_sin], [k_sin])
            A.release(m)

        make_tables(pos_iN, k_piN, NB, cosN, k_cosN, sinN, k_sinN, 1.0, "n")
        make_tables(pos_iO, k_piO, NO, cosO, k_cosO, sinO, k_sinO, QSCALE, "o")

        if STOP == 0:
            return finish()

        def wload(dst, src, reads_none, k_dst, sem, pieces, axis_len):
            step = axis_len // pieces
            for i in range(pieces):
                dma("pool", dst[:, i * step:(i + 1) * step], src[:, i * step:(i + 1) * step], [], [(k_dst, i)], sem)

        def wkeys(k, pieces):
            return [(k, i) for i in range(pieces)]

        def norm_part(x_rows, bufs, s, g_b, k_g):
            (k_xt, xt), (k_hn, hn), (k_sq, sqj), (k_ss, ssq), (k_rs, rstd) = bufs
            dma("sp", xt[:, s], x_rows, [], [(k_xt, s)], "xt%d" % s)
            act(sqj, xt[:, s], AF.Square, [(k_xt, s)], [k_sq, (k_ss, s)], accum=ssq[:, s:s + 1])
            rsqrt_mean(ssq[:, s:s + 1], (k_ss, s), D, rstd[:, s:s + 1], (k_rs, s), 1)
            stt("dve", hn[:, s], xt[:, s], rstd[:, s:s + 1], g_b, ALU.mult, ALU.mult, [(k_xt, s), (k_rs, s), k_g], [(k_hn, s)])

        def transpose_part(bufs, s, hT, k_hT, col0, tb):
            (k_xt, xt), (k_hn, hn), (k_sq, sqj), (k_ss, ssq), (k_rs, rstd) = bufs
            for half in range(2):
                bv = bf_bank(tb[half])
                for kq in range(8):
                    k = half * 8 + kq
                    tr(bv[:, kq, :], hn[:, s, k * 128:(k + 1) * 128], [(k_hn, s)], [BK[tb[half]]])
                tcopy("act" if half == 0 else "dve", hT[:, half * 8:(half + 1) * 8, col0:col0 + 128], bv, [BK[tb[half]]], [(k_hT, col0 // 128, half)])

        def hT_keys(k_hT, blks):
            return [(k_hT, b, h) for b in blks for h in range(2)]

        p1_mark = A.mark()
        k_gattn, gattn = A.alloc("gattn", [D], F32)
        dma("sp", gattn, g_attn.partition_broadcast(128), [], [k_gattn], "c9")
        nb_bufs = (A.alloc("xt", [2, D], F32), A.alloc("hn", [2, D], BF16), A.alloc("sqj", [D], BF16),
                   A.alloc("ssq", [2], F32), A.alloc("rstd", [2], F32))
        k_hT, hT = A.alloc("hT", [16, 512], BF16)
        pO_mark = A.mark()
        k_gq, gq = A.alloc("gq", [512], F32)
        dma("sp", gq, g_q.partition_broadcast(128), [], [k_gq], "c10")
        k_wq, wq = A.alloc("wq", [16, 512], BF16)
        k_wfq, wfq = A.alloc("wfq", [16, 1024], BF16)
        k_wuqn, wuqn = A.alloc("wuqn", [4, 1024], BF16)
        k_wuqp, wuqp = A.alloc("wuqp", [4, 512], BF16)
        w_in_v = w_in.rearrange("(k p) n -> p k n", p=128)
        wload(wq, w_in_v[:, :, 0:512], None, k_wq, "wq", 4, 16)
        w_uq_v = w_uq.rearrange("(k p) (h d) -> p k h d", p=128, d=192)
        for k in range(4):
            dma("pool", wuqn[:, k, :].rearrange("p (h d) -> p h d", d=128), w_uq_v[:, k, :, 0:128], [], [(k_wuqn, k)], "wuqn")
            dma("pool", wuqp[:, k, :].rearrange("p (h d) -> p h d", d=64), w_uq_v[:, k, :, 128:192], [], [(k_wuqp, k)], "wuqp")
        wload(wfq, w_in_v[:, :, 832:1856], None, k_wfq, "wfq", 8, 16)
        k_ssq_q, ssq_q = A.alloc("ssq_q", [1], F32)
        k_rs_q, rs_q = A.alloc("rs_q", [1], F32)
        k_sq2, sq2 = A.alloc("sq2", [512], BF16)
        k_qn, qn = A.alloc("qn", [512], BF16)
        k_qnT, qnT = A.alloc("qnT", [4, 128], BF16)
        k_qnb, qnb = A.alloc("qnb", [1024], BF16)
        k_qpb, qpb = A.alloc("qpb", [8, 64], BF16)
        rt = [A.alloc("rt%d" % i, [8, 32], F32) for i in range(4)]
        k_qtn, qtn_st = A.alloc("qtn_st", [2, 8, 128], BF16)
        k_qtp, qtp_st = A.alloc("qtp_st", [2, 8, 128], BF16)
        k_qf, qf_st = A.alloc("qf_st", [2, 512], BF16)

        QTN_v = QTN.rearrange("h d t -> d h t")
        QTP_v = QTP.rearrange("h d t -> d h t")
        norm_part(x_own[0:128, :], nb_bufs, 0, gattn, k_gattn)
        for ot in range(4):
            for blk in range(4):
                a = ot * 4 + blk
                s = a % 2
                if a + 1 < NO:
                    norm_part(x_own[(a + 1) * 128:(a + 2) * 128, :], nb_bufs, (a + 1) % 2, gattn, k_gattn)
                transpose_part(nb_bufs, s, hT, k_hT, blk * 128, (0, 1))
                hk = hT_keys(k_hT, [blk])
                for k in range(16):
                    mm(banks[2][:, 0:512], hT[:, k, blk * 128:(blk + 1) * 128], wq[:, k, :], k == 0, k == 15, hk + wkeys(k_wq, 4), [BK[2]])
                act(sq2, banks[2][:, 0:512], AF.Square, [BK[2]], [k_sq2, k_ssq_q], accum=ssq_q)
                rsqrt_mean(ssq_q, k_ssq_q, 512, rs_q, k_rs_q, 1)
                stt("dve", qn, banks[2][:, 0:512], rs_q, gq, ALU.mult, ALU.mult, [BK[2], k_rs_q, k_gq], [k_qn])
                b3 = bf_bank(3)
                for k in range(4):
                    tr(b3[:, k, :], qn[:, k * 128:(k + 1) * 128], [k_qn], [BK[3]])
                tcopy("dve", qnT, b3[:, 0:4, :], [BK[3]], [k_qnT])
                for (bk, wsrc, c0, kw) in ((4, wuqn, 0, wkeys(k_wuqn, 4)), (5, wuqn, 512, wkeys(k_wuqn, 4)), (6, wuqp, 0, wkeys(k_wuqp, 4))):
                    for k in range(4):
                        mm(banks[bk][:, 0:512], qnT[:, k, :], wsrc[:, k, c0:c0 + 512], k == 0, k == 3, [k_qnT] + kw, [BK[bk]])
                amul(qnb[:, 0:512], banks[4][:, 0:512], QSCALE, [BK[4]], [(k_qnb, 0)])
                amul(qnb[:, 512:1024], banks[5][:, 0:512], QSCALE, [BK[5]], [(k_qnb, 1)])
                b6 = banks[6][:, 0:512].rearrange("p (h d) -> p h d", d=64)
                x1v, x2v = b6[:, :, 0:32], b6[:, :, 32:64]
                cb_ = cosO[:, a, :].unsqueeze(1).broadcast_to([128, 8, 32])
                sb_ = sinO[:, a, :].unsqueeze(1).broadcast_to([128, 8, 32])
                tt("dve", rt[0][1], x1v, cb_, ALU.mult, [BK[6], k_cosO], [rt[0][0]])
                tt("dve", rt[1][1], x2v, sb_, ALU.mult, [BK[6], k_sinO], [rt[1][0]])
                tt("dve", rt[2][1], x2v, cb_, ALU.mult, [BK[6], k_cosO], [rt[2][0]])
                tt("dve", rt[3][1], x1v, sb_, ALU.mult, [BK[6], k_sinO], [rt[3][0]])
                tt("dve", qpb[:, :, 0:32], rt[0][1], rt[1][1], ALU.subtract, [rt[0][0], rt[1][0]], [(k_qpb, 0)])
                tt("dve", qpb[:, :, 32:64], rt[2][1], rt[3][1], ALU.add, [rt[2][0], rt[3][0]], [(k_qpb, 1)])
                b7 = bf_bank(7)
                for h in range(8):
                    tr(b7[:, h, :], qnb[:, h * 128:(h + 1) * 128], [(k_qnb, h // 4)], [BK[7]])
                for h in range(8):
                    tr(b3[0:64, h, :], qpb[:, h, :], [(k_qpb, 0), (k_qpb, 1)], [BK[3]])
                tcopy("act", qtn_st[:, s], b7, [BK[7]], [(k_qtn, s)])
                tcopy("dve", qtp_st[0:64, s], b3[0:64], [BK[3]], [(k_qtp, s)])
                dma("sp", QTN_v[:, :, a * 128:(a + 1) * 128], qtn_st[:, s], [(k_qtn, s)], [("QTN", a)], "qtn%d" % s)
                dma("sp", QTP_v[:, :, a * 128:(a + 1) * 128], qtp_st[0:64, s], [(k_qtp, s)], [("QTP", a)], "qtp%d" % s)
            hk = hT_keys(k_hT, range(4))
            for h in range(8):
                bk = 4 + (h % 2)
                for k in range(16):
                    mm(banks[bk][:, 0:512], wfq[:, k, h * 128:(h + 1) * 128], hT[:, k, :], k == 0, k == 15, hk + wkeys(k_wfq, 8), [BK[bk]])
                s2 = h % 2
                if h % 2 == 0:
                    amul(qf_st[:, s2], banks[bk][:, 0:512], FSCALE, [BK[bk]], [(k_qf, s2)])
                else:
                    ts("dve", qf_st[:, s2], banks[bk][:, 0:512], FSCALE, None, ALU.mult, None, [BK[bk]], [(k_qf, s2)])
                dma("sp", QTF[h][:, ot * 512:(ot + 1) * 512], qf_st[:, s2], [(k_qf, s2)], [("QTF", h, ot)], "qf%d" % s2)
        A.release(pO_mark)
        if STOP == 1:
            return finish()

        k_gkv, gkv = A.alloc("gkv", [256], F32)
        dma("sp", gkv, g_kv.partition_broadcast(128), [], [k_gkv], "c11")
        k_wlat, wlat = A.alloc("wlat", [16, 328], BF16)
        k_wfk, wfk = A.alloc("wfk", [16, 1024], BF16)
        k_wfv, wfv = A.alloc("wfv", [16, 1024], BF16)
        k_wkn, wkn = A.alloc("wkn", [2, 1024], BF16)
        k_wkv, wkv = A.alloc("wkv", [2, 1024], BF16)
        wload(wlat[:, :, 0:320], w_in_v[:, :, 512:832], None, k_wlat, "wlat", 2, 16)
        dma("pool", wlat[:, :, 320:328], w_in_v[:, :, 3904:3912], [], [(k_wlat, 2)], "wlat2")
        w_ukv_v = w_ukv.rearrange("(k p) (h t d) -> p k t h d", p=128, t=2, d=128)
        for k in range(2):
            dma("pool", wkn[:, k, :].rearrange("p (h d) -> p h d", d=128), w_ukv_v[:, k, 0], [], [(k_wkn, k)], "wkn")
            dma("pool", wkv[:, k, :].rearrange("p (h d) -> p h d", d=128), w_ukv_v[:, k, 1], [], [(k_wkv, k)], "wkv")
        wload(wfv, w_in_v[:, :, 2880:3904], None, k_wfv, "wfv", 8, 16)
        wload(wfk, w_in_v[:, :, 1856:2880], None, k_wfk, "wfk", 8, 16)
        k_ssq_k, ssq_k = A.alloc("ssq_k", [1], F32)
        k_rs_k, rs_k = A.alloc("rs_k", [1], F32)
        k_sq3, sq3 = A.alloc("sq3", [256], BF16)
        k_kvn, kvn = A.alloc("kvn", [256], BF16)
        k_kvnT, kvnT = A.alloc("kvnT", [2, 512], BF16)
        k_kpb, kpb = A.alloc("kpb", [64], BF16)
        rk = [A.alloc("rk%d" % i, [32], F32) for i in range(4)]
        k_vms, vm_st = A.alloc("vm_st", [2, 1024], BF16)
        k_vfs, vf_st = A.alloc("vf_st", [2, 1024], BF16)
        k_kst, kst = A.alloc("kst", [4, 512], BF16)
        wlat_keys = [(k_wlat, 0), (k_wlat, 1), (k_wlat, 2)]
        def b_T(n):
            transpose_part(nb_bufs, n % 2, hT, k_hT, (n % 4) * 128, (0, 1))

        def b_L(n):
            blk = n % 4
            hk = hT_keys(k_hT, [blk])
            for k in range(16):
                mm(banks[2][:, 0:328], hT[:, k, blk * 128:(blk + 1) * 128], wlat[:, k, :], k == 0, k == 15, hk + wlat_keys, [BK[2]])
            act(sq3, banks[2][:, 0:256], AF.Square, [BK[2]], [k_sq3, k_ssq_k], accum=ssq_k)
            rsqrt_mean(ssq_k, k_ssq_k, 256, rs_k, k_rs_k, 1)
            stt("dve", kvn, banks[2][:, 0:256], rs_k, gkv, ALU.mult, ALU.mult, [BK[2], k_rs_k, k_gkv], [k_kvn])
            x1v, x2v = banks[2][:, 256:288], banks[2][:, 288:320]
            cb_, sb_ = cosN[:, n, :], sinN[:, n, :]
            tt("dve", rk[0][1], x1v, cb_, ALU.mult, [BK[2], k_cosN], [rk[0][0]])
            tt("dve", rk[1][1], x2v, sb_, ALU.mult, [BK[2], k_sinN], [rk[1][0]])
            tt("dve", rk[2][1], x2v, cb_, ALU.mult, [BK[2], k_cosN], [rk[2][0]])
            tt("dve", rk[3][1], x1v, sb_, ALU.mult, [BK[2], k_sinN], [rk[3][0]])
            tt("dve", kpb[:, 0:32], rk[0][1], rk[1][1], ALU.subtract, [rk[0][0], rk[1][0]], [(k_kpb, 0)])
            tt("dve", kpb[:, 32:64], rk[2][1], rk[3][1], ALU.add, [rk[2][0], rk[3][0]], [(k_kpb, 1)])
            tcopy("dve", flog[:, n, :], banks[2][:, 320:328], [BK[2]], [(k_flog, n)])

        def b_t(n):
            blk = n % 4
            b3 = bf_bank(3)
            for k in range(2):
                tr(b3[:, k, :], kvn[:, k * 128:(k + 1) * 128], [k_kvn], [BK[3]])
            tr(b3[0:64, 2, :], kpb, [(k_kpb, 0), (k_kpb, 1)], [BK[3]])
            tcopy("act", kvnT[:, :, blk * 128:(blk + 1) * 128], b3[:, 0:2, :], [BK[3]], [(k_kvnT, blk)])
            tcopy("act", kpeT[0:64, n * 128:(n + 1) * 128], b3[0:64, 2, :], [BK[3]], [(k_kpeT, n)])

        def b_VM(n):
            blk = n % 4
            s = n % 2
            for half in range(2):
                bk = 4 + half
                for k in range(2):
                    mm(banks[bk][:, 0:512], kvnT[:, k, blk * 128:(blk + 1) * 128], wkv[:, k, half * 512:(half + 1) * 512], k == 0, k == 1, [(k_kvnT, blk)] + wkeys(k_wkv, 2), [BK[bk]])
                tcopy("act" if half == 0 else "dve", vm_st[:, s, half * 512:(half + 1) * 512], banks[bk][:, 0:512], [BK[bk]], [(k_vms, s, half)])
            dma("sp", VM[n * 128:(n + 1) * 128, :], vm_st[:, s], [(k_vms, s, 0), (k_vms, s, 1)], [("VM", n)], "vm%d" % s)

        def b_VF(n, half):
            blk = n % 4
            s = n % 2
            hk = hT_keys(k_hT, [blk])
            bk = 6 + half
            for k in range(16):
                mm(banks[bk][:, 0:512], hT[:, k, blk * 128:(blk + 1) * 128], wfv[:, k, half * 512:(half + 1) * 512], k == 0, k == 15, hk + wkeys(k_wfv, 8), [BK[bk]])
            tcopy("act" if half == 0 else "dve", vf_st[:, s, half * 512:(half + 1) * 512], banks[bk][:, 0:512], [BK[bk]], [(k_vfs, s, half)])
            if half == 1:
                dma("sp", VF[n * 128:(n + 1) * 128, :], vf_st[:, s], [(k_vfs, s, 0), (k_vfs, s, 1)], [("VF", n)], "vf%d" % s)

        kc_ = [0]

        def b_K(nt):
            hk = hT_keys(k_hT, range(4))
            for h in range(8):
                bk = 4 + (h % 2)
                for k in range(16):
                    mm(banks[bk][:, 0:512], wfk[:, k, h * 128:(h + 1) * 128], hT[:, k, :], k == 0, k == 15, hk + wkeys(k_wfk, 8), [BK[bk]])
                s2 = kc_[0] % 4
                kc_[0] += 1
                tcopy("act" if h % 2 == 0 else "dve", kst[:, s2], banks[bk][:, 0:512], [BK[bk]], [(k_kst, s2)])
                dma("sp", KTF[h][:, nt * 512:(nt + 1) * 512], kst[:, s2], [(k_kst, s2)], [("KTF", h, nt)], "kst%d" % s2)
            kT_keys = [(k_kvnT, b) for b in range(4)]
            for h in range(8):
                bk = 6 + (h % 2)
                for k in range(2):
                    mm(banks[bk][:, 0:512], wkn[:, k, h * 128:(h + 1) * 128], kvnT[:, k, :], k == 0, k == 1, kT_keys + wkeys(k_wkn, 2), [BK[bk]])
                s2 = kc_[0] % 4
                kc_[0] += 1
                tcopy("act" if h % 2 == 0 else "dve", kst[:, s2], banks[bk][:, 0:512], [BK[bk]], [(k_kst, s2)])
                dma("sp", KTM[h][:, nt * 512:(nt + 1) * 512], kst[:, s2], [(k_kst, s2)], [("KTM", h, nt)], "kst%d" % s2)

        NBLK = NT_LIMIT * 4
        norm_part(x_nat[0:128, :], nb_bufs, 0, gattn, k_gattn)
        for n in range(NBLK):
            if n + 1 < NBLK:
                norm_part(x_nat[(n + 1) * 128:(n + 2) * 128, :], nb_bufs, (n + 1) % 2, gattn, k_gattn)
            b_T(n)
            if n >= 1:
                b_VF(n - 1, 0)
            b_L(n)
            if n >= 1:
                b_VM(n - 1)
                b_VF(n - 1, 1)
            b_t(n)
            if n % 4 == 3:
                b_K(n // 4)
        b_VF(NBLK - 1, 0)
        b_VM(NBLK - 1)
        b_VF(NBLK - 1, 1)
        A.release(p1_mark)

        if "cum" in SKIP:
            return finish()
        m = A.mark()
        flog_keys = [(k_flog, n) for n in range(NB)]
        k_z, z = A.alloc("z", [NB, 8], F32)
        k_l, lpos = A.alloc("lpos", [NB, 8], F32)
        k_tb, tbc = A.alloc("tbc", [NB, 8], F32)
        k_pre, pre = A.alloc("pre", [NB, 8], F32)
        k_t4, t4 = A.alloc("t4", [NO, 8, NB], F32)
        tt("dve", z, flog, bfg.unsqueeze(1).broadcast_to([128, NB, 8]), ALU.add, flog_keys + [k_bfg], [k_z])
        ts("dve", z, z, -80.0, None, ALU.max, None, [k_z], [k_z])
        act(lpos, z, AF.Exp, [k_z], [k_l], scale=-1.0)
        act(lpos, lpos, AF.Ln, [k_l, k_oneb], [k_l], bias=oneb, scale=1.0)
        lflat = lpos.rearrange("p n h -> p (n h)")
        mm(banks[0][:, 0:256], trif, lflat, True, True, [k_trif, k_l], [BK[0]])
        mm(banks[1][:, 0:256], onesf, lflat, True, True, [k_onesf, k_l], [BK[1]])
        tcopy("dve", tbc.rearrange("p n h -> p (n h)"), banks[1][:, 0:256], [BK[1]], [k_tb])
        memset("dve", pre[:, 0, :], 0.0, [(k_pre, 0)])
        for n in range(1, NB):
            tt("dve", pre[:, n, :], pre[:, n - 1, :], tbc[:, n - 1, :], ALU.add, [(k_pre, n - 1), k_tb], [(k_pre, n)])
        tt("dve", negc.rearrange("p n h -> p (n h)"), banks[0][:, 0:256], pre.rearrange("p n h -> p (n h)"), ALU.add,
           [BK[0]] + [(k_pre, n) for n in range(NB)], [k_negc])
        tt("dve", t4, sel.unsqueeze(2).broadcast_to([128, NO, 8, NB]),
           negc.rearrange("p n h -> p h n").unsqueeze(1).broadcast_to([128, NO, 8, NB]), ALU.mult, [k_sel, k_negc], [k_t4])
        P.op("dve", lambda e: e.tensor_reduce(out=cown, in_=t4, axis=AX.X, op=ALU.add), reads=[k_t4], writes=[k_cown])
        A.release(m)

        if STOP == 2:
            return finish()

        p2_mark = A.mark()
        k_gout, gout = A.alloc("gout", [D], F32)
        dma("sp", gout, g_out.partition_broadcast(128), [], [k_gout], "c12")
        k_KT, KT = A.alloc("KT", [2, S], BF16)
        k_V, V = A.alloc("V", [2, NB, 130], BF16)
        k_QT, QT = A.alloc("QT", [2, S // 2], BF16)
        k_QP, QP = A.alloc("QP", [2, S // 2], BF16)
        k_cb, cb = A.alloc("cb", [8, S // 2], F32)
        k_dg, dg = A.alloc("dg", [8, 128], F32)
        k_PT, PT = A.alloc("PT", [3, 512], BF16)
        k_tmp, tmpF = A.alloc("tmpF", [3, 512], F32)
        k_osq, osq = A.alloc("osq", [128], F32)
        k_obf, obf = A.alloc("obf", [4, 128], BF16)
        k_of32, of32 = A.alloc("of32", [4, 128], F32)
        k_OTh, OTh = A.alloc("OTh", [2, S // 2], BF16)
        for sl in range(2):
            memset("dve", V[:, sl, :, 128:130], 1.0, [(k_V, sl, "ones")])
            memset("dve", QP[64:128, sl, :], 0.0, [(k_QP, sl, "z")])
        VM_v = VM.rearrange("(n p) c -> p n c", p=128)
        VF_v = VF.rearrange("(n p) c -> p n c", p=128)

        def head_loads(hh):
            fox = hh >= 8
            h = hh % 8
            sl = hh % 2
            KTsrc, Vsrc, Qsrc = (KTF, VF_v, QTF) if fox else (KTM, VM_v, QTN)
            ktag = "KTF" if fox else "KTM"
            for i in range(4):
                dma("sp", KT[:, sl, i * 1024:(i + 1) * 1024], KTsrc[h][:, i * 1024:(i + 1) * 1024],
                    [(ktag, h, 2 * i), (ktag, h, 2 * i + 1)], [(k_KT, sl, i)], "KT%d_%d" % (sl, i))
            vtag = "VF" if fox else "VM"
            for i in range(4):
                dma("sp", V[:, sl, i * 8:(i + 1) * 8, 0:128], Vsrc[:, i * 8:(i + 1) * 8, h * 128:(h + 1) * 128],
                    [(vtag, n) for n in range(i * 8, (i + 1) * 8)], [(k_V, sl, i)], "V%d" % sl)
            if fox:
                dma("sp", QT[:, sl, :], Qsrc[h], [("QTF", h, ot) for ot in range(4)], [(k_QT, sl)], "QT%d" % sl)
            else:
                dma("sp", QT[:, sl, :], Qsrc[h], [("QTN", a) for a in range(NO)], [(k_QT, sl)], "QT%d" % sl)
                dma("sp", QP[0:64, sl, :], QTP[h], [("QTP", a) for a in range(NO)], [(k_QP, sl)], "QP%d" % sl)

        head_loads(0)
        head_loads(1)
        k_nones, nones = A.alloc("nones", [128], F32)
        ts("dve", nones, onesf, -1.0, None, ALU.mult, None, [k_onesf], [k_nones])
        ci = 0
        for h in range(8):
            for a4 in range(4):
                ds_ = ci % 2
                bkc = 6 + ci % 2
                ci += 1
                tt("dve", dg[:, ds_ * 4:(ds_ + 1) * 4, :], identf.unsqueeze(1).broadcast_to([128, 4, 128]),
                   cown[:, a4 * 4:(a4 + 1) * 4, h:h + 1].broadcast_to([128, 4, 128]), ALU.mult, [k_identf, k_cown], [(k_dg, ds_)])
                mm(banks[bkc][:, 0:512], nones, dg[:, ds_ * 4:(ds_ + 1) * 4, :].rearrange("p a q -> p (a q)"), True, True, [k_nones, (k_dg, ds_)], [BK[bkc]])
                tcopy("dve", cb[:, h, a4 * 512:(a4 + 1) * 512], banks[bkc][:, 0:512], [BK[bkc]], [(k_cb, h, a4)])

        SBK = (0, 1, 2)
        OB = ((3, 4), (5, 6))
        LAG = 2
        tasks = []
        gidx = 0
        for hh in range(16):
            for j in range(8):
                npairs = 2 * (j + 1)
                for pp in range(npairs):
                    tasks.append((hh, j, pp, gidx, pp == npairs - 1))
                gidx += 1
        pending_tr = []
        tcount = [0]

        def t_qk(i):
            hh, j, pp, g, last = tasks[i]
            fox = hh >= 8
            sl = hh % 2
            jq = j * 256
            sbk = SBK[i % 3]
            for ii in range(2):
                n = 2 * pp + ii
                mm(banks[sbk][:, ii * 256:(ii + 1) * 256], KT[:, sl, n * 128:(n + 1) * 128], QT[:, sl, jq:jq + 256],
                   True, fox, [(k_KT, sl, n // 8), (k_QT, sl)], [BK[sbk]])
                if not fox:
                    mm(banks[sbk][:, ii * 256:(ii + 1) * 256], kpeT[:, n * 128:(n + 1) * 128], QP[:, sl, jq:jq + 256],
                       False, True, [(k_kpeT, n), (k_kpeT, "z"), (k_QP, sl), (k_QP, sl, "z")], [BK[sbk]])

        def t_sm(i):
            hh, j, pp, g, last = tasks[i]
            fox = hh >= 8
            h = hh % 8
            jq = j * 256
            sbk = SBK[i % 3]
            ps_ = i % 3
            n0 = 2 * pp
            ingroup = n0 >= 4 * j
            kbl = n0 - 4 * j
            if not fox:
                if ingroup:
                    tt("dve", tmpF[:, ps_], banks[sbk][:, 0:512], maskM[:, kbl * 256:(kbl + 2) * 256], ALU.add, [BK[sbk], k_maskM], [(k_tmp, ps_, 0), (k_tmp, ps_, 1)])
                    act(PT[:, ps_], tmpF[:, ps_], AF.Exp, [(k_tmp, ps_, 0), (k_tmp, ps_, 1)], [(k_PT, ps_, 0), (k_PT, ps_, 1)])
                else:
                    act(PT[:, ps_], banks[sbk][:, 0:512], AF.Exp, [BK[sbk]], [(k_PT, ps_, 0), (k_PT, ps_, 1)])
            else:
                tk = [(k_tmp, ps_, 0), (k_tmp, ps_, 1)]
                tt("dve", tmpF[:, ps_].rearrange("p (i q) -> p i q", i=2), banks[sbk][:, 0:512].rearrange("p (i q) -> p i q", i=2),
                   cb[:, h, jq:jq + 256].unsqueeze(1).broadcast_to([128, 2, 256]), ALU.add, [BK[sbk], (k_cb, h, j // 2)], tk)
                if ingroup:
                    tt("dve", tmpF[:, ps_], tmpF[:, ps_], maskF[:, kbl * 256:(kbl + 2) * 256], ALU.add, tk + [k_maskF], tk)
                for ii in range(2):
                    n = n0 + ii
                    act(PT[:, ps_, ii * 256:(ii + 1) * 256], tmpF[:, ps_, ii * 256:(ii + 1) * 256], AF.Exp, tk + [k_negc], [(k_PT, ps_, ii)],
                        bias=negc[:, n, h:h + 1], scale=1.0)

        def t_pv(i, step):
            hh, j, pp, g, last = tasks[i]
            fox = hh >= 8
            sl = hh % 2
            ps_ = i % 3
            ob = OB[g % 2]
            nkb = 4 * (j + 1)
            V_keys = [(k_V, sl, q) for q in range(4)] + [(k_V, sl, "ones")]
            for ii in range(2):
                n = 2 * pp + ii
                for al in range(2):
                    mm(banks[ob[al]][:, 0:129], PT[:, ps_, ii * 256 + al * 128:ii * 256 + (al + 1) * 128], V[:, sl, n, 0:129],
                       n == 0, n == nkb - 1, [(k_PT, ps_, ii)] + V_keys, [BK[ob[al]]])
            if not last:
                return
            for al in range(2):
                a = 2 * j + al
                r_ = (g % 2) * 2 + al
                obk = banks[ob[al]]
                rc = recs[:, a, hh:hh + 1]
                P.op("dve", lambda e, o=rc, i_=obk[:, 128:129]: e.reciprocal(out=o, in_=i_), reads=[BK[ob[al]]], writes=[(k_recs, a, hh)])
                ts("dve", of32[:, r_], obk[:, 0:128], rc, None, ALU.mult, None, [BK[ob[al]], (k_recs, a, hh)], [(k_of32, r_)])
                P.op("dve", lambda e, o_=of32[:, r_], acc=ssq_att[:, a, hh:hh + 1]: e.scalar_tensor_tensor(
                    out=osq, in0=o_, scalar=1.0, in1=o_, op0=ALU.mult, op1=ALU.mult, accum_out=acc),
                    reads=[(k_of32, r_)], writes=[k_osq, (k_ssa, a, hh)])
                tt("dve", obf[:, r_], of32[:, r_], gout[:, hh * 128:(hh + 1) * 128], ALU.mult, [(k_of32, r_), k_gout], [(k_obf, r_)])

                def do_tr(a=a, r_=r_, sl=sl, fox=fox):
                    b7 = bf_bank(7)
                    tsl = tcount[0] % 8
                    tcount[0] += 1
                    tr(b7[:, tsl, :], obf[:, r_], [(k_obf, r_)], [BK[7]])
                    tcopy("act" if fox else "dve", OTh[:, sl, a * 128:(a + 1) * 128], b7[:, tsl, :], [BK[7]], [(k_OTh, sl, a)])
                pending_tr.append((step + 2, do_tr))
            if j == 7:
                def do_store(hh=hh, sl=sl):
                    dma("sp", OTS[hh], OTh[:, sl], [(k_OTh, sl, a) for a in range(NO)], [("OTS", hh)], "OTh%d" % sl)
                    if hh + 2 < 16:
                        head_loads(hh + 2)
                pending_tr.append((step + 2, do_store))

        T = len(tasks)
        for step in range(T + LAG + 3):
            if step < T:
                t_qk(step)
                t_sm(step)
            if 0 <= step - LAG < T:
                t_pv(step - LAG, step)
            while pending_tr and pending_tr[0][0] <= step:
                pending_tr.pop(0)[1]()
        assert not pending_tr
        A.release(p2_mark)
        A.release(attn_mark)
        if STOP == 3:
            return finish()

        k_ssum, ssum = A.alloc("ssum", [NO * 2], F32)
        ssa_keys = [(k_ssa, a, hh) for a in range(NO) for hh in range(16)]
        P.op("dve", lambda e: e.tensor_reduce(out=ssum, in_=ssq_att.rearrange("p a (g h) -> p (a g) h", g=2), axis=AX.X, op=ALU.add),
             reads=ssa_keys, writes=[k_ssum])
        rsqrt_mean(ssum, k_ssum, 1024, rstd_att, k_rsa, NO * 2)
        k_gffn, gffn = A.alloc("gffn", [D], F32)
        k_gfin, gfin = A.alloc("gfin", [D], F32)
        dma("sp", gffn, g_ffn.partition_broadcast(128), [], [k_gffn], "c13")
        dma("sp", gfin, g_fin.partition_broadcast(128), [], [k_gfin], "c14")
        k_x1, x1 = A.alloc("x1", [8, D], F32)
        k_h2T, h2T = A.alloc("h2T", [16, 1024], BF16)
        k_ss2, ss2 = A.alloc("ss2", [8], F32)
        k_rs2, rs2 = A.alloc("rs2", [8], F32)
        k_ss3, ss3 = A.alloc("ss3", [8], F32)
        k_rs3, rs3 = A.alloc("rs3", [8], F32)
        OTS_v = OTS.rearrange("c d t -> d c t")
        w_out_v = w_out.rearrange("(c p) n -> p c n", p=128)
        w_gate_v = w_gate.rearrange("(k p) n -> p k n", p=128)
        w_up_v = w_up.rearrange("(k p) n -> p k n", p=128)
        for tg in range(2):
            mo = A.mark()
            k_OTg, OTg = A.alloc("OTg", [16, 1024], BF16)
            k_wo, wo = A.alloc("wo", [2, 16, 512], BF16)
            k_hn2, hn2 = A.alloc("hn2", [2, D], BF16)
            k_sq4, sq4 = A.alloc("sq4", [D], BF16)
            for i in range(4):
                dma("sp", OTg[:, i * 4:(i + 1) * 4, :], OTS_v[:, i * 4:(i + 1) * 4, tg * 1024:(tg + 1) * 1024],
                    [("OTS", c) for c in range(i * 4, (i + 1) * 4)], [(k_OTg, i)], "OTg%d" % i)
            for i in range(8):
                a = tg * 8 + i
                dma("sp", x1[:, i, :], x_own[a * 128:(a + 1) * 128, :], [], [(k_x1, i, ct) for ct in range(4)], "xo%d" % i)
            uc = 0
            for ct in range(4):
                ws = ct % 2
                for q in range(4):
                    dma("pool", wo[:, ws, q * 4:(q + 1) * 4, :], w_out_v[:, q * 4:(q + 1) * 4, ct * 512:(ct + 1) * 512], [], [(k_wo, ws, q)], "wo%d_%d" % (ws, q))
                for i in range(8):
                    a = tg * 8 + i
                    bm, bf_ = ((0, 1), (2, 3), (4, 5))[uc % 3]
                    uc += 1
                    for c in range(8):
                        mm(banks[bm][:, 0:512], OTg[:, c, i * 128:(i + 1) * 128], wo[:, ws, c, :], c == 0, c == 7, [(k_OTg, c // 4), (k_wo, ws, c // 4)], [BK[bm]])
                    for c in range(8, 16):
                        mm(banks[bf_][:, 0:512], OTg[:, c, i * 128:(i + 1) * 128], wo[:, ws, c, :], c == 8, c == 15, [(k_OTg, c // 4), (k_wo, ws, c // 4)], [BK[bf_]])
                    xs = x1[:, i, ct * 512:(ct + 1) * 512]
                    stt("dve", xs, banks[bm][:, 0:512], rstd_att[:, 2 * a:2 * a + 1], xs, ALU.mult, ALU.add, [BK[bm], k_rsa, (k_x1, i, ct)], [(k_x1, i, ct)])
                    stt("dve", xs, banks[bf_][:, 0:512], rstd_att[:, 2 * a + 1:2 * a + 2], xs, ALU.mult, ALU.add, [BK[bf_], k_rsa, (k_x1, i, ct)], [(k_x1, i, ct)])
            for i in range(8):
                act(sq4, x1[:, i, :], AF.Square, [(k_x1, i, ct) for ct in range(4)], [k_sq4, (k_ss2, i)], accum=ss2[:, i:i + 1])
            k_l2, lnv2 = A.alloc("lnv2", [8], F32)
            act(lnv2, ss2, AF.Ln, [(k_ss2, i) for i in range(8)] + [k_epsb], [k_l2], bias=epsb, scale=1.0 / D)
            act(rs2, lnv2, AF.Exp, [k_l2], [k_rs2], scale=-0.5)
            for i in range(8):
                s = i % 2
                stt("dve", hn2[:, s], x1[:, i, :], rs2[:, i:i + 1], gffn, ALU.mult, ALU.mult, [(k_x1, i, ct) for ct in range(4)] + [k_rs2, k_gffn], [(k_hn2, s)])
                for half in range(2):
                    tb = 6 + half
                    bv = bf_bank(tb)
                    for kq in range(8):
                        k = half * 8 + kq
                        tr(bv[:, kq, :], hn2[:, s, k * 128:(k + 1) * 128], [(k_hn2, s)], [BK[tb]])
                    tcopy("act" if half == 0 else "dve", h2T[:, half * 8:(half + 1) * 8, i * 128:(i + 1) * 128], bv, [BK[tb]], [(k_h2T, i, half)])
            A.release(mo)
            mf = A.mark()
            k_wg, wg = A.alloc("wg", [3, 16, 128], BF16)
            k_wu, wu = A.alloc("wu", [3, 16, 128], BF16)
            k_wd, wd = A.alloc("wd", [8, D], BF16)
            k_aT, aT = A.alloc("aT", [2, 4, 1024], BF16)
            k_sg, sgt = A.alloc("sgt", [2, 512], F32)
            h2keys = [[(k_h2T, i, hf) for i in range(tt_ * 4, tt_ * 4 + 4) for hf in range(2)] for tt_ in range(2)]
            ucnt = [0]
            dcnt = [0]

            def ffn_units(g):
                for cg in range(4):
                    c = g * 4 + cg
                    w3 = c % 3
                    dma("pool", wg[:, w3], w_gate_v[:, :, c * 128:(c + 1) * 128], [], [(k_wg, w3)], "wg%d" % w3)
                    dma("pool", wu[:, w3], w_up_v[:, :, c * 128:(c + 1) * 128], [], [(k_wu, w3)], "wu%d" % w3)
                    dma("pool", wd[:, c % 8, :], w_down[c * 128:(c + 1) * 128, :], [], [(k_wd, c % 8)], "wd%d" % (c % 8))
                    for tt_ in range(2):
                        bg, bu = ((0, 1), (2, 3), (4, 5))[ucnt[0] % 3]
                        ss = ucnt[0] % 2
                        ucnt[0] += 1
                        for k in range(16):
                            mm(banks[bg][:, 0:512], wg[:, w3, k, :], h2T[:, k, tt_ * 512:(tt_ + 1) * 512], k == 0, k == 15, [(k_wg, w3)] + h2keys[tt_], [BK[bg]])
                        for k in range(16):
                            mm(banks[bu][:, 0:512], wu[:, w3, k, :], h2T[:, k, tt_ * 512:(tt_ + 1) * 512], k == 0, k == 15, [(k_wu, w3)] + h2keys[tt_], [BK[bu]])
                        act(sgt[:, ss], banks[bg][:, 0:512], AF.Silu, [BK[bg]], [(k_sg, ss)])
                        tt("dve", aT[:, g % 2, cg, tt_ * 512:(tt_ + 1) * 512], sgt[:, ss], banks[bu][:, 0:512], ALU.mult, [(k_sg, ss), BK[bu]], [(k_aT, g % 2, cg, tt_)])

            def ffn_down(g):
                for i in range(8):
                    for ct in range(4):
                        bk = 6 + dcnt[0] % 2
                        dcnt[0] += 1
                        for cg in range(4):
                            c = g * 4 + cg
                            mm(banks[bk][:, 0:512], aT[:, g % 2, cg, i * 128:(i + 1) * 128], wd[:, c % 8, ct * 512:(ct + 1) * 512], cg == 0, cg == 3,
                               [(k_aT, g % 2, cg, i // 4), (k_wd, c % 8)], [BK[bk]])
                        xs = x1[:, i, ct * 512:(ct + 1) * 512]
                        tt("dve", xs, banks[bk][:, 0:512], xs, ALU.add, [BK[bk], (k_x1, i, ct)], [(k_x1, i, ct)])

            NG = NCH // 4
            for g in range(NG + 1):
                if g < NG:
                    ffn_units(g)
                if g >= 1:
                    ffn_down(g - 1)
            k_sq5, sq5 = A.alloc("sq5", [D], BF16)
            for i in range(8):
                act(sq5, x1[:, i, :], AF.Square, [(k_x1, i, ct) for ct in range(4)], [k_sq5, (k_ss3, i)], accum=ss3[:, i:i + 1])
            k_l3, lnv3 = A.alloc("lnv3", [8], F32)
            act(lnv3, ss3, AF.Ln, [(k_ss3, i) for i in range(8)] + [k_epsb], [k_l3], bias=epsb, scale=1.0 / D)
            act(rs3, lnv3, AF.Exp, [k_l3], [k_rs3], scale=-0.5)
            for i in range(8):
                a = tg * 8 + i
                xk = [(k_x1, i, ct) for ct in range(4)]
                stt("dve", x1[:, i, :], x1[:, i, :], rs3[:, i:i + 1], gfin, ALU.mult, ALU.mult, xk + [k_rs3, k_gfin], xk)
                dma("sp", out_d[a * 128:(a + 1) * 128, :], x1[:, i, :], xk, [("out", a)], "xo%d" % i)
            A.release(mf)
        return finish()


def own_blocks(par):
    r = []
    for j in range(8):
        r += [4 * j, 4 * j + 3] if par == 0 else [4 * j + 1, 4 * j + 2]
    return r


def _consts(par):
    ob = own_blocks(par)
    p = np.arange(128)
    maskM = np.zeros((128, 4, 2, 128), np.float32)
    maskF = np.zeros((128, 4, 2, 128), np.float32)
    for kbl in range(4):
        for al in range(2):
            n = kbl
            ia = ob[al]
            s_idx = n * 128 + p[:, None]
            t_idx = ia * 128 + p[None, :]
            maskM[:, kbl, al, :] = np.where((s_idx // 64) <= (t_idx // 64), 0.0, NEG)
            maskF[:, kbl, al, :] = np.where(s_idx <= t_idx, 0.0, NEG)
    sel = np.zeros((128, NO, NB), np.float32)
    for a, ia in enumerate(ob):
        sel[:, a, ia] = 1.0
    invf = (10000.0 ** (-np.arange(0, 64, 2, dtype=np.float32) / 64)).astype(np.float32)
    return {
        "c_ident": np.eye(128, dtype=np.float32),
        "c_tri": np.triu(np.ones((128, 128), np.float32)),
        "c_ones": np.ones((128, 128), np.float32),
        "c_invf": np.ascontiguousarray(np.broadcast_to(invf, (128, 32))),
        "c_maskM": maskM.reshape(128, 1024),
        "c_maskF": maskF.reshape(128, 1024),
        "c_sel": sel.reshape(128, NO * NB),
    }


_NC_CACHE = {}


def kernel(x, positions, g_attn_norm, w_in, b_forget, g_q_lat, w_uq, g_kv_lat, w_ukv, g_out_mla, g_out_fox,
           w_out, g_ffn_norm, w_gate, w_up, w_down, g_final_norm):
    f = lambda a: np.ascontiguousarray(np.asarray(a, dtype=np.float32))
    x = f(x)
    positions = np.asarray(positions).astype(np.int32)
    shared = {
        "w_in": f(w_in)[0], "w_uq": f(w_uq)[0], "w_ukv": f(w_ukv)[0], "w_out": f(w_out)[0],
        "w_gate": f(w_gate)[0], "w_up": f(w_up)[0], "w_down": f(w_down)[0],
        "g_attn": f(g_attn_norm)[0], "g_q": f(g_q_lat)[0], "g_kv": f(g_kv_lat)[0],
        "g_out": np.ascontiguousarray(np.concatenate([f(g_out_mla)[0], f(g_out_fox)[0]])),
        "g_ffn": f(g_ffn_norm)[0], "g_fin": f(g_final_norm), "b_fg": f(b_forget)[0],
    }
    consts = [_consts(0), _consts(1)]
    in_maps = []
    for c in range(8):
        b, par = c // 2, c % 2
        ob = own_blocks(par)
        xb = x[b].reshape(NB, 128, D)
        pb = positions[b].reshape(NB, 128)
        m = dict(shared)
        m.update(consts[par])
        m["x_nat"] = x[b]
        m["x_own"] = np.ascontiguousarray(xb[ob].reshape(NO * 128, D))
        m["pos_nat"] = np.ascontiguousarray(pb.T)
        m["pos_own"] = np.ascontiguousarray(pb[ob].T)
        in_maps.append(m)
    if STOP < 4:
        for m in in_maps:
            for k in ("w_out", "w_gate", "w_up", "w_down"):
                m[k] = np.ascontiguousarray(m[k][0:128])
            if STOP < 2:
                m["x_nat"] = np.ascontiguousarray(m["x_nat"][0:128])
    if "nc" not in _NC_CACHE:
        _NC_CACHE["nc"] = build_program()
    res = run_bass_kernel_spmd(_NC_CACHE["nc"], in_maps[:NCORES], core_ids=list(range(NCORES)))
    kernel.last_results = res
    out = np.zeros((4, S, D), np.float32) if NCORES < 8 else np.empty((4, S, D), np.float32)
    for c in range(NCORES):
        b, par = c // 2, c % 2
        ob = own_blocks(par)
        o = res.results[c]["out"].reshape(NO, 128, D)
        out[b].reshape(NB, 128, D)[ob] = o
    return out
```

```python
import contextlib
import math
import numpy as np
import concourse.bass as bass
import concourse.mybir as mybir
from concourse.bass_utils import run_bass_kernel_spmd

F32 = mybir.dt.float32
BF16 = mybir.dt.bfloat16
I32 = mybir.dt.int32
U8 = mybir.dt.uint8
AF = mybir.ActivationFunctionType
ALU = mybir.AluOpType
AX = mybir.AxisListType

D = 2048
S = 4096
NB = 32
NO = 16
FF = 5632
NCH = FF // 128
EPS = 1e-6
QSCALE = 1.0 / math.sqrt(192.0)
FSCALE = 1.0 / math.sqrt(128.0)
PI = float(np.pi)
NEG = -1.0e30
ARENA_BYTES = 211968

DEBUG = False
STOP = 99
NCORES = 8
SKIP = set()
NT_LIMIT = 8
LATSTEP = 99
ENGS = ("pe", "act", "dve", "pool", "sp")


class Op:
    __slots__ = ("eng", "fn", "deps", "needed", "count", "dma_sem")

    def __init__(self, eng, fn, dma_sem):
        self.eng = eng
        self.fn = fn
        self.deps = []
        self.needed = False
        self.count = None
        self.dma_sem = dma_sem


def _base(k):
    return k[0] if isinstance(k, tuple) else k


class Prog:
    def __init__(self, nc):
        self.nc = nc
        self.ops = {e: [] for e in ENGS}
        self.state = {}
        self.inherit = {}
        self.touch = {}
        self.last_dma = {}

    def op(self, eng, fn, reads=(), writes=(), dma_sem=None):
        o = Op(eng, fn, dma_sem)
        deps = []
        for b in list(reads) + list(writes):
            if b not in self.state:
                self.state[b] = [list(self.inherit.get(_base(b), [])), []]
        for b in reads:
            deps.extend(self.state[b][0])
            if _base(b) == "ps":
                deps.extend(r for r in self.state[b][1] if r.eng != eng)
        for b in writes:
            st = self.state[b]
            deps.extend(st[0])
            deps.extend(st[1])
        seen = set()
        for d in deps:
            if id(d) in seen or d is o:
                continue
            seen.add(id(d))
            if d.eng == "pe" and eng == "pe" and d.dma_sem is None and dma_sem is None:
                continue
            if d.dma_sem is not None:
                d = self.last_dma[d.dma_sem]
            d.needed = True
            o.deps.append(d)
        for b in reads:
            rl = self.state[b][1]
            if dma_sem is None:
                for i_ in range(len(rl)):
                    if rl[i_].eng == eng and rl[i_].dma_sem is None:
                        del rl[i_]
                        break
            rl.append(o)
        for b in writes:
            self.state[b] = [[o], []]
        tag = ("d", dma_sem) if dma_sem is not None else ("e", eng)
        for b in list(reads) + list(writes):
            self.touch.setdefault(_base(b), {})[tag] = o
        if dma_sem is not None:
            self.last_dma[dma_sem] = o
        self.ops[eng].append(o)
        return o

    def emit(self):
        nc = self.nc
        sem_names = set()
        for e in ENGS:
            for o in self.ops[e]:
                if o.dma_sem is not None:
                    sem_names.add("d_" + o.dma_sem)
        sem_names = sorted(sem_names)
        all_names = ["e_" + e for e in ENGS] + sem_names
        with contextlib.ExitStack() as st:
            sems = {n: st.enter_context(nc.semaphore(n)) for n in all_names}
            cnt = {n: 0 for n in all_names}
            for e in ENGS:
                for o in self.ops[e]:
                    if o.dma_sem is not None:
                        n = "d_" + o.dma_sem
                        cnt[n] += 16
                        o.count = (n, cnt[n])
                    elif o.needed:
                        n = "e_" + e
                        cnt[n] += 1
                        o.count = (n, cnt[n])
            blk = st.enter_context(nc.Block())

            def run(engobj, e):
                waited = {}
                for o in self.ops[e]:
                    need = {}
                    for d in o.deps:
                        n, v = d.count
                        if v > need.get(n, 0):
                            need[n] = v
                    for n, v in need.items():
                        if waited.get(n, 0) >= v:
                            continue
                        waited[n] = v
                        engobj.wait_ge(sems[n], v)
                    ins = o.fn(engobj)
                    if o.dma_sem is not None:
                        ins.then_inc(sems[o.count[0]], 16)
                    elif o.needed:
                        ins.then_inc(sems[o.count[0]], 1)
                if e == "sp":
                    for n in sem_names:
                        if cnt[n] > 0:
                            engobj.wait_ge(sems[n], cnt[n])

            blk.tensor(lambda t: run(t, "pe"))
            blk.scalar(lambda t: run(t, "act"))
            blk.vector(lambda t: run(t, "dve"))
            blk.gpsimd(lambda t: run(t, "pool"))
            blk.sync(lambda t: run(t, "sp"))


_DTSZ = {F32: 4, BF16: 2, I32: 4, U8: 1}


class Arena:
    def __init__(self, P, arena_ap):
        self.P = P
        self.a = arena_ap
        self.top = 0
        self.live = []
        self.dead = []
        self.uid = 0
        self.peak = 0

    def alloc(self, name, shape, dt, parts=128):
        self.uid += 1
        name = "%s#%d" % (name, self.uid)
        n = 1
        for s in shape:
            n *= s
        size = (n * _DTSZ[dt] + 63) // 64 * 64
        off = self.top
        self.top += size
        self.peak = max(self.peak, self.top)
        assert self.top <= ARENA_BYTES, ("SBUF arena overflow", name, self.top)
        inh = []
        for (nm, o, s) in self.dead:
            if o < off + size and off < o + s:
                inh.extend(self.P.touch.get(nm, {}).values())
        self.P.inherit[name] = inh
        self.live.append((name, off, size))
        v = self.a[0:parts, off:off + n * _DTSZ[dt]].bitcast(dt)
        if len(shape) == 2:
            v = v.rearrange("p (a b) -> p a b", a=shape[0])
        elif len(shape) == 3:
            v = v.rearrange("p (a b c) -> p a b c", a=shape[0], b=shape[1])
        return name, v

    def mark(self):
        return (self.top, len(self.live))

    def release(self, mark):
        top, nl = mark
        self.dead.extend(self.live[nl:])
        del self.live[nl:]
        self.top = top


def build_program():
    nc = bass.Bass("TRN2", target_bir_lowering=False)

    def din(name, shape, dt=F32):
        return nc.dram_tensor(name, list(shape), dt, kind="ExternalInput").ap()

    big2 = STOP >= 2
    big4 = STOP >= 4
    x_nat = din("x_nat", [S, D] if big2 else [128, D])
    x_own = din("x_own", [S // 2, D])
    posN_d = din("pos_nat", [128, NB], I32)
    posO_d = din("pos_own", [128, NO], I32)
    w_in = din("w_in", [D, 3912])
    w_uq = din("w_uq", [512, 1536])
    w_ukv = din("w_ukv", [256, 2048])
    w_out = din("w_out", [D, D] if big4 else [128, D])
    w_gate = din("w_gate", [D, FF] if big4 else [128, FF])
    w_up = din("w_up", [D, FF] if big4 else [128, FF])
    w_down = din("w_down", [FF, D] if big4 else [128, D])
    g_attn = din("g_attn", [D])
    g_q = din("g_q", [512])
    g_kv = din("g_kv", [256])
    g_out = din("g_out", [D])
    g_ffn = din("g_ffn", [D])
    g_fin = din("g_fin", [D])
    b_fg = din("b_fg", [8])
    ident_d = din("c_ident", [128, 128])
    tri_d = din("c_tri", [128, 128])
    ones_d = din("c_ones", [128, 128])
    invf_d = din("c_invf", [128, 32])
    maskM_d = din("c_maskM", [128, 1024])
    maskF_d = din("c_maskF", [128, 1024])
    sel_d = din("c_sel", [128, NO * NB])
    out_d = nc.dram_tensor("out", [S // 2, D], F32, kind="ExternalOutput").ap()

    skind = "ExternalOutput" if DEBUG else "Internal"

    def scratch(name, shape):
        return nc.dram_tensor(name, list(shape), BF16, kind=skind).ap()

    KTM = scratch("s_ktm", [8, 128, S])
    KTF = scratch("s_ktf", [8, 128, S])
    VM = scratch("s_vm", [S, 1024])
    VF = scratch("s_vf", [S, 1024])
    QTN = scratch("s_qtn", [8, 128, S // 2])
    QTP = scratch("s_qtp", [8, 64, S // 2])
    QTF = scratch("s_qtf", [8, 128, S // 2])
    OTS = scratch("s_ots", [16, 128, S // 2])

    with contextlib.ExitStack() as es:
        arena_t = es.enter_context(nc.sbuf_tensor("arena", [128, ARENA_BYTES], U8))
        banks = [es.enter_context(nc.psum_tensor("bank%d" % i, [128, 512], F32)) for i in range(8)]
        P = Prog(nc)
        A = Arena(P, arena_t[:])
        BK = [("ps", i) for i in range(8)]

        def bf_bank(i):
            return banks[i][:].bitcast(BF16).rearrange("p (a b) -> p a b", a=8)

        def dma(q, out, in_, reads, writes, sem):
            return P.op(q, lambda e: e.dma_start(out=out, in_=in_), reads=reads, writes=writes, dma_sem=sem)

        def mm(out, lhsT, rhs, start, stop, reads, writes):
            return P.op("pe", lambda e: e.matmul(out, lhsT=lhsT, rhs=rhs, start=start, stop=stop), reads=reads, writes=writes)

        def tr(out, in_, reads, writes):
            return P.op("pe", lambda e: e.transpose(out=out, in_=in_, identity=identb), reads=list(reads) + [k_identb], writes=writes)

        def act(out, in_, func, reads, writes, bias=None, scale=None, accum=None):
            kw = {}
            if bias is not None:
                kw["bias"] = bias
            if scale is not None:
                kw["scale"] = scale
            if accum is not None:
                kw["accum_out"] = accum
            return P.op("act", lambda e: e.activation(out=out, in_=in_, func=func, **kw), reads=reads, writes=writes)

        def amul(out, in_, c, reads, writes):
            return P.op("act", lambda e: e.mul(out=out, in_=in_, mul=c), reads=reads, writes=writes)

        def tcopy(eng, out, in_, reads, writes):
            if eng == "act":
                return P.op("act", lambda e: e.copy(out=out, in_=in_), reads=reads, writes=writes)
            return P.op(eng, lambda e: e.tensor_copy(out=out, in_=in_), reads=reads, writes=writes)

        def tt(eng, out, in0, in1, op, reads, writes):
            return P.op(eng, lambda e: e.tensor_tensor(out=out, in0=in0, in1=in1, op=op), reads=reads, writes=writes)

        def ts(eng, out, in0, s1, s2, op0, op1, reads, writes):
            if s2 is None:
                return P.op(eng, lambda e: e.tensor_scalar(out=out, in0=in0, scalar1=s1, scalar2=None, op0=op0), reads=reads, writes=writes)
            return P.op(eng, lambda e: e.tensor_scalar(out=out, in0=in0, scalar1=s1, scalar2=s2, op0=op0, op1=op1), reads=reads, writes=writes)

        def stt(eng, out, in0, scalar, in1, op0, op1, reads, writes):
            return P.op(eng, lambda e: e.scalar_tensor_tensor(out=out, in0=in0, scalar=scalar, in1=in1, op0=op0, op1=op1), reads=reads, writes=writes)

        def memset(eng, ap, val, writes):
            return P.op(eng, lambda e: e.memset(ap, val), writes=writes)

        def finish():
            P.emit()
            build_program.peak = A.peak
            return nc

        k_identf, identf = A.alloc("identf", [128], F32)
        k_identb, identb = A.alloc("identb", [128], BF16)
        k_epsb, epsb = A.alloc("epsb", [1], F32)
        k_oneb, oneb = A.alloc("oneb", [1], F32)
        k_hpib, hpib = A.alloc("hpib", [1], F32)
        k_rsa, rstd_att = A.alloc("rstd_att", [NO * 2], F32)
        k_ssa, ssq_att = A.alloc("ssq_att", [NO, 16], F32)
        k_recs, recs = A.alloc("recs", [NO, 16], F32)
        dma("sp", identf, ident_d, [], [k_identf], "c")
        memset("dve", epsb, EPS, [k_epsb])
        memset("dve", oneb, 1.0, [k_oneb])
        memset("dve", hpib, PI / 2, [k_hpib])
        cnt_sm = [0]

        k_lnv, lnv_all = A.alloc("lnv", [64], F32)

        def rsqrt_mean(ssq_ap, k_ssq, n_feat, out_ap, k_out, width):
            o = cnt_sm[0] % 2
            cnt_sm[0] += 1
            lnv = lnv_all[:, o * 32:o * 32 + width]
            act(lnv, ssq_ap, AF.Ln, [k_ssq, k_epsb], [(k_lnv, o)], bias=epsb, scale=1.0 / n_feat)
            act(out_ap, lnv, AF.Exp, [(k_lnv, o)], [k_out], scale=-0.5)

        attn_mark = A.mark()
        k_trif, trif = A.alloc("trif", [128], F32)
        k_onesf, onesf = A.alloc("onesf", [128], F32)
        k_maskM, maskM = A.alloc("maskM", [1024], F32)
        k_maskF, maskF = A.alloc("maskF", [1024], F32)
        k_sel, sel = A.alloc("sel", [NO, NB], F32)
        k_cosN, cosN = A.alloc("cosN", [NB, 32], F32)
        k_sinN, sinN = A.alloc("sinN", [NB, 32], F32)
        k_cosO, cosO = A.alloc("cosO", [NO, 32], F32)
        k_sinO, sinO = A.alloc("sinO", [NO, 32], F32)
        k_flog, flog = A.alloc("flog", [NB, 8], F32)
        k_negc, negc = A.alloc("negc", [NB, 8], F32)
        k_cown, cown = A.alloc("cown", [NO, 8], F32)
        k_bfg, bfg = A.alloc("bfg", [8], F32)
        k_kpeT, kpeT = A.alloc("kpeT", [S], BF16)
        memset("dve", kpeT[64:128, :], 0.0, [(k_kpeT, "z")])
        dma("sp", trif, tri_d, [], [k_trif], "c")
        dma("sp", onesf, ones_d, [], [k_onesf], "c")
        dma("sp", maskM, maskM_d, [], [k_maskM], "c")
        dma("sp", maskF, maskF_d, [], [k_maskF], "c")
        dma("sp", sel, sel_d.rearrange("p (a n) -> p a n", a=NO), [], [k_sel], "c")
        dma("sp", bfg, b_fg.partition_broadcast(128), [], [k_bfg], "c")

        k_piN, pos_iN = A.alloc("pos_iN", [NB], I32)
        k_piO, pos_iO = A.alloc("pos_iO", [NO], I32)
        k_if, invf = A.alloc("invf", [32], F32)
        dma("sp", pos_iN, posN_d, [], [k_piN], "c")
        dma("sp", pos_iO, posO_d, [], [k_piO], "c")
        dma("sp", invf, invf_d, [], [k_if], "c")
        k_join, joinb = A.alloc("joinb", [1], F32)
        memset("dve", joinb, 0.0, [k_join, k_identf, k_trif, k_onesf, k_maskM, k_maskF, k_sel, k_bfg, k_piN, k_piO, k_if])
        tcopy("dve", identb, identf, [k_identf], [k_identb])

        def make_tables(pos_i, k_pi, nb, cos_t, k_cos, sin_t, k_sin, scale, tag):
            m = A.mark()
            k_pf, pos_f = A.alloc("pos_f", [nb], F32)
            k_ang, ang = A.alloc("ang", [nb, 32], F32)
            k_t1, t1 = A.alloc("t1", [nb, 32], F32)
            k_kk, kk = A.alloc("kk", [nb, 32], F32)
            k_rr, rr = A.alloc("rr", [nb, 32], F32)
            k_ra, ra = A.alloc("ra", [nb, 32], F32)
            k_bb, bb = A.alloc("bb", [nb, 32], F32)
            k_sg, sg = A.alloc("sg", [nb, 32], F32)
            tcopy("dve", pos_f, pos_i, [k_pi], [k_pf])
            tt("dve", ang, pos_f.unsqueeze(2).broadcast_to([128, nb, 32]), invf.unsqueeze(1).broadcast_to([128, nb, 32]), ALU.mult, [k_pf, k_if], [k_ang])
            MAGIC = 12582912.0
            C1 = 6.28125
            C2 = 2 * PI - 6.28125
            ts("dve", t1, ang, 1.0 / (2 * PI), MAGIC, ALU.mult, ALU.add, [k_ang], [k_t1])
            ts("dve", kk, t1, -MAGIC, None, ALU.add, None, [k_t1], [k_kk])
            stt("dve", t1, kk, -C1, ang, ALU.mult, ALU.add, [k_kk, k_ang], [k_t1])
            stt("dve", rr, kk, -C2, t1, ALU.mult, ALU.add, [k_kk, k_t1], [k_rr])
            ts("dve", rr, rr, -3.1415925, 3.1415925, ALU.max, ALU.min, [k_rr], [k_rr])
            stt("dve", ra, rr, -1.0, rr, ALU.mult, ALU.max, [k_rr], [k_ra])
            act(cos_t, ra, AF.Sin, [k_ra, k_hpib], [k_cos], bias=hpib, scale=-1.0)
            ts("dve", t1, ra, -PI / 2, None, ALU.add, None, [k_ra], [k_t1])
            stt("dve", bb, t1, -1.0, t1, ALU.mult, ALU.max, [k_t1], [k_bb])
            act(sin_t, bb, AF.Sin, [k_bb, k_hpib], [k_sin], bias=hpib, scale=-1.0)
            act(sg, rr, AF.Sign, [k_rr], [k_sg])
            tt("dve", sin_t, sin_t, sg, ALU.mult, [k_sin, k_sg], [k_sin])
            if scale != 1.0:
                ts("dve", cos_t, cos_t, scale, None, ALU.mult, None, [k_cos], [k_cos])
                ts("dve", sin_t, sin_t, scale, None, ALU.mult, None, [k_sin], [k_sin])
            A.release(m)

        make_tables(pos_iN, k_piN, NB, cosN, k_cosN, sinN, k_sinN, 1.0, "n")
        make_tables(pos_iO, k_piO, NO, cosO, k_cosO, sinO, k_sinO, QSCALE, "o")

        if STOP == 0:
            return finish()

        def wload(dst, src, reads_none, k_dst, sem, pieces, axis_len):
            step = axis_len // pieces
            for i in range(pieces):
                dma("pool", dst[:, i * step:(i + 1) * step], src[:, i * step:(i + 1) * step], [], [(k_dst, i)], sem)

        def wkeys(k, pieces):
            return [(k, i) for i in range(pieces)]

        def norm_part(x_rows, bufs, s, g_b, k_g):
            (k_xt, xt), (k_hn, hn), (k_sq, sqj), (k_ss, ssq), (k_rs, rstd) = bufs
            dma("sp", xt[:, s], x_rows, [], [(k_xt, s)], "xt%d" % s)
            act(sqj, xt[:, s], AF.Square, [(k_xt, s)], [k_sq, (k_ss, s)], accum=ssq[:, s:s + 1])
            rsqrt_mean(ssq[:, s:s + 1], (k_ss, s), D, rstd[:, s:s + 1], (k_rs, s), 1)
            stt("dve", hn[:, s], xt[:, s], rstd[:, s:s + 1], g_b, ALU.mult, ALU.mult, [(k_xt, s), (k_rs, s), k_g], [(k_hn, s)])

        def transpose_part(bufs, s, hT, k_hT, col0, tb):
            (k_xt, xt), (k_hn, hn), (k_sq, sqj), (k_ss, ssq), (k_rs, rstd) = bufs
            for half in range(2):
                bv = bf_bank(tb[half])
                for kq in range(8):
                    k = half * 8 + kq
                    tr(bv[:, kq, :], hn[:, s, k * 128:(k + 1) * 128], [(k_hn, s)], [BK[tb[half]]])
                tcopy("act" if half == 0 else "dve", hT[:, half * 8:(half + 1) * 8, col0:col0 + 128], bv, [BK[tb[half]]], [(k_hT, col0 // 128, half)])

        def hT_keys(k_hT, blks):
            return [(k_hT, b, h) for b in blks for h in range(2)]

        p1_mark = A.mark()
        k_gattn, gattn = A.alloc("gattn", [D], F32)
        dma("sp", gattn, g_attn.partition_broadcast(128), [], [k_gattn], "c9")
        nb_bufs = (A.alloc("xt", [2, D], F32), A.alloc("hn", [2, D], BF16), A.alloc("sqj", [D], BF16),
                   A.alloc("ssq", [2], F32), A.alloc("rstd", [2], F32))
        k_hT, hT = A.alloc("hT", [16, 512], BF16)
        pO_mark = A.mark()
        k_gq, gq = A.alloc("gq", [512], F32)
        dma("sp", gq, g_q.partition_broadcast(128), [], [k_gq], "c10")
        k_wq, wq = A.alloc("wq", [16, 512], BF16)
        k_wfq, wfq = A.alloc("wfq", [16, 1024], BF16)
        k_wuqn, wuqn = A.alloc("wuqn", [4, 1024], BF16)
        k_wuqp, wuqp = A.alloc("wuqp", [4, 512], BF16)
        w_in_v = w_in.rearrange("(k p) n -> p k n", p=128)
        wload(wq, w_in_v[:, :, 0:512], None, k_wq, "wq", 4, 16)
        w_uq_v = w_uq.rearrange("(k p) (h d) -> p k h d", p=128, d=192)
        for k in range(4):
            dma("pool", wuqn[:, k, :].rearrange("p (h d) -> p h d", d=128), w_uq_v[:, k, :, 0:128], [], [(k_wuqn, k)], "wuqn")
            dma("pool", wuqp[:, k, :].rearrange("p (h d) -> p h d", d=64), w_uq_v[:, k, :, 128:192], [], [(k_wuqp, k)], "wuqp")
        wload(wfq, w_in_v[:, :, 832:1856], None, k_wfq, "wfq", 8, 16)
        k_ssq_q, ssq_q = A.alloc("ssq_q", [1], F32)
        k_rs_q, rs_q = A.alloc("rs_q", [1], F32)
        k_sq2, sq2 = A.alloc("sq2", [512], BF16)
        k_qn, qn = A.alloc("qn", [512], BF16)
        k_qnT, qnT = A.alloc("qnT", [4, 128], BF16)
        k_qnb, qnb = A.alloc("qnb", [1024], BF16)
        k_qpb, qpb = A.alloc("qpb", [8, 64], BF16)
        rt = [A.alloc("rt%d" % i, [8, 32], F32) for i in range(4)]
        k_qtn, qtn_st = A.alloc("qtn_st", [2, 8, 128], BF16)
        k_qtp, qtp_st = A.alloc("qtp_st", [2, 8, 128], BF16)
        k_qf, qf_st = A.alloc("qf_st", [2, 512], BF16)

        QTN_v = QTN.rearrange("h d t -> d h t")
        QTP_v = QTP.rearrange("h d t -> d h t")
        norm_part(x_own[0:128, :], nb_bufs, 0, gattn, k_gattn)
        for ot in range(4):
            for blk in range(4):
                a = ot * 4 + blk
                s = a % 2
                if a + 1 < NO:
                    norm_part(x_own[(a + 1) * 128:(a + 2) * 128, :], nb_bufs, (a + 1) % 2, gattn, k_gattn)
                transpose_part(nb_bufs, s, hT, k_hT, blk * 128, (0, 1))
                hk = hT_keys(k_hT, [blk])
                for k in range(16):
                    mm(banks[2][:, 0:512], hT[:, k, blk * 128:(blk + 1) * 128], wq[:, k, :], k == 0, k == 15, hk + wkeys(k_wq, 4), [BK[2]])
                act(sq2, banks[2][:, 0:512], AF.Square, [BK[2]], [k_sq2, k_ssq_q], accum=ssq_q)
                rsqrt_mean(ssq_q, k_ssq_q, 512, rs_q, k_rs_q, 1)
                stt("dve", qn, banks[2][:, 0:512], rs_q, gq, ALU.mult, ALU.mult, [BK[2], k_rs_q, k_gq], [k_qn])
                b3 = bf_bank(3)
                for k in range(4):
                    tr(b3[:, k, :], qn[:, k * 128:(k + 1) * 128], [k_qn], [BK[3]])
                tcopy("dve", qnT, b3[:, 0:4, :], [BK[3]], [k_qnT])
                for (bk, wsrc, c0, kw) in ((4, wuqn, 0, wkeys(k_wuqn, 4)), (5, wuqn, 512, wkeys(k_wuqn, 4)), (6, wuqp, 0, wkeys(k_wuqp, 4))):
                    for k in range(4):
                        mm(banks[bk][:, 0:512], qnT[:, k, :], wsrc[:, k, c0:c0 + 512], k == 0, k == 3, [k_qnT] + kw, [BK[bk]])
                amul(qnb[:, 0:512], banks[4][:, 0:512], QSCALE, [BK[4]], [(k_qnb, 0)])
                amul(qnb[:, 512:1024], banks[5][:, 0:512], QSCALE, [BK[5]], [(k_qnb, 1)])
                b6 = banks[6][:, 0:512].rearrange("p (h d) -> p h d", d=64)
                x1v, x2v = b6[:, :, 0:32], b6[:, :, 32:64]
                cb_ = cosO[:, a, :].unsqueeze(1).broadcast_to([128, 8, 32])
                sb_ = sinO[:, a, :].unsqueeze(1).broadcast_to([128, 8, 32])
                tt("dve", rt[0][1], x1v, cb_, ALU.mult, [BK[6], k_cosO], [rt[0][0]])
                tt("dve", rt[1][1], x2v, sb_, ALU.mult, [BK[6], k_sinO], [rt[1][0]])
                tt("dve", rt[2][1], x2v, cb_, ALU.mult, [BK[6], k_cosO], [rt[2][0]])
                tt("dve", rt[3][1], x1v, sb_, ALU.mult, [BK[6], k_sinO], [rt[3][0]])
                tt("dve", qpb[:, :, 0:32], rt[0][1], rt[1][1], ALU.subtract, [rt[0][0], rt[1][0]], [(k_qpb, 0)])
                tt("dve", qpb[:, :, 32:64], rt[2][1], rt[3][1], ALU.add, [rt[2][0], rt[3][0]], [(k_qpb, 1)])
                b7 = bf_bank(7)
                for h in range(8):
                    tr(b7[:, h, :], qnb[:, h * 128:(h + 1) * 128], [(k_qnb, h // 4)], [BK[7]])
                for h in range(8):
                    tr(b3[0:64, h, :], qpb[:, h, :], [(k_qpb, 0), (k_qpb, 1)], [BK[3]])
                tcopy("act", qtn_st[:, s], b7, [BK[7]], [(k_qtn, s)])
                tcopy("dve", qtp_st[0:64, s], b3[0:64], [BK[3]], [(k_qtp, s)])
                dma("sp", QTN_v[:, :, a * 128:(a + 1) * 128], qtn_st[:, s], [(k_qtn, s)], [("QTN", a)], "qtn%d" % s)
                dma("sp", QTP_v[:, :, a * 128:(a + 1) * 128], qtp_st[0:64, s], [(k_qtp, s)], [("QTP", a)], "qtp%d" % s)
            hk = hT_keys(k_hT, range(4))
            for h in range(8):
                bk = 4 + (h % 2)
                for k in range(16):
                    mm(banks[bk][:, 0:512], wfq[:, k, h * 128:(h + 1) * 128], hT[:, k, :], k == 0, k == 15, hk + wkeys(k_wfq, 8), [BK[bk]])
                s2 = h % 2
                if h % 2 == 0:
                    amul(qf_st[:, s2], banks[bk][:, 0:512], FSCALE, [BK[bk]], [(k_qf, s2)])
                else:
                    ts("dve", qf_st[:, s2], banks[bk][:, 0:512], FSCALE, None, ALU.mult, None, [BK[bk]], [(k_qf, s2)])
                dma("sp", QTF[h][:, ot * 512:(ot + 1) * 512], qf_st[:, s2], [(k_qf, s2)], [("QTF", h, ot)], "qf%d" % s2)
        A.release(pO_mark)
        if STOP == 1:
            return finish()

        k_gkv, gkv = A.alloc("gkv", [256], F32)
        dma("sp", gkv, g_kv.partition_broadcast(128), [], [k_gkv], "c11")
        k_wlat, wlat = A.alloc("wlat", [16, 328], BF16)
        k_wfk, wfk = A.alloc("wfk", [16, 1024], BF16)
        k_wfv, wfv = A.alloc("wfv", [16, 1024], BF16)
        k_wkn, wkn = A.alloc("wkn", [2, 1024], BF16)
        k_wkv, wkv = A.alloc("wkv", [2, 1024], BF16)
        wload(wlat[:, :, 0:320], w_in_v[:, :, 512:832], None, k_wlat, "wlat", 2, 16)
        dma("pool", wlat[:, :, 320:328], w_in_v[:, :, 3904:3912], [], [(k_wlat, 2)], "wlat2")
        w_ukv_v = w_ukv.rearrange("(k p) (h t d) -> p k t h d", p=128, t=2, d=128)
        for k in range(2):
            dma("pool", wkn[:, k, :].rearrange("p (h d) -> p h d", d=128), w_ukv_v[:, k, 0], [], [(k_wkn, k)], "wkn")
            dma("pool", wkv[:, k, :].rearrange("p (h d) -> p h d", d=128), w_ukv_v[:, k, 1], [], [(k_wkv, k)], "wkv")
        wload(wfv, w_in_v[:, :, 2880:3904], None, k_wfv, "wfv", 8, 16)
        wload(wfk, w_in_v[:, :, 1856:2880], None, k_wfk, "wfk", 8, 16)
        k_ssq_k, ssq_k = A.alloc("ssq_k", [1], F32)
        k_rs_k, rs_k = A.alloc("rs_k", [1], F32)
        k_sq3, sq3 = A.alloc("sq3", [256], BF16)
        k_kvn, kvn = A.alloc("kvn", [256], BF16)
        k_kvnT, kvnT = A.alloc("kvnT", [2, 512], BF16)
        k_kpb, kpb = A.alloc("kpb", [64], BF16)
        rk = [A.alloc("rk%d" % i, [32], F32) for i in range(4)]
        k_vms, vm_st = A.alloc("vm_st", [2, 1024], BF16)
        k_vfs, vf_st = A.alloc("vf_st", [2, 1024], BF16)
        k_kst, kst = A.alloc("kst", [4, 512], BF16)
        wlat_keys = [(k_wlat, 0), (k_wlat, 1), (k_wlat, 2)]
        def b_T(n):
            transpose_part(nb_bufs, n % 2, hT, k_hT, (n % 4) * 128, (0, 1))

        def b_L(n):
            blk = n % 4
            hk = hT_keys(k_hT, [blk])
            for k in range(16):
                mm(banks[2][:, 0:328], hT[:, k, blk * 128:(blk + 1) * 128], wlat[:, k, :], k == 0, k == 15, hk + wlat_keys, [BK[2]])
            act(sq3, banks[2][:, 0:256], AF.Square, [BK[2]], [k_sq3, k_ssq_k], accum=ssq_k)
            rsqrt_mean(ssq_k, k_ssq_k, 256, rs_k, k_rs_k, 1)
            stt("dve", kvn, banks[2][:, 0:256], rs_k, gkv, ALU.mult, ALU.mult, [BK[2], k_rs_k, k_gkv], [k_kvn])
            x1v, x2v = banks[2][:, 256:288], banks[2][:, 288:320]
            cb_, sb_ = cosN[:, n, :], sinN[:, n, :]
            tt("dve", rk[0][1], x1v, cb_, ALU.mult, [BK[2], k_cosN], [rk[0][0]])
            tt("dve", rk[1][1], x2v, sb_, ALU.mult, [BK[2], k_sinN], [rk[1][0]])
            tt("dve", rk[2][1], x2v, cb_, ALU.mult, [BK[2], k_cosN], [rk[2][0]])
            tt("dve", rk[3][1], x1v, sb_, ALU.mult, [BK[2], k_sinN], [rk[3][0]])
            tt("dve", kpb[:, 0:32], rk[0][1], rk[1][1], ALU.subtract, [rk[0][0], rk[1][0]], [(k_kpb, 0)])
            tt("dve", kpb[:, 32:64], rk[2][1], rk[3][1], ALU.add, [rk[2][0], rk[3][0]], [(k_kpb, 1)])
            tcopy("dve", flog[:, n, :], banks[2][:, 320:328], [BK[2]], [(k_flog, n)])

        def b_t(n):
            blk = n % 4
            b3 = bf_bank(3)
            for k in range(2):
                tr(b3[:, k, :], kvn[:, k * 128:(k + 1) * 128], [k_kvn], [BK[3]])
            tr(b3[0:64, 2, :], kpb, [(k_kpb, 0), (k_kpb, 1)], [BK[3]])
            tcopy("act", kvnT[:, :, blk * 128:(blk + 1) * 128], b3[:, 0:2, :], [BK[3]], [(k_kvnT, blk)])
            tcopy("act", kpeT[0:64, n * 128:(n + 1) * 128], b3[0:64, 2, :], [BK[3]], [(k_kpeT, n)])

        def b_VM(n):
            blk = n % 4
            s = n % 2
            for half in range(2):
                bk = 4 + half
                for k in range(2):
                    mm(banks[bk][:, 0:512], kvnT[:, k, blk * 128:(blk + 1) * 128], wkv[:, k, half * 512:(half + 1) * 512], k == 0, k == 1, [(k_kvnT, blk)] + wkeys(k_wkv, 2), [BK[bk]])
                tcopy("act" if half == 0 else "dve", vm_st[:, s, half * 512:(half + 1) * 512], banks[bk][:, 0:512], [BK[bk]], [(k_vms, s, half)])
            dma("sp", VM[n * 128:(n + 1) * 128, :], vm_st[:, s], [(k_vms, s, 0), (k_vms, s, 1)], [("VM", n)], "vm%d" % s)

        def b_VF(n, half):
            blk = n % 4
            s = n % 2
            hk = hT_keys(k_hT, [blk])
            bk = 6 + half
            for k in range(16):
                mm(banks[bk][:, 0:512], hT[:, k, blk * 128:(blk + 1) * 128], wfv[:, k, half * 512:(half + 1) * 512], k == 0, k == 15, hk + wkeys(k_wfv, 8), [BK[bk]])
            tcopy("act" if half == 0 else "dve", vf_st[:, s, half * 512:(half + 1) * 512], banks[bk][:, 0:512], [BK[bk]], [(k_vfs, s, half)])
            if half == 1:
                dma("sp", VF[n * 128:(n + 1) * 128, :], vf_st[:, s], [(k_vfs, s, 0), (k_vfs, s, 1)], [("VF", n)], "vf%d" % s)

        kc_ = [0]

        def b_K(nt):
            hk = hT_keys(k_hT, range(4))
            for h in range(8):
                bk = 4 + (h % 2)
                for k in range(16):
                    mm(banks[bk][:, 0:512], wfk[:, k, h * 128:(h + 1) * 128], hT[:, k, :], k == 0, k == 15, hk + wkeys(k_wfk, 8), [BK[bk]])
                s2 = kc_[0] % 4
                kc_[0] += 1
                tcopy("act" if h % 2 == 0 else "dve", kst[:, s2], banks[bk][:, 0:512], [BK[bk]], [(k_kst, s2)])
                dma("sp", KTF[h][:, nt * 512:(nt + 1) * 512], kst[:, s2], [(k_kst, s2)], [("KTF", h, nt)], "kst%d" % s2)
            kT_keys = [(k_kvnT, b) for b in range(4)]
            for h in range(8):
                bk = 6 + (h % 2)
                for k in range(2):
                    mm(banks[bk][:, 0:512], wkn[:, k, h * 128:(h + 1) * 128], kvnT[:, k, :], k == 0, k == 1, kT_keys + wkeys(k_wkn, 2), [BK[bk]])
                s2 = kc_[0] % 4
                kc_[0] += 1
                tcopy("act" if h % 2 == 0 else "dve", kst[:, s2], banks[bk][:, 0:512], [BK[bk]], [(k_kst, s2)])
                dma("sp", KTM[h][:, nt * 512:(nt + 1) * 512], kst[:, s2], [(k_kst, s2)], [("KTM", h, nt)], "kst%d" % s2)

        NBLK = NT_LIMIT * 4
        norm_part(x_nat[0:128, :], nb_bufs, 0, gattn, k_gattn)
        for n in range(NBLK):
            if n + 1 < NBLK:
                norm_part(x_nat[(n + 1) * 128:(n + 2) * 128, :], nb_bufs, (n + 1) % 2, gattn, k_gattn)
            b_T(n)
            if n >= 1:
                b_VF(n - 1, 0)
            b_L(n)
            if n >= 1:
                b_VM(n - 1)
                b_VF(n - 1, 1)
            b_t(n)
            if n % 4 == 3:
                b_K(n // 4)
        b_VF(NBLK - 1, 0)
        b_VM(NBLK - 1)
        b_VF(NBLK - 1, 1)
        A.release(p1_mark)

        if "cum" in SKIP:
            return finish()
        m = A.mark()
        flog_keys = [(k_flog, n) for n in range(NB)]
        k_z, z = A.alloc("z", [NB, 8], F32)
        k_l, lpos = A.alloc("lpos", [NB, 8], F32)
        k_tb, tbc = A.alloc("tbc", [NB, 8], F32)
        k_pre, pre = A.alloc("pre", [NB, 8], F32)
        k_t4, t4 = A.alloc("t4", [NO, 8, NB], F32)
        tt("dve", z, flog, bfg.unsqueeze(1).broadcast_to([128, NB, 8]), ALU.add, flog_keys + [k_bfg], [k_z])
        ts("dve", z, z, -80.0, None, ALU.max, None, [k_z], [k_z])
        act(lpos, z, AF.Exp, [k_z], [k_l], scale=-1.0)
        act(lpos, lpos, AF.Ln, [k_l, k_oneb], [k_l], bias=oneb, scale=1.0)
        lflat = lpos.rearrange("p n h -> p (n h)")
        mm(banks[0][:, 0:256], trif, lflat, True, True, [k_trif, k_l], [BK[0]])
        mm(banks[1][:, 0:256], onesf, lflat, True, True, [k_onesf, k_l], [BK[1]])
        tcopy("dve", tbc.rearrange("p n h -> p (n h)"), banks[1][:, 0:256], [BK[1]], [k_tb])
        memset("dve", pre[:, 0, :], 0.0, [(k_pre, 0)])
        for n in range(1, NB):
            tt("dve", pre[:, n, :], pre[:, n - 1, :], tbc[:, n - 1, :], ALU.add, [(k_pre, n - 1), k_tb], [(k_pre, n)])
        tt("dve", negc.rearrange("p n h -> p (n h)"), banks[0][:, 0:256], pre.rearrange("p n h -> p (n h)"), ALU.add,
           [BK[0]] + [(k_pre, n) for n in range(NB)], [k_negc])
        tt("dve", t4, sel.unsqueeze(2).broadcast_to([128, NO, 8, NB]),
           negc.rearrange("p n h -> p h n").unsqueeze(1).broadcast_to([128, NO, 8, NB]), ALU.mult, [k_sel, k_negc], [k_t4])
        P.op("dve", lambda e: e.tensor_reduce(out=cown, in_=t4, axis=AX.X, op=ALU.add), reads=[k_t4], writes=[k_cown])
        A.release(m)

        if STOP == 2:
            return finish()

        p2_mark = A.mark()
        k_gout, gout = A.alloc("gout", [D], F32)
        dma("sp", gout, g_out.partition_broadcast(128), [], [k_gout], "c12")
        k_KT, KT = A.alloc("KT", [2, S], BF16)
        k_V, V = A.alloc("V", [2, NB, 130], BF16)
        k_QT, QT = A.alloc("QT", [2, S // 2], BF16)
        k_QP, QP = A.alloc("QP", [2, S // 2], BF16)
        k_cb, cb = A.alloc("cb", [8, S // 2], F32)
        k_dg, dg = A.alloc("dg", [8, 128], F32)
        k_PT, PT = A.alloc("PT", [3, 512], BF16)
        k_tmp, tmpF = A.alloc("tmpF", [3, 512], F32)
        k_osq, osq = A.alloc("osq", [128], F32)
        k_obf, obf = A.alloc("obf", [4, 128], BF16)
        k_of32, of32 = A.alloc("of32", [4, 128], F32)
        k_OTh, OTh = A.alloc("OTh", [2, S // 2], BF16)
        for sl in range(2):
            memset("dve", V[:, sl, :, 128:130], 1.0, [(k_V, sl, "ones")])
            memset("dve", QP[64:128, sl, :], 0.0, [(k_QP, sl, "z")])
        VM_v = VM.rearrange("(n p) c -> p n c", p=128)
        VF_v = VF.rearrange("(n p) c -> p n c", p=128)

        def head_loads(hh):
            fox = hh >= 8
            h = hh % 8
            sl = hh % 2
            KTsrc, Vsrc, Qsrc = (KTF, VF_v, QTF) if fox else (KTM, VM_v, QTN)
            ktag = "KTF" if fox else "KTM"
            for i in range(4):
                dma("sp", KT[:, sl, i * 1024:(i + 1) * 1024], KTsrc[h][:, i * 1024:(i + 1) * 1024],
                    [(ktag, h, 2 * i), (ktag, h, 2 * i + 1)], [(k_KT, sl, i)], "KT%d_%d" % (sl, i))
            vtag = "VF" if fox else "VM"
            for i in range(4):
                dma("sp", V[:, sl, i * 8:(i + 1) * 8, 0:128], Vsrc[:, i * 8:(i + 1) * 8, h * 128:(h + 1) * 128],
                    [(vtag, n) for n in range(i * 8, (i + 1) * 8)], [(k_V, sl, i)], "V%d" % sl)
            if fox:
                dma("sp", QT[:, sl, :], Qsrc[h], [("QTF", h, ot) for ot in range(4)], [(k_QT, sl)], "QT%d" % sl)
            else:
                dma("sp", QT[:, sl, :], Qsrc[h], [("QTN", a) for a in range(NO)], [(k_QT, sl)], "QT%d" % sl)
                dma("sp", QP[0:64, sl, :], QTP[h], [("QTP", a) for a in range(NO)], [(k_QP, sl)], "QP%d" % sl)

        head_loads(0)
        head_loads(1)
        k_nones, nones = A.alloc("nones", [128], F32)
        ts("dve", nones, onesf, -1.0, None, ALU.mult, None, [k_onesf], [k_nones])
        items = [(h, a4) for h in range(8) for a4 in range(4)]

        def cb_mult(ci):
            h, a4 = items[ci]
            ds_ = ci % 2
            tt("dve", dg[:, ds_ * 4:(ds_ + 1) * 4, :], identf.unsqueeze(1).broadcast_to([128, 4, 128]),
               cown[:, a4 * 4:(a4 + 1) * 4, h:h + 1].broadcast_to([128, 4, 128]), ALU.mult, [k_identf, k_cown], [(k_dg, ds_)])

        cb_mult(0)
        cb_mult(1)
        for ci in range(len(items)):
            h, a4 = items[ci]
            ds_ = ci % 2
            bkc = 6 + ci % 2
            mm(banks[bkc][:, 0:512], nones, dg[:, ds_ * 4:(ds_ + 1) * 4, :].rearrange("p a q -> p (a q)"), True, True, [k_nones, (k_dg, ds_)], [BK[bkc]])
            if ci >= 1:
                hp, ap = items[ci - 1]
                bkp = 6 + (ci - 1) % 2
                tcopy("dve", cb[:, hp, ap * 512:(ap + 1) * 512], banks[bkp][:, 0:512], [BK[bkp]], [(k_cb, hp, ap)])
            if ci + 2 < len(items):
                cb_mult(ci + 2)
        hp, ap = items[-1]
        bkp = 6 + (len(items) - 1) % 2
        tcopy("dve", cb[:, hp, ap * 512:(ap + 1) * 512], banks[bkp][:, 0:512], [BK[bkp]], [(k_cb, hp, ap)])

        SBK = (0, 1, 2)
        OB = ((3, 4), (5, 6))
        LAG = 2
        tasks = []
        gidx = 0
        for hh in range(16):
            for j in range(8):
                npairs = 2 * (j + 1)
                for pp in range(npairs):
                    tasks.append((hh, j, pp, gidx, pp == npairs - 1))
                gidx += 1
        pending_tr = []
        tcount = [0]

        def t_qk(i):
            hh, j, pp, g, last = tasks[i]
            fox = hh >= 8
            sl = hh % 2
            jq = j * 256
            sbk = SBK[i % 3]
            for ii in range(2):
                n = 2 * pp + ii
                mm(banks[sbk][:, ii * 256:(ii + 1) * 256], KT[:, sl, n * 128:(n + 1) * 128], QT[:, sl, jq:jq + 256],
                   True, fox, [(k_KT, sl, n // 8), (k_QT, sl)], [BK[sbk]])
                if not fox:
                    mm(banks[sbk][:, ii * 256:(ii + 1) * 256], kpeT[:, n * 128:(n + 1) * 128], QP[:, sl, jq:jq + 256],
                       False, True, [(k_kpeT, n), (k_kpeT, "z"), (k_QP, sl), (k_QP, sl, "z")], [BK[sbk]])

        def t_sm(i):
            hh, j, pp, g, last = tasks[i]
            fox = hh >= 8
            h = hh % 8
            jq = j * 256
            sbk = SBK[i % 3]
            ps_ = i % 3
            n0 = 2 * pp
            ingroup = n0 >= 4 * j
            kbl = n0 - 4 * j
            if not fox:
                if ingroup:
                    tt("dve", tmpF[:, ps_], banks[sbk][:, 0:512], maskM[:, kbl * 256:(kbl + 2) * 256], ALU.add, [BK[sbk], k_maskM], [(k_tmp, ps_, 0), (k_tmp, ps_, 1)])
                    act(PT[:, ps_], tmpF[:, ps_], AF.Exp, [(k_tmp, ps_, 0), (k_tmp, ps_, 1)], [(k_PT, ps_, 0), (k_PT, ps_, 1)])
                else:
                    act(PT[:, ps_], banks[sbk][:, 0:512], AF.Exp, [BK[sbk]], [(k_PT, ps_, 0), (k_PT, ps_, 1)])
            else:
                tk = [(k_tmp, ps_, 0), (k_tmp, ps_, 1)]
                tt("dve", tmpF[:, ps_].rearrange("p (i q) -> p i q", i=2), banks[sbk][:, 0:512].rearrange("p (i q) -> p i q", i=2),
                   cb[:, h, jq:jq + 256].unsqueeze(1).broadcast_to([128, 2, 256]), ALU.add, [BK[sbk], (k_cb, h, j // 2)], tk)
                if ingroup:
                    tt("dve", tmpF[:, ps_], tmpF[:, ps_], maskF[:, kbl * 256:(kbl + 2) * 256], ALU.add, tk + [k_maskF], tk)
                for ii in range(2):
                    n = n0 + ii
                    act(PT[:, ps_, ii * 256:(ii + 1) * 256], tmpF[:, ps_, ii * 256:(ii + 1) * 256], AF.Exp, tk + [k_negc], [(k_PT, ps_, ii)],
                        bias=negc[:, n, h:h + 1], scale=1.0)

        def t_pv(i, step):
            hh, j, pp, g, last = tasks[i]
            fox = hh >= 8
            sl = hh % 2
            ps_ = i % 3
            ob = OB[g % 2]
            nkb = 4 * (j + 1)
            V_keys = [(k_V, sl, q) for q in range(4)] + [(k_V, sl, "ones")]
            for ii in range(2):
                n = 2 * pp + ii
                for al in range(2):
                    mm(banks[ob[al]][:, 0:129], PT[:, ps_, ii * 256 + al * 128:ii * 256 + (al + 1) * 128], V[:, sl, n, 0:129],
                       n == 0, n == nkb - 1, [(k_PT, ps_, ii)] + V_keys, [BK[ob[al]]])
            if not last:
                return

            def do_evac(hh=hh, j=j, g=g, sl=sl, fox=fox, ob=ob, step=step):
                for al in range(2):
                    a = 2 * j + al
                    r_ = (g % 2) * 2 + al
                    obk = banks[ob[al]]
                    rc = recs[:, a, hh:hh + 1]
                    P.op("dve", lambda e, o=rc, i_=obk[:, 128:129]: e.reciprocal(out=o, in_=i_), reads=[BK[ob[al]]], writes=[(k_recs, a, hh)])
                    ts("dve", of32[:, r_], obk[:, 0:128], rc, None, ALU.mult, None, [BK[ob[al]], (k_recs, a, hh)], [(k_of32, r_)])
                    P.op("dve", lambda e, o_=of32[:, r_], acc=ssq_att[:, a, hh:hh + 1]: e.scalar_tensor_tensor(
                        out=osq, in0=o_, scalar=1.0, in1=o_, op0=ALU.mult, op1=ALU.mult, accum_out=acc),
                        reads=[(k_of32, r_)], writes=[k_osq, (k_ssa, a, hh)])
                    tt("dve", obf[:, r_], of32[:, r_], gout[:, hh * 128:(hh + 1) * 128], ALU.mult, [(k_of32, r_), k_gout], [(k_obf, r_)])

                    def do_tr(a=a, r_=r_, sl=sl, fox=fox):
                        b7 = bf_bank(7)
                        tsl = tcount[0] % 8
                        tcount[0] += 1
                        tr(b7[:, tsl, :], obf[:, r_], [(k_obf, r_)], [BK[7]])
                        tcopy("act" if fox else "dve", OTh[:, sl, a * 128:(a + 1) * 128], b7[:, tsl, :], [BK[7]], [(k_OTh, sl, a)])
                    pending_tr.append((step + 3, do_tr))
                if j == 7:
                    def do_store(hh=hh, sl=sl):
                        dma("sp", OTS[hh], OTh[:, sl], [(k_OTh, sl, a) for a in range(NO)], [("OTS", hh)], "OTh%d" % sl)
                        if hh + 2 < 16:
                            head_loads(hh + 2)
                    pending_tr.append((step + 3, do_store))
            pending_tr.append((step + 1, do_evac))
            pending_tr.sort(key=lambda t_: t_[0])

        T = len(tasks)
        for step in range(T + LAG + 6):
            if step < T:
                t_qk(step)
                t_sm(step)
            if 0 <= step - LAG < T:
                t_pv(step - LAG, step)
            while pending_tr and pending_tr[0][0] <= step:
                pending_tr.pop(0)[1]()
                pending_tr.sort(key=lambda t_: t_[0])
        assert not pending_tr
        A.release(p2_mark)
        A.release(attn_mark)
        if STOP == 3:
            return finish()

        k_ssum, ssum = A.alloc("ssum", [NO * 2], F32)
        ssa_keys = [(k_ssa, a, hh) for a in range(NO) for hh in range(16)]
        P.op("dve", lambda e: e.tensor_reduce(out=ssum, in_=ssq_att.rearrange("p a (g h) -> p (a g) h", g=2), axis=AX.X, op=ALU.add),
             reads=ssa_keys, writes=[k_ssum])
        rsqrt_mean(ssum, k_ssum, 1024, rstd_att, k_rsa, NO * 2)
        k_gffn, gffn = A.alloc("gffn", [D], F32)
        k_gfin, gfin = A.alloc("gfin", [D], F32)
        dma("sp", gffn, g_ffn.partition_broadcast(128), [], [k_gffn], "c13")
        dma("sp", gfin, g_fin.partition_broadcast(128), [], [k_gfin], "c14")
        k_x1, x1 = A.alloc("x1", [8, D], F32)
        k_h2T, h2T = A.alloc("h2T", [16, 1024], BF16)
        k_ss2, ss2 = A.alloc("ss2", [8], F32)
        k_rs2, rs2 = A.alloc("rs2", [8], F32)
        k_ss3, ss3 = A.alloc("ss3", [8], F32)
        k_rs3, rs3 = A.alloc("rs3", [8], F32)
        OTS_v = OTS.rearrange("c d t -> d c t")
        w_out_v = w_out.rearrange("(c p) n -> p c n", p=128)
        w_gate_v = w_gate.rearrange("(k p) n -> p k n", p=128)
        w_up_v = w_up.rearrange("(k p) n -> p k n", p=128)
        for tg in range(2):
            mo = A.mark()
            k_OTg, OTg = A.alloc("OTg", [16, 1024], BF16)
            k_wo, wo = A.alloc("wo", [2, 16, 512], BF16)
            k_hn2, hn2 = A.alloc("hn2", [2, D], BF16)
            k_sq4, sq4 = A.alloc("sq4", [D], BF16)
            for i in range(4):
                dma("sp", OTg[:, i * 4:(i + 1) * 4, :], OTS_v[:, i * 4:(i + 1) * 4, tg * 1024:(tg + 1) * 1024],
                    [("OTS", c) for c in range(i * 4, (i + 1) * 4)], [(k_OTg, i)], "OTg%d" % i)
            for i in range(8):
                a = tg * 8 + i
                dma("sp", x1[:, i, :], x_own[a * 128:(a + 1) * 128, :], [], [(k_x1, i, ct) for ct in range(4)], "xo%d" % i)
            uc = 0
            for ct in range(4):
                ws = ct % 2
                for q in range(4):
                    dma("pool", wo[:, ws, q * 4:(q + 1) * 4, :], w_out_v[:, q * 4:(q + 1) * 4, ct * 512:(ct + 1) * 512], [], [(k_wo, ws, q)], "wo%d_%d" % (ws, q))
                for i in range(8):
                    a = tg * 8 + i
                    bm, bf_ = ((0, 1), (2, 3), (4, 5))[uc % 3]
                    uc += 1
                    for c in range(8):
                        mm(banks[bm][:, 0:512], OTg[:, c, i * 128:(i + 1) * 128], wo[:, ws, c, :], c == 0, c == 7, [(k_OTg, c // 4), (k_wo, ws, c // 4)], [BK[bm]])
                    for c in range(8, 16):
                        mm(banks[bf_][:, 0:512], OTg[:, c, i * 128:(i + 1) * 128], wo[:, ws, c, :], c == 8, c == 15, [(k_OTg, c // 4), (k_wo, ws, c // 4)], [BK[bf_]])
                    xs = x1[:, i, ct * 512:(ct + 1) * 512]
                    stt("dve", xs, banks[bm][:, 0:512], rstd_att[:, 2 * a:2 * a + 1], xs, ALU.mult, ALU.add, [BK[bm], k_rsa, (k_x1, i, ct)], [(k_x1, i, ct)])
                    stt("dve", xs, banks[bf_][:, 0:512], rstd_att[:, 2 * a + 1:2 * a + 2], xs, ALU.mult, ALU.add, [BK[bf_], k_rsa, (k_x1, i, ct)], [(k_x1, i, ct)])
            for i in range(8):
                act(sq4, x1[:, i, :], AF.Square, [(k_x1, i, ct) for ct in range(4)], [k_sq4, (k_ss2, i)], accum=ss2[:, i:i + 1])
            k_l2, lnv2 = A.alloc("lnv2", [8], F32)
            act(lnv2, ss2, AF.Ln, [(k_ss2, i) for i in range(8)] + [k_epsb], [k_l2], bias=epsb, scale=1.0 / D)
            act(rs2, lnv2, AF.Exp, [k_l2], [k_rs2], scale=-0.5)
            for i in range(8):
                s = i % 2
                stt("dve", hn2[:, s], x1[:, i, :], rs2[:, i:i + 1], gffn, ALU.mult, ALU.mult, [(k_x1, i, ct) for ct in range(4)] + [k_rs2, k_gffn], [(k_hn2, s)])
                for half in range(2):
                    tb = 6 + half
                    bv = bf_bank(tb)
                    for kq in range(8):
                        k = half * 8 + kq
                        tr(bv[:, kq, :], hn2[:, s, k * 128:(k + 1) * 128], [(k_hn2, s)], [BK[tb]])
                    tcopy("act" if half == 0 else "dve", h2T[:, half * 8:(half + 1) * 8, i * 128:(i + 1) * 128], bv, [BK[tb]], [(k_h2T, i, half)])
            A.release(mo)
            mf = A.mark()
            k_wg, wg = A.alloc("wg", [3, 16, 128], BF16)
            k_wu, wu = A.alloc("wu", [3, 16, 128], BF16)
            k_wd, wd = A.alloc("wd", [8, D], BF16)
            k_aT, aT = A.alloc("aT", [2, 4, 1024], BF16)
            k_sg, sgt = A.alloc("sgt", [2, 512], F32)
            h2keys = [[(k_h2T, i, hf) for i in range(tt_ * 4, tt_ * 4 + 4) for hf in range(2)] for tt_ in range(2)]
            ucnt = [0]
            dcnt = [0]

            def ffn_units(g):
                for cg in range(4):
                    c = g * 4 + cg
                    w3 = c % 3
                    dma("pool", wg[:, w3], w_gate_v[:, :, c * 128:(c + 1) * 128], [], [(k_wg, w3)], "wg%d" % w3)
                    dma("pool", wu[:, w3], w_up_v[:, :, c * 128:(c + 1) * 128], [], [(k_wu, w3)], "wu%d" % w3)
                    dma("pool", wd[:, c % 8, :], w_down[c * 128:(c + 1) * 128, :], [], [(k_wd, c % 8)], "wd%d" % (c % 8))
                    for tt_ in range(2):
                        bg, bu = ((0, 1), (2, 3), (4, 5))[ucnt[0] % 3]
                        ss = ucnt[0] % 2
                        ucnt[0] += 1
                        for k in range(16):
                            mm(banks[bg][:, 0:512], wg[:, w3, k, :], h2T[:, k, tt_ * 512:(tt_ + 1) * 512], k == 0, k == 15, [(k_wg, w3)] + h2keys[tt_], [BK[bg]])
                        for k in range(16):
                            mm(banks[bu][:, 0:512], wu[:, w3, k, :], h2T[:, k, tt_ * 512:(tt_ + 1) * 512], k == 0, k == 15, [(k_wu, w3)] + h2keys[tt_], [BK[bu]])
                        act(sgt[:, ss], banks[bg][:, 0:512], AF.Silu, [BK[bg]], [(k_sg, ss)])
                        tt("dve", aT[:, g % 2, cg, tt_ * 512:(tt_ + 1) * 512], sgt[:, ss], banks[bu][:, 0:512], ALU.mult, [(k_sg, ss), BK[bu]], [(k_aT, g % 2, cg, tt_)])

            def ffn_down(g):
                for i in range(8):
                    for ct in range(4):
                        bk = 6 + dcnt[0] % 2
                        dcnt[0] += 1
                        for cg in range(4):
                            c = g * 4 + cg
                            mm(banks[bk][:, 0:512], aT[:, g % 2, cg, i * 128:(i + 1) * 128], wd[:, c % 8, ct * 512:(ct + 1) * 512], cg == 0, cg == 3,
                               [(k_aT, g % 2, cg, i // 4), (k_wd, c % 8)], [BK[bk]])
                        xs = x1[:, i, ct * 512:(ct + 1) * 512]
                        tt("dve", xs, banks[bk][:, 0:512], xs, ALU.add, [BK[bk], (k_x1, i, ct)], [(k_x1, i, ct)])

            NG = NCH // 4
            for g in range(NG + 1):
                if g < NG:
                    ffn_units(g)
                if g >= 1:
                    ffn_down(g - 1)
            k_sq5, sq5 = A.alloc("sq5", [D], BF16)
            for i in range(8):
                act(sq5, x1[:, i, :], AF.Square, [(k_x1, i, ct) for ct in range(4)], [k_sq5, (k_ss3, i)], accum=ss3[:, i:i + 1])
            k_l3, lnv3 = A.alloc("lnv3", [8], F32)
            act(lnv3, ss3, AF.Ln, [(k_ss3, i) for i in range(8)] + [k_epsb], [k_l3], bias=epsb, scale=1.0 / D)
            act(rs3, lnv3, AF.Exp, [k_l3], [k_rs3], scale=-0.5)
            for i in range(8):
                a = tg * 8 + i
                xk = [(k_x1, i, ct) for ct in range(4)]
                stt("dve", x1[:, i, :], x1[:, i, :], rs3[:, i:i + 1], gfin, ALU.mult, ALU.mult, xk + [k_rs3, k_gfin], xk)
                dma("sp", out_d[a * 128:(a + 1) * 128, :], x1[:, i, :], xk, [("out", a)], "xo%d" % i)
            A.release(mf)
        return finish()


def own_blocks(par):
    r = []
    for j in range(8):
        r += [4 * j, 4 * j + 3] if par == 0 else [4 * j + 1, 4 * j + 2]
    return r


def _consts(par):
    ob = own_blocks(par)
    p = np.arange(128)
    maskM = np.zeros((128, 4, 2, 128), np.float32)
    maskF = np.zeros((128, 4, 2, 128), np.float32)
    for kbl in range(4):
        for al in range(2):
            n = kbl
            ia = ob[al]
            s_idx = n * 128 + p[:, None]
            t_idx = ia * 128 + p[None, :]
            maskM[:, kbl, al, :] = np.where((s_idx // 64) <= (t_idx // 64), 0.0, NEG)
            maskF[:, kbl, al, :] = np.where(s_idx <= t_idx, 0.0, NEG)
    sel = np.zeros((128, NO, NB), np.float32)
    for a, ia in enumerate(ob):
        sel[:, a, ia] = 1.0
    invf = (10000.0 ** (-np.arange(0, 64, 2, dtype=np.float32) / 64)).astype(np.float32)
    return {
        "c_ident": np.eye(128, dtype=np.float32),
        "c_tri": np.triu(np.ones((128, 128), np.float32)),
        "c_ones": np.ones((128, 128), np.float32),
        "c_invf": np.ascontiguousarray(np.broadcast_to(invf, (128, 32))),
        "c_maskM": maskM.reshape(128, 1024),
        "c_maskF": maskF.reshape(128, 1024),
        "c_sel": sel.reshape(128, NO * NB),
    }


_NC_CACHE = {}


def kernel(x, positions, g_attn_norm, w_in, b_forget, g_q_lat, w_uq, g_kv_lat, w_ukv, g_out_mla, g_out_fox,
           w_out, g_ffn_norm, w_gate, w_up, w_down, g_final_norm):
    f = lambda a: np.ascontiguousarray(np.asarray(a, dtype=np.float32))
    x = f(x)
    positions = np.asarray(positions).astype(np.int32)
    shared = {
        "w_in": f(w_in)[0], "w_uq": f(w_uq)[0], "w_ukv": f(w_ukv)[0], "w_out": f(w_out)[0],
        "w_gate": f(w_gate)[0], "w_up": f(w_up)[0], "w_down": f(w_down)[0],
        "g_attn": f(g_attn_norm)[0], "g_q": f(g_q_lat)[0], "g_kv": f(g_kv_lat)[0],
        "g_out": np.ascontiguousarray(np.concatenate([f(g_out_mla)[0], f(g_out_fox)[0]])),
        "g_ffn": f(g_ffn_norm)[0], "g_fin": f(g_final_norm), "b_fg": f(b_forget)[0],
    }
    consts = [_consts(0), _consts(1)]
    in_maps = []
    for c in range(8):
        b, par = c // 2, c % 2
        ob = own_blocks(par)
        xb = x[b].reshape(NB, 128, D)
        pb = positions[b].reshape(NB, 128)
        m = dict(shared)
        m.update(consts[par])
        m["x_nat"] = x[b]
        m["x_own"] = np.ascontiguousarray(xb[ob].reshape(NO * 128, D))
        m["pos_nat"] = np.ascontiguousarray(pb.T)
        m["pos_own"] = np.ascontiguousarray(pb[ob].T)
        in_maps.append(m)
    if STOP < 4:
        for m in in_maps:
            for k in ("w_out", "w_gate", "w_up", "w_down"):
                m[k] = np.ascontiguousarray(m[k][0:128])
            if STOP < 2:
                m["x_nat"] = np.ascontiguousarray(m["x_nat"][0:128])
    if "nc" not in _NC_CACHE:
        _NC_CACHE["nc"] = build_program()
    res = run_bass_kernel_spmd(_NC_CACHE["nc"], in_maps[:NCORES], core_ids=list(range(NCORES)))
    kernel.last_results = res
    out = np.zeros((4, S, D), np.float32) if NCORES < 8 else np.empty((4, S, D), np.float32)
    for c in range(NCORES):
        b, par = c // 2, c % 2
        ob = own_blocks(par)
        o = res.results[c]["out"].reshape(NO, 128, D)
        out[b].reshape(NB, 128, D)[ob] = o
    return out
```

```python
import contextlib
import math
import numpy as np
import concourse.bass as bass
import concourse.mybir as mybir
from concourse.bass_utils import run_bass_kernel_spmd

F32 = mybir.dt.float32
BF16 = mybir.dt.bfloat16
I32 = mybir.dt.int32
U8 = mybir.dt.uint8
AF = mybir.ActivationFunctionType
ALU = mybir.AluOpType
AX = mybir.AxisListType

D = 2048
S = 4096
NB = 32
NO = 16
FF = 5632
NCH = FF // 128
EPS = 1e-6
QSCALE = 1.0 / math.sqrt(192.0)
FSCALE = 1.0 / math.sqrt(128.0)
PI = float(np.pi)
NEG = -1.0e30
ARENA_BYTES = 211968

DEBUG = False
STOP = 99
NCORES = 8
SKIP = set()
NT_LIMIT = 8
LATSTEP = 99
ENGS = ("pe", "act", "dve", "pool", "sp")


class Op:
    __slots__ = ("eng", "fn", "deps", "needed", "count", "dma_sem")

    def __init__(self, eng, fn, dma_sem):
        self.eng = eng
        self.fn = fn
        self.deps = []
        self.needed = False
        self.count = None
        self.dma_sem = dma_sem


def _base(k):
    return k[0] if isinstance(k, tuple) else k


class Prog:
    def __init__(self, nc):
        self.nc = nc
        self.ops = {e: [] for e in ENGS}
        self.state = {}
        self.inherit = {}
        self.touch = {}
        self.last_dma = {}

    def op(self, eng, fn, reads=(), writes=(), dma_sem=None):
        o = Op(eng, fn, dma_sem)
        deps = []
        for b in list(reads) + list(writes):
            if b not in self.state:
                self.state[b] = [list(self.inherit.get(_base(b), [])), []]
        for b in reads:
            deps.extend(self.state[b][0])
            if _base(b) == "ps":
                deps.extend(r for r in self.state[b][1] if r.eng != eng)
        for b in writes:
            st = self.state[b]
            deps.extend(st[0])
            deps.extend(st[1])
        seen = set()
        for d in deps:
            if id(d) in seen or d is o:
                continue
            seen.add(id(d))
            if d.eng == "pe" and eng == "pe" and d.dma_sem is None and dma_sem is None:
                continue
            if d.dma_sem is not None:
                d = self.last_dma[d.dma_sem]
            d.needed = True
            o.deps.append(d)
        for b in reads:
            rl = self.state[b][1]
            if dma_sem is None:
                for i_ in range(len(rl)):
                    if rl[i_].eng == eng and rl[i_].dma_sem is None:
                        del rl[i_]
                        break
            rl.append(o)
        for b in writes:
            self.state[b] = [[o], []]
        tag = ("d", dma_sem) if dma_sem is not None else ("e", eng)
        for b in list(reads) + list(writes):
            self.touch.setdefault(_base(b), {})[tag] = o
        if dma_sem is not None:
            self.last_dma[dma_sem] = o
        self.ops[eng].append(o)
        return o

    def emit(self):
        nc = self.nc
        sem_names = set()
        for e in ENGS:
            for o in self.ops[e]:
                if o.dma_sem is not None:
                    sem_names.add("d_" + o.dma_sem)
        sem_names = sorted(sem_names)
        all_names = ["e_" + e for e in ENGS] + sem_names
        with contextlib.ExitStack() as st:
            sems = {n: st.enter_context(nc.semaphore(n)) for n in all_names}
            cnt = {n: 0 for n in all_names}
            for e in ENGS:
                for o in self.ops[e]:
                    if o.dma_sem is not None:
                        n = "d_" + o.dma_sem
                        cnt[n] += 16
                        o.count = (n, cnt[n])
                    elif o.needed:
                        n = "e_" + e
                        cnt[n] += 1
                        o.count = (n, cnt[n])
            blk = st.enter_context(nc.Block())

            def run(engobj, e):
                waited = {}
                for o in self.ops[e]:
                    need = {}
                    for d in o.deps:
                        n, v = d.count
                        if v > need.get(n, 0):
                            need[n] = v
                    for n, v in need.items():
                        if waited.get(n, 0) >= v:
                            continue
                        waited[n] = v
                        engobj.wait_ge(sems[n], v)
                    ins = o.fn(engobj)
                    if o.dma_sem is not None:
                        ins.then_inc(sems[o.count[0]], 16)
                    elif o.needed:
                        ins.then_inc(sems[o.count[0]], 1)
                if e == "sp":
                    for n in sem_names:
                        if cnt[n] > 0:
                            engobj.wait_ge(sems[n], cnt[n])

            blk.tensor(lambda t: run(t, "pe"))
            blk.scalar(lambda t: run(t, "act"))
            blk.vector(lambda t: run(t, "dve"))
            blk.gpsimd(lambda t: run(t, "pool"))
            blk.sync(lambda t: run(t, "sp"))


_DTSZ = {F32: 4, BF16: 2, I32: 4, U8: 1}


class Arena:
    def __init__(self, P, arena_ap):
        self.P = P
        self.a = arena_ap
        self.top = 0
        self.live = []
        self.dead = []
        self.uid = 0
        self.peak = 0

    def alloc(self, name, shape, dt, parts=128):
        self.uid += 1
        name = "%s#%d" % (name, self.uid)
        n = 1
        for s in shape:
            n *= s
        size = (n * _DTSZ[dt] + 63) // 64 * 64
        off = self.top
        self.top += size
        self.peak = max(self.peak, self.top)
        assert self.top <= ARENA_BYTES, ("SBUF arena overflow", name, self.top)
        inh = []
        for (nm, o, s) in self.dead:
            if o < off + size and off < o + s:
                inh.extend(self.P.touch.get(nm, {}).values())
        self.P.inherit[name] = inh
        self.live.append((name, off, size))
        v = self.a[0:parts, off:off + n * _DTSZ[dt]].bitcast(dt)
        if len(shape) == 2:
            v = v.rearrange("p (a b) -> p a b", a=shape[0])
        elif len(shape) == 3:
            v = v.rearrange("p (a b c) -> p a b c", a=shape[0], b=shape[1])
        return name, v

    def mark(self):
        return (self.top, len(self.live))

    def release(self, mark):
        top, nl = mark
        self.dead.extend(self.live[nl:])
        del self.live[nl:]
        self.top = top


def build_program():
    nc = bass.Bass("TRN2", target_bir_lowering=False)

    def din(name, shape, dt=F32):
        return nc.dram_tensor(name, list(shape), dt, kind="ExternalInput").ap()

    big2 = STOP >= 2
    big4 = STOP >= 4
    x_nat = din("x_nat", [S, D] if big2 else [128, D])
    x_own = din("x_own", [S // 2, D])
    posN_d = din("pos_nat", [128, NB], I32)
    posO_d = din("pos_own", [128, NO], I32)
    w_in = din("w_in", [D, 3912])
    w_uq = din("w_uq", [512, 1536])
    w_ukv = din("w_ukv", [256, 2048])
    w_out = din("w_out", [D, D] if big4 else [128, D])
    w_gate = din("w_gate", [D, FF] if big4 else [128, FF])
    w_up = din("w_up", [D, FF] if big4 else [128, FF])
    w_down = din("w_down", [FF, D] if big4 else [128, D])
    g_attn = din("g_attn", [D])
    g_q = din("g_q", [512])
    g_kv = din("g_kv", [256])
    g_out = din("g_out", [D])
    g_ffn = din("g_ffn", [D])
    g_fin = din("g_fin", [D])
    b_fg = din("b_fg", [8])
    ident_d = din("c_ident", [128, 128])
    tri_d = din("c_tri", [128, 128])
    ones_d = din("c_ones", [128, 128])
    invf_d = din("c_invf", [128, 32])
    maskM_d = din("c_maskM", [128, 1024])
    maskF_d = din("c_maskF", [128, 1024])
    sel_d = din("c_sel", [128, NO * NB])
    out_d = nc.dram_tensor("out", [S // 2, D], F32, kind="ExternalOutput").ap()

    skind = "ExternalOutput" if DEBUG else "Internal"

    def scratch(name, shape):
        return nc.dram_tensor(name, list(shape), BF16, kind=skind).ap()

    KTM = scratch("s_ktm", [8, 128, S])
    KTF = scratch("s_ktf", [8, 128, S])
    VM = scratch("s_vm", [S, 1024])
    VF = scratch("s_vf", [S, 1024])
    QTN = scratch("s_qtn", [8, 128, S // 2])
    QTP = scratch("s_qtp", [8, 64, S // 2])
    QTF = scratch("s_qtf", [8, 128, S // 2])
    OTS = scratch("s_ots", [16, 128, S // 2])

    with contextlib.ExitStack() as es:
        arena_t = es.enter_context(nc.sbuf_tensor("arena", [128, ARENA_BYTES], U8))
        banks = [es.enter_context(nc.psum_tensor("bank%d" % i, [128, 512], F32)) for i in range(8)]
        P = Prog(nc)
        A = Arena(P, arena_t[:])
        BK = [("ps", i) for i in range(8)]

        def bf_bank(i):
            return banks[i][:].bitcast(BF16).rearrange("p (a b) -> p a b", a=8)

        def dma(q, out, in_, reads, writes, sem):
            return P.op(q, lambda e: e.dma_start(out=out, in_=in_), reads=reads, writes=writes, dma_sem=sem)

        def mm(out, lhsT, rhs, start, stop, reads, writes):
            return P.op("pe", lambda e: e.matmul(out, lhsT=lhsT, rhs=rhs, start=start, stop=stop), reads=reads, writes=writes)

        def tr(out, in_, reads, writes):
            return P.op("pe", lambda e: e.transpose(out=out, in_=in_, identity=identb), reads=list(reads) + [k_identb], writes=writes)

        def act(out, in_, func, reads, writes, bias=None, scale=None, accum=None):
            kw = {}
            if bias is not None:
                kw["bias"] = bias
            if scale is not None:
                kw["scale"] = scale
            if accum is not None:
                kw["accum_out"] = accum
            return P.op("act", lambda e: e.activation(out=out, in_=in_, func=func, **kw), reads=reads, writes=writes)

        def amul(out, in_, c, reads, writes):
            return P.op("act", lambda e: e.mul(out=out, in_=in_, mul=c), reads=reads, writes=writes)

        def tcopy(eng, out, in_, reads, writes):
            if eng == "act":
                return P.op("act", lambda e: e.copy(out=out, in_=in_), reads=reads, writes=writes)
            return P.op(eng, lambda e: e.tensor_copy(out=out, in_=in_), reads=reads, writes=writes)

        def tt(eng, out, in0, in1, op, reads, writes):
            return P.op(eng, lambda e: e.tensor_tensor(out=out, in0=in0, in1=in1, op=op), reads=reads, writes=writes)

        def ts(eng, out, in0, s1, s2, op0, op1, reads, writes):
            if s2 is None:
                return P.op(eng, lambda e: e.tensor_scalar(out=out, in0=in0, scalar1=s1, scalar2=None, op0=op0), reads=reads, writes=writes)
            return P.op(eng, lambda e: e.tensor_scalar(out=out, in0=in0, scalar1=s1, scalar2=s2, op0=op0, op1=op1), reads=reads, writes=writes)

        def stt(eng, out, in0, scalar, in1, op0, op1, reads, writes):
            return P.op(eng, lambda e: e.scalar_tensor_tensor(out=out, in0=in0, scalar=scalar, in1=in1, op0=op0, op1=op1), reads=reads, writes=writes)

        def memset(eng, ap, val, writes):
            return P.op(eng, lambda e: e.memset(ap, val), writes=writes)

        def finish():
            P.emit()
            build_program.peak = A.peak
            return nc

        k_identf, identf = A.alloc("identf", [128], F32)
        k_identb, identb = A.alloc("identb", [128], BF16)
        k_epsb, epsb = A.alloc("epsb", [1], F32)
        k_oneb, oneb = A.alloc("oneb", [1], F32)
        k_hpib, hpib = A.alloc("hpib", [1], F32)
        k_rsa, rstd_att = A.alloc("rstd_att", [NO * 2], F32)
        k_ssa, ssq_att = A.alloc("ssq_att", [NO, 16], F32)
        k_recs, recs = A.alloc("recs", [NO, 16], F32)
        dma("sp", identf, ident_d, [], [k_identf], "c")
        memset("dve", epsb, EPS, [k_epsb])
        memset("dve", oneb, 1.0, [k_oneb])
        memset("dve", hpib, PI / 2, [k_hpib])
        cnt_sm = [0]

        k_lnv, lnv_all = A.alloc("lnv", [64], F32)

        def rsqrt_mean(ssq_ap, k_ssq, n_feat, out_ap, k_out, width):
            o = cnt_sm[0] % 2
            cnt_sm[0] += 1
            lnv = lnv_all[:, o * 32:o * 32 + width]
            act(lnv, ssq_ap, AF.Ln, [k_ssq, k_epsb], [(k_lnv, o)], bias=epsb, scale=1.0 / n_feat)
            act(out_ap, lnv, AF.Exp, [(k_lnv, o)], [k_out], scale=-0.5)

        attn_mark = A.mark()
        k_trif, trif = A.alloc("trif", [128], F32)
        k_onesf, onesf = A.alloc("onesf", [128], F32)
        k_maskM, maskM = A.alloc("maskM", [1024], F32)
        k_maskF, maskF = A.alloc("maskF", [1024], F32)
        k_sel, sel = A.alloc("sel", [NO, NB], F32)
        k_cosN, cosN = A.alloc("cosN", [NB, 32], F32)
        k_sinN, sinN = A.alloc("sinN", [NB, 32], F32)
        k_cosO, cosO = A.alloc("cosO", [NO, 32], F32)
        k_sinO, sinO = A.alloc("sinO", [NO, 32], F32)
        k_flog, flog = A.alloc("flog", [NB, 8], F32)
        k_negc, negc = A.alloc("negc", [NB, 8], F32)
        k_cown, cown = A.alloc("cown", [NO, 8], F32)
        k_bfg, bfg = A.alloc("bfg", [8], F32)
        k_kpeT, kpeT = A.alloc("kpeT", [S], BF16)
        memset("dve", kpeT[64:128, :], 0.0, [(k_kpeT, "z")])
        dma("sp", trif, tri_d, [], [k_trif], "c")
        dma("sp", onesf, ones_d, [], [k_onesf], "c")
        dma("sp", maskM, maskM_d, [], [k_maskM], "c")
        dma("sp", maskF, maskF_d, [], [k_maskF], "c")
        dma("sp", sel, sel_d.rearrange("p (a n) -> p a n", a=NO), [], [k_sel], "c")
        dma("sp", bfg, b_fg.partition_broadcast(128), [], [k_bfg], "c")

        k_piN, pos_iN = A.alloc("pos_iN", [NB], I32)
        k_piO, pos_iO = A.alloc("pos_iO", [NO], I32)
        k_if, invf = A.alloc("invf", [32], F32)
        dma("sp", pos_iN, posN_d, [], [k_piN], "c")
        dma("sp", pos_iO, posO_d, [], [k_piO], "c")
        dma("sp", invf, invf_d, [], [k_if], "c")
        k_join, joinb = A.alloc("joinb", [1], F32)
        memset("dve", joinb, 0.0, [k_join, k_identf, k_trif, k_onesf, k_maskM, k_maskF, k_sel, k_bfg, k_piN, k_piO, k_if])
        tcopy("dve", identb, identf, [k_identf], [k_identb])

        def make_tables(pos_i, k_pi, nb, cos_t, k_cos, sin_t, k_sin, scale, tag):
            m = A.mark()
            k_pf, pos_f = A.alloc("pos_f", [nb], F32)
            k_ang, ang = A.alloc("ang", [nb, 32], F32)
            k_t1, t1 = A.alloc("t1", [nb, 32], F32)
            k_kk, kk = A.alloc("kk", [nb, 32], F32)
            k_rr, rr = A.alloc("rr", [nb, 32], F32)
            k_ra, ra = A.alloc("ra", [nb, 32], F32)
            k_bb, bb = A.alloc("bb", [nb, 32], F32)
            k_sg, sg = A.alloc("sg", [nb, 32], F32)
            tcopy("dve", pos_f, pos_i, [k_pi], [k_pf])
            tt("dve", ang, pos_f.unsqueeze(2).broadcast_to([128, nb, 32]), invf.unsqueeze(1).broadcast_to([128, nb, 32]), ALU.mult, [k_pf, k_if], [k_ang])
            MAGIC = 12582912.0
            C1 = 6.28125
            C2 = 2 * PI - 6.28125
            ts("dve", t1, ang, 1.0 / (2 * PI), MAGIC, ALU.mult, ALU.add, [k_ang], [k_t1])
            ts("dve", kk, t1, -MAGIC, None, ALU.add, None, [k_t1], [k_kk])
            stt("dve", t1, kk, -C1, ang, ALU.mult, ALU.add, [k_kk, k_ang], [k_t1])
            stt("dve", rr, kk, -C2, t1, ALU.mult, ALU.add, [k_kk, k_t1], [k_rr])
            ts("dve", rr, rr, -3.1415925, 3.1415925, ALU.max, ALU.min, [k_rr], [k_rr])
            stt("dve", ra, rr, -1.0, rr, ALU.mult, ALU.max, [k_rr], [k_ra])
            act(cos_t, ra, AF.Sin, [k_ra, k_hpib], [k_cos], bias=hpib, scale=-1.0)
            ts("dve", t1, ra, -PI / 2, None, ALU.add, None, [k_ra], [k_t1])
            stt("dve", bb, t1, -1.0, t1, ALU.mult, ALU.max, [k_t1], [k_bb])
            act(sin_t, bb, AF.Sin, [k_bb, k_hpib], [k_sin], bias=hpib, scale=-1.0)
            act(sg, rr, AF.Sign, [k_rr], [k_sg])
            tt("dve", sin_t, sin_t, sg, ALU.mult, [k_sin, k_sg], [k_sin])
            if scale != 1.0:
                ts("dve", cos_t, cos_t, scale, None, ALU.mult, None, [k_cos], [k_cos])
                ts("dve", sin_t, sin_t, scale, None, ALU.mult, None, [k_sin], [k_sin])
            A.release(m)

        make_tables(pos_iN, k_piN, NB, cosN, k_cosN, sinN, k_sinN, 1.0, "n")
        make_tables(pos_iO, k_piO, NO, cosO, k_cosO, sinO, k_sinO, QSCALE, "o")

        if STOP == 0:
            return finish()

        def wload(dst, src, reads_none, k_dst, sem, pieces, axis_len):
            step = axis_len // pieces
            for i in range(pieces):
                dma("pool", dst[:, i * step:(i + 1) * step], src[:, i * step:(i + 1) * step], [], [(k_dst, i)], sem)

        def wkeys(k, pieces):
            return [(k, i) for i in range(pieces)]

        def norm_part(x_rows, bufs, s, g_b, k_g):
            (k_xt, xt), (k_hn, hn), (k_sq, sqj), (k_ss, ssq), (k_rs, rstd) = bufs
            dma("sp", xt[:, s], x_rows, [], [(k_xt, s)], "xt%d" % s)
            act(sqj, xt[:, s], AF.Square, [(k_xt, s)], [k_sq, (k_ss, s)], accum=ssq[:, s:s + 1])
            rsqrt_mean(ssq[:, s:s + 1], (k_ss, s), D, rstd[:, s:s + 1], (k_rs, s), 1)
            stt("dve", hn[:, s], xt[:, s], rstd[:, s:s + 1], g_b, ALU.mult, ALU.mult, [(k_xt, s), (k_rs, s), k_g], [(k_hn, s)])

        def transpose_part(bufs, s, hT, k_hT, col0, tb):
            (k_xt, xt), (k_hn, hn), (k_sq, sqj), (k_ss, ssq), (k_rs, rstd) = bufs
            for half in range(2):
                bv = bf_bank(tb[half])
                for kq in range(8):
                    k = half * 8 + kq
                    tr(bv[:, kq, :], hn[:, s, k * 128:(k + 1) * 128], [(k_hn, s)], [BK[tb[half]]])
                tcopy("act" if half == 0 else "dve", hT[:, half * 8:(half + 1) * 8, col0:col0 + 128], bv, [BK[tb[half]]], [(k_hT, col0 // 128, half)])

        def hT_keys(k_hT, blks):
            return [(k_hT, b, h) for b in blks for h in range(2)]

        p1_mark = A.mark()
        k_gattn, gattn = A.alloc("gattn", [D], F32)
        dma("sp", gattn, g_attn.partition_broadcast(128), [], [k_gattn], "c9")
        nb_bufs = (A.alloc("xt", [2, D], F32), A.alloc("hn", [2, D], BF16), A.alloc("sqj", [D], BF16),
                   A.alloc("ssq", [2], F32), A.alloc("rstd", [2], F32))
        k_hT, hT = A.alloc("hT", [16, 512], BF16)
        pO_mark = A.mark()
        k_gq, gq = A.alloc("gq", [512], F32)
        dma("sp", gq, g_q.partition_broadcast(128), [], [k_gq], "c10")
        k_wq, wq = A.alloc("wq", [16, 512], BF16)
        k_wfq, wfq = A.alloc("wfq", [16, 1024], BF16)
        k_wuqn, wuqn = A.alloc("wuqn", [4, 1024], BF16)
        k_wuqp, wuqp = A.alloc("wuqp", [4, 512], BF16)
        w_in_v = w_in.rearrange("(k p) n -> p k n", p=128)
        wload(wq, w_in_v[:, :, 0:512], None, k_wq, "wq", 4, 16)
        w_uq_v = w_uq.rearrange("(k p) (h d) -> p k h d", p=128, d=192)
        for k in range(4):
            dma("pool", wuqn[:, k, :].rearrange("p (h d) -> p h d", d=128), w_uq_v[:, k, :, 0:128], [], [(k_wuqn, k)], "wuqn")
            dma("pool", wuqp[:, k, :].rearrange("p (h d) -> p h d", d=64), w_uq_v[:, k, :, 128:192], [], [(k_wuqp, k)], "wuqp")
        wload(wfq, w_in_v[:, :, 832:1856], None, k_wfq, "wfq", 8, 16)
        k_ssq_q, ssq_q = A.alloc("ssq_q", [1], F32)
        k_rs_q, rs_q = A.alloc("rs_q", [1], F32)
        k_sq2, sq2 = A.alloc("sq2", [512], BF16)
        k_qn, qn = A.alloc("qn", [512], BF16)
        k_qnT, qnT = A.alloc("qnT", [4, 128], BF16)
        k_qnb, qnb = A.alloc("qnb", [1024], BF16)
        k_qpb, qpb = A.alloc("qpb", [8, 64], BF16)
        rt = [A.alloc("rt%d" % i, [8, 32], F32) for i in range(4)]
        k_qtn, qtn_st = A.alloc("qtn_st", [2, 8, 128], BF16)
        k_qtp, qtp_st = A.alloc("qtp_st", [2, 8, 128], BF16)
        k_qf, qf_st = A.alloc("qf_st", [2, 512], BF16)

        QTN_v = QTN.rearrange("h d t -> d h t")
        QTP_v = QTP.rearrange("h d t -> d h t")
        norm_part(x_own[0:128, :], nb_bufs, 0, gattn, k_gattn)
        for ot in range(4):
            for blk in range(4):
                a = ot * 4 + blk
                s = a % 2
                if a + 1 < NO:
                    norm_part(x_own[(a + 1) * 128:(a + 2) * 128, :], nb_bufs, (a + 1) % 2, gattn, k_gattn)
                transpose_part(nb_bufs, s, hT, k_hT, blk * 128, (0, 1))
                hk = hT_keys(k_hT, [blk])
                for k in range(16):
                    mm(banks[2][:, 0:512], hT[:, k, blk * 128:(blk + 1) * 128], wq[:, k, :], k == 0, k == 15, hk + wkeys(k_wq, 4), [BK[2]])
                act(sq2, banks[2][:, 0:512], AF.Square, [BK[2]], [k_sq2, k_ssq_q], accum=ssq_q)
                rsqrt_mean(ssq_q, k_ssq_q, 512, rs_q, k_rs_q, 1)
                stt("dve", qn, banks[2][:, 0:512], rs_q, gq, ALU.mult, ALU.mult, [BK[2], k_rs_q, k_gq], [k_qn])
                b3 = bf_bank(3)
                for k in range(4):
                    tr(b3[:, k, :], qn[:, k * 128:(k + 1) * 128], [k_qn], [BK[3]])
                tcopy("dve", qnT, b3[:, 0:4, :], [BK[3]], [k_qnT])
                for (bk, wsrc, c0, kw) in ((4, wuqn, 0, wkeys(k_wuqn, 4)), (5, wuqn, 512, wkeys(k_wuqn, 4)), (6, wuqp, 0, wkeys(k_wuqp, 4))):
                    for k in range(4):
                        mm(banks[bk][:, 0:512], qnT[:, k, :], wsrc[:, k, c0:c0 + 512], k == 0, k == 3, [k_qnT] + kw, [BK[bk]])
                amul(qnb[:, 0:512], banks[4][:, 0:512], QSCALE, [BK[4]], [(k_qnb, 0)])
                amul(qnb[:, 512:1024], banks[5][:, 0:512], QSCALE, [BK[5]], [(k_qnb, 1)])
                b6 = banks[6][:, 0:512].rearrange("p (h d) -> p h d", d=64)
                x1v, x2v = b6[:, :, 0:32], b6[:, :, 32:64]
                cb_ = cosO[:, a, :].unsqueeze(1).broadcast_to([128, 8, 32])
                sb_ = sinO[:, a, :].unsqueeze(1).broadcast_to([128, 8, 32])
                tt("dve", rt[0][1], x1v, cb_, ALU.mult, [BK[6], k_cosO], [rt[0][0]])
                tt("dve", rt[1][1], x2v, sb_, ALU.mult, [BK[6], k_sinO], [rt[1][0]])
                tt("dve", rt[2][1], x2v, cb_, ALU.mult, [BK[6], k_cosO], [rt[2][0]])
                tt("dve", rt[3][1], x1v, sb_, ALU.mult, [BK[6], k_sinO], [rt[3][0]])
                tt("dve", qpb[:, :, 0:32], rt[0][1], rt[1][1], ALU.subtract, [rt[0][0], rt[1][0]], [(k_qpb, 0)])
                tt("dve", qpb[:, :, 32:64], rt[2][1], rt[3][1], ALU.add, [rt[2][0], rt[3][0]], [(k_qpb, 1)])
                b7 = bf_bank(7)
                for h in range(8):
                    tr(b7[:, h, :], qnb[:, h * 128:(h + 1) * 128], [(k_qnb, h // 4)], [BK[7]])
                for h in range(8):
                    tr(b3[0:64, h, :], qpb[:, h, :], [(k_qpb, 0), (k_qpb, 1)], [BK[3]])
                tcopy("act", qtn_st[:, s], b7, [BK[7]], [(k_qtn, s)])
                tcopy("dve", qtp_st[0:64, s], b3[0:64], [BK[3]], [(k_qtp, s)])
                dma("sp", QTN_v[:, :, a * 128:(a + 1) * 128], qtn_st[:, s], [(k_qtn, s)], [("QTN", a)], "qtn%d" % s)
                dma("sp", QTP_v[:, :, a * 128:(a + 1) * 128], qtp_st[0:64, s], [(k_qtp, s)], [("QTP", a)], "qtp%d" % s)
            hk = hT_keys(k_hT, range(4))
            for h in range(8):
                bk = 4 + (h % 2)
                for k in range(16):
                    mm(banks[bk][:, 0:512], wfq[:, k, h * 128:(h + 1) * 128], hT[:, k, :], k == 0, k == 15, hk + wkeys(k_wfq, 8), [BK[bk]])
                s2 = h % 2
                if h % 2 == 0:
                    amul(qf_st[:, s2], banks[bk][:, 0:512], FSCALE, [BK[bk]], [(k_qf, s2)])
                else:
                    ts("dve", qf_st[:, s2], banks[bk][:, 0:512], FSCALE, None, ALU.mult, None, [BK[bk]], [(k_qf, s2)])
                dma("sp", QTF[h][:, ot * 512:(ot + 1) * 512], qf_st[:, s2], [(k_qf, s2)], [("QTF", h, ot)], "qf%d" % s2)
        A.release(pO_mark)
        if STOP == 1:
            return finish()

        k_gkv, gkv = A.alloc("gkv", [256], F32)
        dma("sp", gkv, g_kv.partition_broadcast(128), [], [k_gkv], "c11")
        k_wlat, wlat = A.alloc("wlat", [16, 328], BF16)
        k_wfk, wfk = A.alloc("wfk", [16, 1024], BF16)
        k_wfv, wfv = A.alloc("wfv", [16, 1024], BF16)
        k_wkn, wkn = A.alloc("wkn", [2, 1024], BF16)
        k_wkv, wkv = A.alloc("wkv", [2, 1024], BF16)
        wload(wlat[:, :, 0:320], w_in_v[:, :, 512:832], None, k_wlat, "wlat", 2, 16)
        dma("pool", wlat[:, :, 320:328], w_in_v[:, :, 3904:3912], [], [(k_wlat, 2)], "wlat2")
        w_ukv_v = w_ukv.rearrange("(k p) (h t d) -> p k t h d", p=128, t=2, d=128)
        for k in range(2):
            dma("pool", wkn[:, k, :].rearrange("p (h d) -> p h d", d=128), w_ukv_v[:, k, 0], [], [(k_wkn, k)], "wkn")
            dma("pool", wkv[:, k, :].rearrange("p (h d) -> p h d", d=128), w_ukv_v[:, k, 1], [], [(k_wkv, k)], "wkv")
        wload(wfv, w_in_v[:, :, 2880:3904], None, k_wfv, "wfv", 8, 16)
        wload(wfk, w_in_v[:, :, 1856:2880], None, k_wfk, "wfk", 8, 16)
        k_ssq_k, ssq_k = A.alloc("ssq_k", [1], F32)
        k_rs_k, rs_k = A.alloc("rs_k", [1], F32)
        k_sq3, sq3 = A.alloc("sq3", [256], BF16)
        k_kvn, kvn = A.alloc("kvn", [256], BF16)
        k_kvnT, kvnT = A.alloc("kvnT", [2, 512], BF16)
        k_kpb, kpb = A.alloc("kpb", [64], BF16)
        rk = [A.alloc("rk%d" % i, [32], F32) for i in range(4)]
        k_vms, vm_st = A.alloc("vm_st", [2, 1024], BF16)
        k_vfs, vf_st = A.alloc("vf_st", [2, 1024], BF16)
        k_kst, kst = A.alloc("kst", [4, 512], BF16)
        wlat_keys = [(k_wlat, 0), (k_wlat, 1), (k_wlat, 2)]
        def b_T(n):
            transpose_part(nb_bufs, n % 2, hT, k_hT, (n % 4) * 128, (0, 1))

        def b_L(n):
            blk = n % 4
            hk = hT_keys(k_hT, [blk])
            for k in range(16):
                mm(banks[2][:, 0:328], hT[:, k, blk * 128:(blk + 1) * 128], wlat[:, k, :], k == 0, k == 15, hk + wlat_keys, [BK[2]])
            act(sq3, banks[2][:, 0:256], AF.Square, [BK[2]], [k_sq3, k_ssq_k], accum=ssq_k)
            rsqrt_mean(ssq_k, k_ssq_k, 256, rs_k, k_rs_k, 1)
            stt("dve", kvn, banks[2][:, 0:256], rs_k, gkv, ALU.mult, ALU.mult, [BK[2], k_rs_k, k_gkv], [k_kvn])
            x1v, x2v = banks[2][:, 256:288], banks[2][:, 288:320]
            cb_, sb_ = cosN[:, n, :], sinN[:, n, :]
            tt("dve", rk[0][1], x1v, cb_, ALU.mult, [BK[2], k_cosN], [rk[0][0]])
            tt("dve", rk[1][1], x2v, sb_, ALU.mult, [BK[2], k_sinN], [rk[1][0]])
            tt("dve", rk[2][1], x2v, cb_, ALU.mult, [BK[2], k_cosN], [rk[2][0]])
            tt("dve", rk[3][1], x1v, sb_, ALU.mult, [BK[2], k_sinN], [rk[3][0]])
            tt("dve", kpb[:, 0:32], rk[0][1], rk[1][1], ALU.subtract, [rk[0][0], rk[1][0]], [(k_kpb, 0)])
            tt("dve", kpb[:, 32:64], rk[2][1], rk[3][1], ALU.add, [rk[2][0], rk[3][0]], [(k_kpb, 1)])
            tcopy("dve", flog[:, n, :], banks[2][:, 320:328], [BK[2]], [(k_flog, n)])

        def b_t(n):
            blk = n % 4
            b3 = bf_bank(3)
            for k in range(2):
                tr(b3[:, k, :], kvn[:, k * 128:(k + 1) * 128], [k_kvn], [BK[3]])
            tr(b3[0:64, 2, :], kpb, [(k_kpb, 0), (k_kpb, 1)], [BK[3]])
            tcopy("act", kvnT[:, :, blk * 128:(blk + 1) * 128], b3[:, 0:2, :], [BK[3]], [(k_kvnT, blk)])
            tcopy("act", kpeT[0:64, n * 128:(n + 1) * 128], b3[0:64, 2, :], [BK[3]], [(k_kpeT, n)])

        def b_VM(n):
            blk = n % 4
            s = n % 2
            for half in range(2):
                bk = 4 + half
                for k in range(2):
                    mm(banks[bk][:, 0:512], kvnT[:, k, blk * 128:(blk + 1) * 128], wkv[:, k, half * 512:(half + 1) * 512], k == 0, k == 1, [(k_kvnT, blk)] + wkeys(k_wkv, 2), [BK[bk]])
                tcopy("act" if half == 0 else "dve", vm_st[:, s, half * 512:(half + 1) * 512], banks[bk][:, 0:512], [BK[bk]], [(k_vms, s, half)])
            dma("sp", VM[n * 128:(n + 1) * 128, :], vm_st[:, s], [(k_vms, s, 0), (k_vms, s, 1)], [("VM", n)], "vm%d" % s)

        def b_VF(n, half):
            blk = n % 4
            s = n % 2
            hk = hT_keys(k_hT, [blk])
            bk = 6 + half
            for k in range(16):
                mm(banks[bk][:, 0:512], hT[:, k, blk * 128:(blk + 1) * 128], wfv[:, k, half * 512:(half + 1) * 512], k == 0, k == 15, hk + wkeys(k_wfv, 8), [BK[bk]])
            tcopy("act" if half == 0 else "dve", vf_st[:, s, half * 512:(half + 1) * 512], banks[bk][:, 0:512], [BK[bk]], [(k_vfs, s, half)])
            if half == 1:
                dma("sp", VF[n * 128:(n + 1) * 128, :], vf_st[:, s], [(k_vfs, s, 0), (k_vfs, s, 1)], [("VF", n)], "vf%d" % s)

        kc_ = [0]

        def b_K(nt):
            hk = hT_keys(k_hT, range(4))
            for h in range(8):
                bk = 4 + (h % 2)
                for k in range(16):
                    mm(banks[bk][:, 0:512], wfk[:, k, h * 128:(h + 1) * 128], hT[:, k, :], k == 0, k == 15, hk + wkeys(k_wfk, 8), [BK[bk]])
                s2 = kc_[0] % 4
                kc_[0] += 1
                tcopy("act" if h % 2 == 0 else "dve", kst[:, s2], banks[bk][:, 0:512], [BK[bk]], [(k_kst, s2)])
                dma("sp", KTF[h][:, nt * 512:(nt + 1) * 512], kst[:, s2], [(k_kst, s2)], [("KTF", h, nt)], "kst%d" % s2)
            kT_keys = [(k_kvnT, b) for b in range(4)]
            for h in range(8):
                bk = 6 + (h % 2)
                for k in range(2):
                    mm(banks[bk][:, 0:512], wkn[:, k, h * 128:(h + 1) * 128], kvnT[:, k, :], k == 0, k == 1, kT_keys + wkeys(k_wkn, 2), [BK[bk]])
                s2 = kc_[0] % 4
                kc_[0] += 1
                tcopy("act" if h % 2 == 0 else "dve", kst[:, s2], banks[bk][:, 0:512], [BK[bk]], [(k_kst, s2)])
                dma("sp", KTM[h][:, nt * 512:(nt + 1) * 512], kst[:, s2], [(k_kst, s2)], [("KTM", h, nt)], "kst%d" % s2)

        NBLK = NT_LIMIT * 4
        norm_part(x_nat[0:128, :], nb_bufs, 0, gattn, k_gattn)
        for n in range(NBLK):
            if n + 1 < NBLK:
                norm_part(x_nat[(n + 1) * 128:(n + 2) * 128, :], nb_bufs, (n + 1) % 2, gattn, k_gattn)
            b_T(n)
            if n >= 1:
                b_VF(n - 1, 0)
            b_L(n)
            if n >= 1:
                b_VM(n - 1)
                b_VF(n - 1, 1)
            b_t(n)
            if n % 4 == 3:
                b_K(n // 4)
        b_VF(NBLK - 1, 0)
        b_VM(NBLK - 1)
        b_VF(NBLK - 1, 1)
        A.release(p1_mark)

        if "cum" in SKIP:
            return finish()
        m = A.mark()
        flog_keys = [(k_flog, n) for n in range(NB)]
        k_z, z = A.alloc("z", [NB, 8], F32)
        k_l, lpos = A.alloc("lpos", [NB, 8], F32)
        k_tb, tbc = A.alloc("tbc", [NB, 8], F32)
        k_pre, pre = A.alloc("pre", [NB, 8], F32)
        k_t4, t4 = A.alloc("t4", [NO, 8, NB], F32)
        tt("dve", z, flog, bfg.unsqueeze(1).broadcast_to([128, NB, 8]), ALU.add, flog_keys + [k_bfg], [k_z])
        ts("dve", z, z, -80.0, None, ALU.max, None, [k_z], [k_z])
        act(lpos, z, AF.Exp, [k_z], [k_l], scale=-1.0)
        act(lpos, lpos, AF.Ln, [k_l, k_oneb], [k_l], bias=oneb, scale=1.0)
        lflat = lpos.rearrange("p n h -> p (n h)")
        mm(banks[0][:, 0:256], trif, lflat, True, True, [k_trif, k_l], [BK[0]])
        mm(banks[1][:, 0:256], onesf, lflat, True, True, [k_onesf, k_l], [BK[1]])
        tcopy("dve", tbc.rearrange("p n h -> p (n h)"), banks[1][:, 0:256], [BK[1]], [k_tb])
        memset("dve", pre[:, 0, :], 0.0, [(k_pre, 0)])
        for n in range(1, NB):
            tt("dve", pre[:, n, :], pre[:, n - 1, :], tbc[:, n - 1, :], ALU.add, [(k_pre, n - 1), k_tb], [(k_pre, n)])
        tt("dve", negc.rearrange("p n h -> p (n h)"), banks[0][:, 0:256], pre.rearrange("p n h -> p (n h)"), ALU.add,
           [BK[0]] + [(k_pre, n) for n in range(NB)], [k_negc])
        tt("dve", t4, sel.unsqueeze(2).broadcast_to([128, NO, 8, NB]),
           negc.rearrange("p n h -> p h n").unsqueeze(1).broadcast_to([128, NO, 8, NB]), ALU.mult, [k_sel, k_negc], [k_t4])
        P.op("dve", lambda e: e.tensor_reduce(out=cown, in_=t4, axis=AX.X, op=ALU.add), reads=[k_t4], writes=[k_cown])
        A.release(m)

        if STOP == 2:
            return finish()

        p2_mark = A.mark()
        k_gout, gout = A.alloc("gout", [D], F32)
        dma("sp", gout, g_out.partition_broadcast(128), [], [k_gout], "c12")
        k_KT, KT = A.alloc("KT", [2, S], BF16)
        k_V, V = A.alloc("V", [2, NB, 130], BF16)
        k_QT, QT = A.alloc("QT", [2, S // 2], BF16)
        k_QP, QP = A.alloc("QP", [2, S // 2], BF16)
        k_cb, cb = A.alloc("cb", [8, S // 2], F32)
        k_dg, dg = A.alloc("dg", [8, 128], F32)
        k_PT, PT = A.alloc("PT", [4, 512], BF16)
        k_tmp, tmpF = A.alloc("tmpF", [4, 512], F32)
        k_osq, osq = A.alloc("osq", [128], F32)
        k_obf, obf = A.alloc("obf", [4, 128], BF16)
        k_of32, of32 = A.alloc("of32", [4, 128], F32)
        k_OTh, OTh = A.alloc("OTh", [2, S // 2], BF16)
        for sl in range(2):
            memset("dve", V[:, sl, :, 128:130], 1.0, [(k_V, sl, "ones")])
            memset("dve", QP[64:128, sl, :], 0.0, [(k_QP, sl, "z")])
        VM_v = VM.rearrange("(n p) c -> p n c", p=128)
        VF_v = VF.rearrange("(n p) c -> p n c", p=128)

        def head_loads(hh):
            fox = hh >= 8
            h = hh % 8
            sl = hh % 2
            KTsrc, Vsrc, Qsrc = (KTF, VF_v, QTF) if fox else (KTM, VM_v, QTN)
            ktag = "KTF" if fox else "KTM"
            for i in range(4):
                dma("sp", KT[:, sl, i * 1024:(i + 1) * 1024], KTsrc[h][:, i * 1024:(i + 1) * 1024],
                    [(ktag, h, 2 * i), (ktag, h, 2 * i + 1)], [(k_KT, sl, i)], "KT%d_%d" % (sl, i))
            vtag = "VF" if fox else "VM"
            for i in range(4):
                dma("sp", V[:, sl, i * 8:(i + 1) * 8, 0:128], Vsrc[:, i * 8:(i + 1) * 8, h * 128:(h + 1) * 128],
                    [(vtag, n) for n in range(i * 8, (i + 1) * 8)], [(k_V, sl, i)], "V%d" % sl)
            if fox:
                dma("sp", QT[:, sl, :], Qsrc[h], [("QTF", h, ot) for ot in range(4)], [(k_QT, sl)], "QT%d" % sl)
            else:
                dma("sp", QT[:, sl, :], Qsrc[h], [("QTN", a) for a in range(NO)], [(k_QT, sl)], "QT%d" % sl)
                dma("sp", QP[0:64, sl, :], QTP[h], [("QTP", a) for a in range(NO)], [(k_QP, sl)], "QP%d" % sl)

        head_loads(0)
        head_loads(1)
        k_nones, nones = A.alloc("nones", [128], F32)
        ts("dve", nones, onesf, -1.0, None, ALU.mult, None, [k_onesf], [k_nones])
        items = [(h, a4) for h in range(8) for a4 in range(4)]

        def cb_mult(ci):
            h, a4 = items[ci]
            ds_ = ci % 2
            tt("dve", dg[:, ds_ * 4:(ds_ + 1) * 4, :], identf.unsqueeze(1).broadcast_to([128, 4, 128]),
               cown[:, a4 * 4:(a4 + 1) * 4, h:h + 1].broadcast_to([128, 4, 128]), ALU.mult, [k_identf, k_cown], [(k_dg, ds_)])

        cb_mult(0)
        cb_mult(1)
        for ci in range(len(items)):
            h, a4 = items[ci]
            ds_ = ci % 2
            bkc = 6 + ci % 2
            mm(banks[bkc][:, 0:512], nones, dg[:, ds_ * 4:(ds_ + 1) * 4, :].rearrange("p a q -> p (a q)"), True, True, [k_nones, (k_dg, ds_)], [BK[bkc]])
            if ci >= 1:
                hp, ap = items[ci - 1]
                bkp = 6 + (ci - 1) % 2
                tcopy("dve", cb[:, hp, ap * 512:(ap + 1) * 512], banks[bkp][:, 0:512], [BK[bkp]], [(k_cb, hp, ap)])
            if ci + 2 < len(items):
                cb_mult(ci + 2)
        hp, ap = items[-1]
        bkp = 6 + (len(items) - 1) % 2
        tcopy("dve", cb[:, hp, ap * 512:(ap + 1) * 512], banks[bkp][:, 0:512], [BK[bkp]], [(k_cb, hp, ap)])

        SBK = (0, 1, 2)
        OB = ((3, 4), (5, 6))
        LAG = 3
        tasks = []
        gidx = 0
        for hh in range(16):
            for j in range(8):
                npairs = 2 * (j + 1)
                for pp in range(npairs):
                    tasks.append((hh, j, pp, gidx, pp == npairs - 1))
                gidx += 1
        pending_tr = []
        tcount = [0]

        def t_qk(i):
            hh, j, pp, g, last = tasks[i]
            fox = hh >= 8
            sl = hh % 2
            jq = j * 256
            sbk = SBK[i % 3]
            for ii in range(2):
                n = 2 * pp + ii
                mm(banks[sbk][:, ii * 256:(ii + 1) * 256], KT[:, sl, n * 128:(n + 1) * 128], QT[:, sl, jq:jq + 256],
                   True, fox, [(k_KT, sl, n // 8), (k_QT, sl)], [BK[sbk]])
                if not fox:
                    mm(banks[sbk][:, ii * 256:(ii + 1) * 256], kpeT[:, n * 128:(n + 1) * 128], QP[:, sl, jq:jq + 256],
                       False, True, [(k_kpeT, n), (k_kpeT, "z"), (k_QP, sl), (k_QP, sl, "z")], [BK[sbk]])

        def t_sm(i):
            hh, j, pp, g, last = tasks[i]
            fox = hh >= 8
            h = hh % 8
            jq = j * 256
            sbk = SBK[i % 3]
            ps_ = i % 4
            n0 = 2 * pp
            ingroup = n0 >= 4 * j
            kbl = n0 - 4 * j
            if not fox:
                if ingroup:
                    tt("dve", tmpF[:, ps_], banks[sbk][:, 0:512], maskM[:, kbl * 256:(kbl + 2) * 256], ALU.add, [BK[sbk], k_maskM], [(k_tmp, ps_, 0), (k_tmp, ps_, 1)])
                    act(PT[:, ps_], tmpF[:, ps_], AF.Exp, [(k_tmp, ps_, 0), (k_tmp, ps_, 1)], [(k_PT, ps_, 0), (k_PT, ps_, 1)])
                else:
                    act(PT[:, ps_], banks[sbk][:, 0:512], AF.Exp, [BK[sbk]], [(k_PT, ps_, 0), (k_PT, ps_, 1)])
            else:
                tk = [(k_tmp, ps_, 0), (k_tmp, ps_, 1)]
                tt("dve", tmpF[:, ps_].rearrange("p (i q) -> p i q", i=2), banks[sbk][:, 0:512].rearrange("p (i q) -> p i q", i=2),
                   cb[:, h, jq:jq + 256].unsqueeze(1).broadcast_to([128, 2, 256]), ALU.add, [BK[sbk], (k_cb, h, j // 2)], tk)
                if ingroup:
                    tt("dve", tmpF[:, ps_], tmpF[:, ps_], maskF[:, kbl * 256:(kbl + 2) * 256], ALU.add, tk + [k_maskF], tk)
                for ii in range(2):
                    n = n0 + ii
                    act(PT[:, ps_, ii * 256:(ii + 1) * 256], tmpF[:, ps_, ii * 256:(ii + 1) * 256], AF.Exp, tk + [k_negc], [(k_PT, ps_, ii)],
                        bias=negc[:, n, h:h + 1], scale=1.0)

        def t_pv(i, step):
            hh, j, pp, g, last = tasks[i]
            fox = hh >= 8
            sl = hh % 2
            ps_ = i % 4
            ob = OB[g % 2]
            nkb = 4 * (j + 1)
            V_keys = [(k_V, sl, q) for q in range(4)] + [(k_V, sl, "ones")]
            for ii in range(2):
                n = 2 * pp + ii
                for al in range(2):
                    mm(banks[ob[al]][:, 0:129], PT[:, ps_, ii * 256 + al * 128:ii * 256 + (al + 1) * 128], V[:, sl, n, 0:129],
                       n == 0, n == nkb - 1, [(k_PT, ps_, ii)] + V_keys, [BK[ob[al]]])
            if not last:
                return

            def do_evac(hh=hh, j=j, g=g, sl=sl, fox=fox, ob=ob, step=step):
                for al in range(2):
                    a = 2 * j + al
                    r_ = (g % 2) * 2 + al
                    obk = banks[ob[al]]
                    rc = recs[:, a, hh:hh + 1]
                    P.op("dve", lambda e, o=rc, i_=obk[:, 128:129]: e.reciprocal(out=o, in_=i_), reads=[BK[ob[al]]], writes=[(k_recs, a, hh)])
                    ts("dve", of32[:, r_], obk[:, 0:128], rc, None, ALU.mult, None, [BK[ob[al]], (k_recs, a, hh)], [(k_of32, r_)])
                    P.op("dve", lambda e, o_=of32[:, r_], acc=ssq_att[:, a, hh:hh + 1]: e.scalar_tensor_tensor(
                        out=osq, in0=o_, scalar=1.0, in1=o_, op0=ALU.mult, op1=ALU.mult, accum_out=acc),
                        reads=[(k_of32, r_)], writes=[k_osq, (k_ssa, a, hh)])
                    tt("dve", obf[:, r_], of32[:, r_], gout[:, hh * 128:(hh + 1) * 128], ALU.mult, [(k_of32, r_), k_gout], [(k_obf, r_)])

                    def do_tr(a=a, r_=r_, sl=sl, fox=fox):
                        b7 = bf_bank(7)
                        tsl = tcount[0] % 8
                        tcount[0] += 1
                        tr(b7[:, tsl, :], obf[:, r_], [(k_obf, r_)], [BK[7]])
                        tcopy("act" if fox else "dve", OTh[:, sl, a * 128:(a + 1) * 128], b7[:, tsl, :], [BK[7]], [(k_OTh, sl, a)])
                    pending_tr.append((step + 3, do_tr))
                if j == 7:
                    def do_store(hh=hh, sl=sl):
                        dma("sp", OTS[hh], OTh[:, sl], [(k_OTh, sl, a) for a in range(NO)], [("OTS", hh)], "OTh%d" % sl)
                        if hh + 2 < 16:
                            head_loads(hh + 2)
                    pending_tr.append((step + 3, do_store))
            pending_tr.append((step + 1, do_evac))
            pending_tr.sort(key=lambda t_: t_[0])

        T = len(tasks)
        for step in range(T + LAG + 6):
            if step < T:
                t_qk(step)
                t_sm(step)
            if 0 <= step - LAG < T:
                t_pv(step - LAG, step)
            while pending_tr and pending_tr[0][0] <= step:
                pending_tr.pop(0)[1]()
                pending_tr.sort(key=lambda t_: t_[0])
        assert not pending_tr
        A.release(p2_mark)
        A.release(attn_mark)
        if STOP == 3:
            return finish()

        k_ssum, ssum = A.alloc("ssum", [NO * 2], F32)
        ssa_keys = [(k_ssa, a, hh) for a in range(NO) for hh in range(16)]
        P.op("dve", lambda e: e.tensor_reduce(out=ssum, in_=ssq_att.rearrange("p a (g h) -> p (a g) h", g=2), axis=AX.X, op=ALU.add),
             reads=ssa_keys, writes=[k_ssum])
        rsqrt_mean(ssum, k_ssum, 1024, rstd_att, k_rsa, NO * 2)
        k_gffn, gffn = A.alloc("gffn", [D], F32)
        k_gfin, gfin = A.alloc("gfin", [D], F32)
        dma("sp", gffn, g_ffn.partition_broadcast(128), [], [k_gffn], "c13")
        dma("sp", gfin, g_fin.partition_broadcast(128), [], [k_gfin], "c14")
        k_x1, x1 = A.alloc("x1", [8, D], F32)
        k_h2T, h2T = A.alloc("h2T", [16, 1024], BF16)
        k_ss2, ss2 = A.alloc("ss2", [8], F32)
        k_rs2, rs2 = A.alloc("rs2", [8], F32)
        k_ss3, ss3 = A.alloc("ss3", [8], F32)
        k_rs3, rs3 = A.alloc("rs3", [8], F32)
        OTS_v = OTS.rearrange("c d t -> d c t")
        w_out_v = w_out.rearrange("(c p) n -> p c n", p=128)
        w_gate_v = w_gate.rearrange("(k p) n -> p k n", p=128)
        w_up_v = w_up.rearrange("(k p) n -> p k n", p=128)
        for tg in range(2):
            mo = A.mark()
            k_OTg, OTg = A.alloc("OTg", [16, 1024], BF16)
            k_wo, wo = A.alloc("wo", [2, 16, 512], BF16)
            k_hn2, hn2 = A.alloc("hn2", [2, D], BF16)
            k_sq4, sq4 = A.alloc("sq4", [D], BF16)
            for i in range(4):
                dma("sp", OTg[:, i * 4:(i + 1) * 4, :], OTS_v[:, i * 4:(i + 1) * 4, tg * 1024:(tg + 1) * 1024],
                    [("OTS", c) for c in range(i * 4, (i + 1) * 4)], [(k_OTg, i)], "OTg%d" % i)
            for i in range(8):
                a = tg * 8 + i
                dma("sp", x1[:, i, :], x_own[a * 128:(a + 1) * 128, :], [], [(k_x1, i, ct) for ct in range(4)], "xo%d" % i)
            uc = 0
            for ct in range(4):
                ws = ct % 2
                for q in range(4):
                    dma("pool", wo[:, ws, q * 4:(q + 1) * 4, :], w_out_v[:, q * 4:(q + 1) * 4, ct * 512:(ct + 1) * 512], [], [(k_wo, ws, q)], "wo%d_%d" % (ws, q))
                for i in range(8):
                    a = tg * 8 + i
                    bm, bf_ = ((0, 1), (2, 3), (4, 5))[uc % 3]
                    uc += 1
                    for c in range(8):
                        mm(banks[bm][:, 0:512], OTg[:, c, i * 128:(i + 1) * 128], wo[:, ws, c, :], c == 0, c == 7, [(k_OTg, c // 4), (k_wo, ws, c // 4)], [BK[bm]])
                    for c in range(8, 16):
                        mm(banks[bf_][:, 0:512], OTg[:, c, i * 128:(i + 1) * 128], wo[:, ws, c, :], c == 8, c == 15, [(k_OTg, c // 4), (k_wo, ws, c // 4)], [BK[bf_]])
                    xs = x1[:, i, ct * 512:(ct + 1) * 512]
                    stt("dve", xs, banks[bm][:, 0:512], rstd_att[:, 2 * a:2 * a + 1], xs, ALU.mult, ALU.add, [BK[bm], k_rsa, (k_x1, i, ct)], [(k_x1, i, ct)])
                    stt("dve", xs, banks[bf_][:, 0:512], rstd_att[:, 2 * a + 1:2 * a + 2], xs, ALU.mult, ALU.add, [BK[bf_], k_rsa, (k_x1, i, ct)], [(k_x1, i, ct)])
            for i in range(8):
                act(sq4, x1[:, i, :], AF.Square, [(k_x1, i, ct) for ct in range(4)], [k_sq4, (k_ss2, i)], accum=ss2[:, i:i + 1])
            k_l2, lnv2 = A.alloc("lnv2", [8], F32)
            act(lnv2, ss2, AF.Ln, [(k_ss2, i) for i in range(8)] + [k_epsb], [k_l2], bias=epsb, scale=1.0 / D)
            act(rs2, lnv2, AF.Exp, [k_l2], [k_rs2], scale=-0.5)
            for i in range(8):
                s = i % 2
                stt("dve", hn2[:, s], x1[:, i, :], rs2[:, i:i + 1], gffn, ALU.mult, ALU.mult, [(k_x1, i, ct) for ct in range(4)] + [k_rs2, k_gffn], [(k_hn2, s)])
                for half in range(2):
                    tb = 6 + half
                    bv = bf_bank(tb)
                    for kq in range(8):
                        k = half * 8 + kq
                        tr(bv[:, kq, :], hn2[:, s, k * 128:(k + 1) * 128], [(k_hn2, s)], [BK[tb]])
                    tcopy("act" if half == 0 else "dve", h2T[:, half * 8:(half + 1) * 8, i * 128:(i + 1) * 128], bv, [BK[tb]], [(k_h2T, i, half)])
            A.release(mo)
            mf = A.mark()
            k_wg, wg = A.alloc("wg", [3, 16, 128], BF16)
            k_wu, wu = A.alloc("wu", [3, 16, 128], BF16)
            k_wd, wd = A.alloc("wd", [8, D], BF16)
            k_aT, aT = A.alloc("aT", [2, 4, 1024], BF16)
            k_sg, sgt = A.alloc("sgt", [2, 512], F32)
            h2keys = [[(k_h2T, i, hf) for i in range(tt_ * 4, tt_ * 4 + 4) for hf in range(2)] for tt_ in range(2)]
            ucnt = [0]
            dcnt = [0]

            def ffn_units(g):
                for cg in range(4):
                    c = g * 4 + cg
                    w3 = c % 3
                    dma("pool", wg[:, w3], w_gate_v[:, :, c * 128:(c + 1) * 128], [], [(k_wg, w3)], "wg%d" % w3)
                    dma("pool", wu[:, w3], w_up_v[:, :, c * 128:(c + 1) * 128], [], [(k_wu, w3)], "wu%d" % w3)
                    dma("pool", wd[:, c % 8, :], w_down[c * 128:(c + 1) * 128, :], [], [(k_wd, c % 8)], "wd%d" % (c % 8))
                    for tt_ in range(2):
                        bg, bu = ((0, 1), (2, 3), (4, 5))[ucnt[0] % 3]
                        ss = ucnt[0] % 2
                        ucnt[0] += 1
                        for k in range(16):
                            mm(banks[bg][:, 0:512], wg[:, w3, k, :], h2T[:, k, tt_ * 512:(tt_ + 1) * 512], k == 0, k == 15, [(k_wg, w3)] + h2keys[tt_], [BK[bg]])
                        for k in range(16):
                            mm(banks[bu][:, 0:512], wu[:, w3, k, :], h2T[:, k, tt_ * 512:(tt_ + 1) * 512], k == 0, k == 15, [(k_wu, w3)] + h2keys[tt_], [BK[bu]])
                        act(sgt[:, ss], banks[bg][:, 0:512], AF.Silu, [BK[bg]], [(k_sg, ss)])
                        tt("dve", aT[:, g % 2, cg, tt_ * 512:(tt_ + 1) * 512], sgt[:, ss], banks[bu][:, 0:512], ALU.mult, [(k_sg, ss), BK[bu]], [(k_aT, g % 2, cg, tt_)])

            def ffn_down(g):
                for i in range(8):
                    for ct in range(4):
                        bk = 6 + dcnt[0] % 2
                        dcnt[0] += 1
                        for cg in range(4):
                            c = g * 4 + cg
                            mm(banks[bk][:, 0:512], aT[:, g % 2, cg, i * 128:(i + 1) * 128], wd[:, c % 8, ct * 512:(ct + 1) * 512], cg == 0, cg == 3,
                               [(k_aT, g % 2, cg, i // 4), (k_wd, c % 8)], [BK[bk]])
                        xs = x1[:, i, ct * 512:(ct + 1) * 512]
                        tt("dve", xs, banks[bk][:, 0:512], xs, ALU.add, [BK[bk], (k_x1, i, ct)], [(k_x1, i, ct)])

            NG = NCH // 4
            for g in range(NG + 1):
                if g < NG:
                    ffn_units(g)
                if g >= 1:
                    ffn_down(g - 1)
            k_sq5, sq5 = A.alloc("sq5", [D], BF16)
            for i in range(8):
                act(sq5, x1[:, i, :], AF.Square, [(k_x1, i, ct) for ct in range(4)], [k_sq5, (k_ss3, i)], accum=ss3[:, i:i + 1])
            k_l3, lnv3 = A.alloc("lnv3", [8], F32)
            act(lnv3, ss3, AF.Ln, [(k_ss3, i) for i in range(8)] + [k_epsb], [k_l3], bias=epsb, scale=1.0 / D)
            act(rs3, lnv3, AF.Exp, [k_l3], [k_rs3], scale=-0.5)
            for i in range(8):
                a = tg * 8 + i
                xk = [(k_x1, i, ct) for ct in range(4)]
                stt("dve", x1[:, i, :], x1[:, i, :], rs3[:, i:i + 1], gfin, ALU.mult, ALU.mult, xk + [k_rs3, k_gfin], xk)
                dma("sp", out_d[a * 128:(a + 1) * 128, :], x1[:, i, :], xk, [("out", a)], "xo%d" % i)
            A.release(mf)
        return finish()


def own_blocks(par):
    r = []
    for j in range(8):
        r += [4 * j, 4 * j + 3] if par == 0 else [4 * j + 1, 4 * j + 2]
    return r


def _consts(par):
    ob = own_blocks(par)
    p = np.arange(128)
    maskM = np.zeros((128, 4, 2, 128), np.float32)
    maskF = np.zeros((128, 4, 2, 128), np.float32)
    for kbl in range(4):
        for al in range(2):
            n = kbl
            ia = ob[al]
            s_idx = n * 128 + p[:, None]
            t_idx = ia * 128 + p[None, :]
            maskM[:, kbl, al, :] = np.where((s_idx // 64) <= (t_idx // 64), 0.0, NEG)
            maskF[:, kbl, al, :] = np.where(s_idx <= t_idx, 0.0, NEG)
    sel = np.zeros((128, NO, NB), np.float32)
    for a, ia in enumerate(ob):
        sel[:, a, ia] = 1.0
    invf = (10000.0 ** (-np.arange(0, 64, 2, dtype=np.float32) / 64)).astype(np.float32)
    return {
        "c_ident": np.eye(128, dtype=np.float32),
        "c_tri": np.triu(np.ones((128, 128), np.float32)),
        "c_ones": np.ones((128, 128), np.float32),
        "c_invf": np.ascontiguousarray(np.broadcast_to(invf, (128, 32))),
        "c_maskM": maskM.reshape(128, 1024),
        "c_maskF": maskF.reshape(128, 1024),
        "c_sel": sel.reshape(128, NO * NB),
    }


_NC_CACHE = {}


def kernel(x, positions, g_attn_norm, w_in, b_forget, g_q_lat, w_uq, g_kv_lat, w_ukv, g_out_mla, g_out_fox,
           w_out, g_ffn_norm, w_gate, w_up, w_down, g_final_norm):
    f = lambda a: np.ascontiguousarray(np.asarray(a, dtype=np.float32))
    x = f(x)
    positions = np.asarray(positions).astype(np.int32)
    shared = {
        "w_in": f(w_in)[0], "w_uq": f(w_uq)[0], "w_ukv": f(w_ukv)[0], "w_out": f(w_out)[0],
        "w_gate": f(w_gate)[0], "w_up": f(w_up)[0], "w_down": f(w_down)[0],
        "g_attn": f(g_attn_norm)[0], "g_q": f(g_q_lat)[0], "g_kv": f(g_kv_lat)[0],
        "g_out": np.ascontiguousarray(np.concatenate([f(g_out_mla)[0], f(g_out_fox)[0]])),
        "g_ffn": f(g_ffn_norm)[0], "g_fin": f(g_final_norm), "b_fg": f(b_forget)[0],
    }
    consts = [_consts(0), _consts(1)]
    in_maps = []
    for c in range(8):
        b, par = c // 2, c % 2
        ob = own_blocks(par)
        xb = x[b].reshape(NB, 128, D)
        pb = positions[b].reshape(NB, 128)
        m = dict(shared)
        m.update(consts[par])
        m["x_nat"] = x[b]
        m["x_own"] = np.ascontiguousarray(xb[ob].reshape(NO * 128, D))
        m["pos_nat"] = np.ascontiguousarray(pb.T)
        m["pos_own"] = np.ascontiguousarray(pb[ob].T)
        in_maps.append(m)
    if STOP < 4:
        for m in in_maps:
            for k in ("w_out", "w_gate", "w_up", "w_down"):
                m[k] = np.ascontiguousarray(m[k][0:128])
            if STOP < 2:
                m["x_nat"] = np.ascontiguousarray(m["x_nat"][0:128])
    if "nc" not in _NC_CACHE:
        _NC_CACHE["nc"] = build_program()
    res = run_bass_kernel_spmd(_NC_CACHE["nc"], in_maps[:NCORES], core_ids=list(range(NCORES)))
    kernel.last_results = res
    out = np.zeros((4, S, D), np.float32) if NCORES < 8 else np.empty((4, S, D), np.float32)
    for c in range(NCORES):
        b, par = c // 2, c % 2
        ob = own_blocks(par)
        o = res.results[c]["out"].reshape(NO, 128, D)
        out[b].reshape(NB, 128, D)[ob] = o
    return out
```
